# Optimizing a Trainium2 kernel written in Bass

```python
import math
import jax, jax.numpy as jnp
from jax import lax
import numpy as np

D_MODEL = 1024
BATCH = 8
SEQ = 2048
DEPTH = 1

MEM_LEN = 256
SSM_HEAD_DIM = 64
SSM_HEADS = D_MODEL // SSM_HEAD_DIM
SSM_D_INNER = SSM_HEADS * SSM_HEAD_DIM
SSM_GROUPS = 2
SSM_STATE = 128
CONV_WIDTH = 4
CHUNK = 128
CONV_DIM = SSM_D_INNER + 2 * SSM_GROUPS * SSM_STATE
ATTN_HEAD_DIM = 64
ATTN_HEADS = D_MODEL // ATTN_HEAD_DIM
ATTN_WIDTH = ATTN_HEADS * ATTN_HEAD_DIM
Q_BLOCK = 128
MIX_WIDTH = SSM_D_INNER + ATTN_WIDTH
IN_COLS = 2 * SSM_D_INNER + 2 * SSM_GROUPS * SSM_STATE + SSM_HEADS + 3 * ATTN_WIDTH + ATTN_HEADS
XATTN_HEADS = 4
XATTN_HEAD_DIM = D_MODEL // XATTN_HEADS
D_FF = 4 * D_MODEL
EPS = 1e-5

kernel_name = "hymba_ssd_fox_memxattn_layer"


def rms_norm(u, g):
    uf = u.astype(jnp.float32)
    y = uf * lax.rsqrt(jnp.mean(uf * uf, axis=-1, keepdims=True) + EPS)
    return (y * g.astype(jnp.float32)).astype(u.dtype)


def segsum(a):
    T = a.shape[-1]
    x = jnp.broadcast_to(a[..., :, None], a.shape + (T,))
    x = jnp.where(jnp.tril(jnp.ones((T, T), dtype=bool), -1), x, 0.0)
    x = jnp.cumsum(x, axis=-2)
    return jnp.where(jnp.tril(jnp.ones((T, T), dtype=bool)), x, -jnp.inf)


def causal_depthwise_conv(u, w, b):
    c = u.shape[-1]
    out = lax.conv_general_dilated(
        u, w[:, None, :].astype(u.dtype), window_strides=(1,),
        padding=[(CONV_WIDTH - 1, 0)], dimension_numbers=("NWC", "WIO", "NWC"),
        feature_group_count=c)
    return out + b.astype(u.dtype)


def ssd_chunked(xh, dt, A, Bm, Cm):
    b, S, g, r, p = xh.shape
    n = Bm.shape[-1]
    c = S // CHUNK
    X = (xh * dt[..., None]).reshape(b, c, CHUNK, g, r, p)
    dA = (dt * A).reshape(b, c, CHUNK, g, r).transpose(0, 3, 4, 1, 2)
    Bc = Bm.reshape(b, c, CHUNK, g, n)
    Cc = Cm.reshape(b, c, CHUNK, g, n)
    A_cs = jnp.cumsum(dA, axis=-1)
    Lmat = jnp.exp(segsum(dA))
    CB = jnp.einsum("bclgn,bcsgn->bcgls", Cc, Bc)
    y_diag = jnp.einsum("bcgls,bgrcls,bcsgrp->bclgrp", CB, Lmat, X)
    decay_states = jnp.exp(A_cs[..., -1:] - A_cs)
    states = jnp.einsum("bclgn,bgrcl,bclgrp->bcgrpn", Bc, decay_states, X)
    states = jnp.concatenate([jnp.zeros_like(states[:, :1]), states], axis=1)
    A_last = jnp.pad(A_cs[..., -1], ((0, 0), (0, 0), (0, 0), (1, 0)))
    chunk_decay = jnp.exp(segsum(A_last))
    new_states = jnp.einsum("bgrzc,bcgrpn->bzgrpn", chunk_decay, states)
    states_in = new_states[:, :-1]
    y_off = jnp.einsum("bclgn,bcgrpn,bgrcl->bclgrp", Cc, states_in, jnp.exp(A_cs))
    return (y_diag + y_off).reshape(b, S, g, r, p)


def forgetting_attention(q, k, v, log_f):
    b, S, h, d = q.shape
    cum = jnp.cumsum(log_f, axis=1).transpose(0, 2, 1)
    scale = d ** -0.5
    outs = []
    for i in range(S // Q_BLOCK):
        qs, qe = i * Q_BLOCK, (i + 1) * Q_BLOCK
        s = jnp.einsum("bqhd,bkhd->bhqk", q[:, qs:qe], k[:, :qe]) * scale
        s = s + cum[:, :, qs:qe, None] - cum[:, :, None, :qe]
        mask = jnp.arange(qs, qe)[:, None] >= jnp.arange(qe)[None, :]
        s = jnp.where(mask, s, -jnp.inf)
        pr = jax.nn.softmax(s, axis=-1)
        outs.append(jnp.einsum("bhqk,bkhd->bqhd", pr, v[:, :qe]))
    return jnp.concatenate(outs, axis=1)


def parallel_mixer(h, w_in, conv_w, conv_b, dt_bias, a_log, d_skip, ssm_norm_w,
                   g_q, g_k, f_bias, w_out):
    b, S, _ = h.shape
    proj = h @ w_in
    sizes = [SSM_D_INNER, CONV_DIM, SSM_HEADS, ATTN_WIDTH, ATTN_WIDTH, ATTN_WIDTH]
    idx = list(np.cumsum(sizes))
    z, xbc, dt_raw, q, k, v, f_raw = jnp.split(proj, idx, axis=-1)
    xbc = jax.nn.silu(causal_depthwise_conv(xbc, conv_w, conv_b)).astype(jnp.float32)
    xs, Bm, Cm = jnp.split(xbc, [SSM_D_INNER, SSM_D_INNER + SSM_GROUPS * SSM_STATE], axis=-1)
    r = SSM_HEADS // SSM_GROUPS
    xs = xs.reshape(b, S, SSM_GROUPS, r, SSM_HEAD_DIM)
    Bm = Bm.reshape(b, S, SSM_GROUPS, SSM_STATE)
    Cm = Cm.reshape(b, S, SSM_GROUPS, SSM_STATE)
    dt = jax.nn.softplus(dt_raw.astype(jnp.float32) + dt_bias.astype(jnp.float32))
    dt = dt.reshape(b, S, SSM_GROUPS, r)
    A = -jnp.exp(a_log.astype(jnp.float32)).reshape(SSM_GROUPS, r)
    y = ssd_chunked(xs, dt, A, Bm, Cm)
    y = y + d_skip.astype(jnp.float32).reshape(SSM_GROUPS, r)[..., None] * xs
    y = y.reshape(b, S, SSM_D_INNER) * jax.nn.silu(z.astype(jnp.float32))
    y = y.reshape(b, S, SSM_GROUPS, SSM_D_INNER // SSM_GROUPS)
    y = y * lax.rsqrt(jnp.mean(y * y, axis=-1, keepdims=True) + EPS)
    y = y.reshape(b, S, SSM_D_INNER) * ssm_norm_w.astype(jnp.float32)
    q = rms_norm(q.astype(jnp.float32).reshape(b, S, ATTN_HEADS, ATTN_HEAD_DIM), g_q)
    k = rms_norm(k.astype(jnp.float32).reshape(b, S, ATTN_HEADS, ATTN_HEAD_DIM), g_k)
    v = v.astype(jnp.float32).reshape(b, S, ATTN_HEADS, ATTN_HEAD_DIM)
    log_f = jax.nn.log_sigmoid(f_raw.astype(jnp.float32) + f_bias.astype(jnp.float32))
    o = forgetting_attention(q, k, v, log_f).reshape(b, S, ATTN_WIDTH)
    mixed = jnp.concatenate([y, o], axis=-1).astype(h.dtype)
    return mixed @ w_out


def memory_cross_attention(h, mem_n, xq_w, xkv_w, xg_q, xg_k, xo_w):
    b, S, _ = h.shape
    q = (h @ xq_w).astype(jnp.float32).reshape(b, S, XATTN_HEADS, XATTN_HEAD_DIM)
    kv = (mem_n @ xkv_w).astype(jnp.float32)
    k, v = jnp.split(kv, 2, axis=-1)
    k = k.reshape(b, MEM_LEN, XATTN_HEADS, XATTN_HEAD_DIM)
    v = v.reshape(b, MEM_LEN, XATTN_HEADS, XATTN_HEAD_DIM)
    q = rms_norm(q, xg_q)
    k = rms_norm(k, xg_k)
    s = jnp.einsum("bqhd,bkhd->bhqk", q, k) * (XATTN_HEAD_DIM ** -0.5)
    pr = jax.nn.softmax(s, axis=-1)
    o = jnp.einsum("bhqk,bkhd->bqhd", pr, v).reshape(b, S, D_MODEL).astype(h.dtype)
    return o @ xo_w


def squared_relu_mlp(h, w_up, w_down):
    u = jax.nn.relu(h @ w_up)
    return (u * u) @ w_down


def setup_inputs(seed: int = 0) -> dict:
    key = jax.random.key(seed)
    ks = jax.random.split(key, 24)
    f32 = jnp.float32

    def nrm(k, shape, fan_in):
        return jax.random.normal(k, shape, f32) * (fan_in ** -0.5)

    def gain(k, shape):
        return 1.0 + 0.02 * jax.random.normal(k, shape, f32)

    dt0 = jnp.exp(jax.random.uniform(ks[6], (DEPTH, SSM_HEADS), f32,
                                     math.log(1e-3), math.log(1e-1)))
    dt_bias = dt0 + jnp.log(-jnp.expm1(-dt0))
    return {
        "x": jax.random.normal(ks[0], (BATCH, SEQ, D_MODEL), f32),
        "mem": jax.random.normal(ks[1], (BATCH, MEM_LEN, D_MODEL), f32),
        "g_mix": gain(ks[2], (DEPTH, D_MODEL)),
        "w_in": nrm(ks[3], (DEPTH, D_MODEL, IN_COLS), D_MODEL),
        "conv_w": nrm(ks[4], (DEPTH, CONV_WIDTH, CONV_DIM), CONV_WIDTH),
        "conv_b": 0.02 * jax.random.normal(ks[5], (DEPTH, CONV_DIM), f32),
        "dt_bias": dt_bias,
        "a_log": jnp.log(jax.random.uniform(ks[7], (DEPTH, SSM_HEADS), f32, 1.0, 16.0)),
        "d_skip": gain(ks[8], (DEPTH, SSM_HEADS)),
        "ssm_norm_w": gain(ks[9], (DEPTH, SSM_D_INNER)),
        "g_q": gain(ks[10], (DEPTH, ATTN_HEAD_DIM)),
        "g_k": gain(ks[11], (DEPTH, ATTN_HEAD_DIM)),
        "f_bias": jax.random.uniform(ks[12], (DEPTH, ATTN_HEADS), f32, 2.0, 6.0),
        "w_out": nrm(ks[13], (DEPTH, MIX_WIDTH, D_MODEL), MIX_WIDTH),
        "g_xattn": gain(ks[14], (DEPTH, D_MODEL)),
        "g_mem": gain(ks[15], (DEPTH, D_MODEL)),
        "xq_w": nrm(ks[16], (DEPTH, D_MODEL, D_MODEL), D_MODEL),
        "xkv_w": nrm(ks[17], (DEPTH, D_MODEL, 2 * D_MODEL), D_MODEL),
        "xg_q": gain(ks[18], (DEPTH, XATTN_HEAD_DIM)),
        "xg_k": gain(ks[19], (DEPTH, XATTN_HEAD_DIM)),
        "xo_w": nrm(ks[20], (DEPTH, D_MODEL, D_MODEL), D_MODEL),
        "g_mlp": gain(ks[21], (DEPTH, D_MODEL)),
        "w_up": nrm(ks[22], (DEPTH, D_MODEL, D_FF), D_MODEL),
        "w_down": nrm(ks[23], (DEPTH, D_FF, D_MODEL), D_FF),
    }


def reference(x, mem, g_mix, w_in, conv_w, conv_b, dt_bias, a_log, d_skip, ssm_norm_w,
              g_q, g_k, f_bias, w_out, g_xattn, g_mem, xq_w, xkv_w, xg_q, xg_k, xo_w,
              g_mlp, w_up, w_down):
    for l in range(DEPTH):
        h = rms_norm(x, g_mix[l])
        x = x + parallel_mixer(h, w_in[l], conv_w[l], conv_b[l], dt_bias[l], a_log[l],
                               d_skip[l], ssm_norm_w[l], g_q[l], g_k[l], f_bias[l], w_out[l])
        h = rms_norm(x, g_xattn[l])
        mem_n = rms_norm(mem, g_mem[l])
        x = x + memory_cross_attention(h, mem_n, xq_w[l], xkv_w[l], xg_q[l], xg_k[l], xo_w[l])
        h = rms_norm(x, g_mlp[l])
        x = x + squared_relu_mlp(h, w_up[l], w_down[l])
    return x
```

```python
import numpy as np
import concourse.bass as bass
import concourse.mybir as mybir
from concourse.bass_utils import run_bass_kernel_spmd

F32 = mybir.dt.float32
BF16 = mybir.dt.bfloat16
AF = mybir.ActivationFunctionType
ALU = mybir.AluOpType

ENGS = ("pe", "act", "dve", "pool", "sp")

S_LEN = 2048
NT = 16
DM = 1024
EPS = 1e-5
NEG = -30000.0


class Reg:
    __slots__ = ("W", "R", "bank")

    def __init__(self, bank=None):
        self.W = []
        self.R = []
        self.bank = bank


def regs(n):
    return [Reg() for _ in range(n)]


class _Op:
    __slots__ = ("i", "eng", "fns", "est", "preds", "kind", "epoch", "tab", "nun", "succ", "ready", "start",
                 "fin", "pos", "dkey", "dval", "done")


LAT_X = 0.2
LAT_S = 0.05
TAB_COST = 2.0


class Sched:
    def __init__(self, nc, n_dma_sems=32):
        self.nc = nc
        self.sem = {}
        self._ctx = []
        for e in ENGS:
            if e == "sp":
                continue
            c = nc.semaphore("s_" + e)
            self.sem[e] = c.__enter__()
            self._ctx.append(c)
        self.n_dma = n_dma_sems
        for i in range(n_dma_sems):
            c = nc.semaphore("s_dma%d" % i)
            self.sem[("dma", i)] = c.__enter__()
            self._ctx.append(c)
        self.ops = []
        self.epoch = 0
        self._pend = []
        self.bank_last = {}
        self.bank_w = {}
        self.bank_r = {}
        self.pe_pending = False

    def _new(self, eng, fns, est, kind, tab, reads, writes):
        op = _Op()
        op.i = len(self.ops)
        op.eng = eng
        op.fns = fns
        op.est = est
        op.kind = kind
        op.tab = tab
        op.epoch = self.epoch
        op.succ = []
        op.done = False
        preds = {}

        def add(p, raw):
            if p is op or p.epoch != op.epoch:
                return
            need = True
            if kind != "dma" and p.kind != "dma" and p.eng == eng and not raw and eng == "pe":
                need = False
            preds[p] = preds.get(p, False) or need

        for r in reads:
            for w in r.W:
                add(w, True)
            if r.bank is not None and eng in ("act", "dve"):
                lr = self.bank_last.get((r.bank, "dve" if eng == "act" else "act"))
                if lr is not None:
                    add(lr, False)
                lr = self.bank_last.get((r.bank, eng))
                if lr is not None:
                    add(lr, False)
            if r.bank is not None:
                bw = self.bank_w.get(r.bank)
                if bw is not None:
                    add(bw, True)
        for w in writes:
            for x in w.W:
                add(x, False)
            for x in w.R:
                add(x, False)
            if w.bank is not None and eng == "pe":
                for x in self.bank_r.get(w.bank, ()):
                    add(x, False)
        op.preds = preds
        for r in reads:
            if r.bank is not None and eng != "pe":
                self.bank_r.setdefault(r.bank, []).append(op)
        for w in writes:
            if w.bank is not None and eng == "pe":
                self.bank_w[w.bank] = op
                self.bank_r[w.bank] = []
        for r in reads:
            r.R.append(op)
            if r.bank is not None and eng in ("act", "dve"):
                self.bank_last[(r.bank, eng)] = op
        for w in writes:
            w.W = [op]
            w.R = []
        self.ops.append(op)
        return op

    def op(self, eng, fn, reads=(), writes=(), sig=True, est=0.5, tab=None):
        reads = list(reads)
        writes = list(writes)
        if eng == "pe" and not sig:
            self._pend.append((fn, reads, writes, est))
            self.pe_pending = True
            return
        if eng == "pe":
            fns = [p[0] for p in self._pend] + [fn]
            for p in self._pend:
                reads += p[1]
                writes += p[2]
                est += p[3]
            self._pend = []
            self.pe_pending = False
        else:
            assert not self._pend, "non-PE op declared inside an open PE group"
            fns = [fn]
        self._new(eng, fns, est, "op", tab, reads, writes)

    def dma(self, eng, out, in_, reads=(), writes=(), **kw):
        assert not self._pend
        nbytes = 4
        for d in out.shape:
            nbytes *= d
        op = self._new(eng, [(out, in_, kw)], 2.0 + nbytes / 150e3, "dma", None, list(reads), list(writes))
        return op

    def final_wait(self, eng, regs_):
        self._new(eng, [], 0.01, "wait", None, list(regs_), [])

    def barrier(self):
        assert not self._pend
        self.epoch += 1
        self.bank_last = {}
        self.bank_w = {}
        self.bank_r = {}

    def _schedule(self):
        free = {e: 0.0 for e in ENGS}
        tabcur = [None]
        win = {"pe": 64, "act": 96, "dve": 96, "pool": 1, "sp": 1}
        order = []
        nep = self.epoch + 1
        byep = [[] for _ in range(nep)]
        for o in self.ops:
            byep[o.epoch].append(o)
        t_ep = 0.0
        for ep in range(nep):
            ops = byep[ep]
            lst = {e: [] for e in ENGS}
            for o in ops:
                lst[o.eng].append(o)
                o.nun = len(o.preds)
                for p in o.preds:
                    p.succ.append(o)
                o.ready = t_ep
            head = {e: 0 for e in ENGS}
            for e in ENGS:
                free[e] = max(free[e], t_ep)
            remaining = len(ops)
            while remaining:
                best = None
                for e in ENGS:
                    L = lst[e]
                    h = head[e]
                    while h < len(L) and L[h].done:
                        h += 1
                    head[e] = h
                    k = h
                    seen = 0
                    w = win[e]
                    while k < len(L) and seen < w:
                        o = L[k]
                        k += 1
                        if o.done:
                            continue
                        seen += 1
                        if o.nun:
                            continue
                        st = max(free[e], o.ready)
                        if e == "act" and o.tab is not None and o.tab != tabcur[0]:
                            st += TAB_COST
                        key = (st, o.i)
                        if best is None or key < best[0]:
                            best = (key, o)
                assert best is not None, "scheduler deadlock"
                (st, _), o = best
                e = o.eng
                if e == "act" and o.tab is not None:
                    tabcur[0] = o.tab
                o.start = st
                if o.kind == "dma":
                    iss = 1.5 if e == "pool" else 0.1
                    free[e] = st + iss
                    o.fin = st + iss + o.est
                else:
                    free[e] = st + o.est
                    o.fin = st + o.est
                o.done = True
                remaining -= 1
                order.append(o)
                for sopp in o.succ:
                    sopp.nun -= 1
                    lat = LAT_S if (sopp.eng == e and o.kind != "dma") else LAT_X
                    if o.fin + lat > sopp.ready:
                        sopp.ready = o.fin + lat
            t_ep = max([t_ep] + [o.fin for o in ops])
        self.est_total = t_ep
        return order

    def emit(self):
        assert not self._pend
        order = self._schedule()
        q = {e: [] for e in ENGS}
        pos = {e: 0 for e in ENGS}
        waited = {e: {} for e in ENGS}
        dma_tot = [0] * self.n_dma
        rr_sw = 0
        rr_hw = 0
        cur_ep = {e: 0 for e in ENGS}
        snap = {}
        last_ep = 0

        def wait(e, key, v):
            if waited[e].get(key, 0) >= v:
                return
            waited[e][key] = v
            h = self.sem[key]
            q[e].append(lambda eng_, h=h, v=v: eng_.wait_ge(h, v))

        for o in order:
            e = o.eng
            if o.epoch > last_ep:
                tot = {k: v for k, v in pos.items() if k != "sp" and v > 0}
                for k in range(self.n_dma):
                    if dma_tot[k] > 0:
                        tot[("dma", k)] = dma_tot[k]
                for ep in range(last_ep + 1, o.epoch + 1):
                    snap[ep] = tot
                last_ep = o.epoch
            if o.epoch > cur_ep[e]:
                for key, v in snap[o.epoch].items():
                    if key != e:
                        wait(e, key, v)
                cur_ep[e] = o.epoch
            for p, need in o.preds.items():
                if not need:
                    continue
                if p.kind == "dma":
                    wait(e, p.dkey, p.dval)
                else:
                    wait(e, p.eng, p.pos)
            if o.kind == "dma":
                half = self.n_dma // 2
                if e == "pool":
                    k = rr_sw
                    rr_sw = (rr_sw + 1) % half
                else:
                    k = half + rr_hw
                    rr_hw = (rr_hw + 1) % half
                key = ("dma", k)
                if dma_tot[k] > 0:
                    wait(e, key, dma_tot[k])
                dma_tot[k] += 16
                o.dkey = key
                o.dval = dma_tot[k]
                out, in_, kw = o.fns[0]
                h = self.sem[key]
                q[e].append(lambda eng_, out=out, in_=in_, h=h, kw=kw:
                            eng_.dma_start(out=out, in_=in_, **kw).then_inc(h, 16))
            elif o.kind == "op":
                pos[e] += 1
                o.pos = pos[e]
                h = self.sem[e]
                for f in o.fns[:-1]:
                    q[e].append(lambda eng_, f=f: f(eng_))
                f = o.fns[-1]
                q[e].append(lambda eng_, f=f, h=h: f(eng_).then_inc(h, 1))
        nc = self.nc
        with nc.Block() as block:
            @block.tensor
            def _(e):
                for f in q["pe"]:
                    f(e)

            @block.scalar
            def _(e):
                for f in q["act"]:
                    f(e)

            @block.vector
            def _(e):
                for f in q["dve"]:
                    f(e)

            @block.gpsimd
            def _(e):
                for f in q["pool"]:
                    f(e)

            @block.sync
            def _(e):
                for f in q["sp"]:
                    f(e)


def _fsz(ap):
    n = 1
    for d in ap.shape[1:]:
        n *= d
    return n


_TAB = {}


class Ops:
    def __init__(self, S):
        self.S = S
        self.flip = 0
        if not _TAB:
            _TAB.update({AF.Exp: "E", AF.Ln: "E", AF.Silu: "S", AF.Square: "S", AF.Sqrt: "Q"})

    def _e(self, eng, n, fixed=0.15, per=0.00105):
        if eng == "pool":
            return 0.3 + n * 0.0023
        return fixed + n * per

    def act(self, out, in_, func, reads, writes, bias=None, scale=None, accum=None):
        kw = {}
        if bias is not None:
            kw["bias"] = bias
        if scale is not None:
            kw["scale"] = scale
        if accum is not None:
            kw["accum_out"] = accum
        est = 0.22 + _fsz(out) * 0.00104 + (0.1 if accum is not None else 0.0)
        self.S.op("act", lambda e: e.activation(out=out, in_=in_, func=func, **kw), reads, writes,
                  est=est, tab=_TAB.get(func))

    def tt(self, out, in0, in1, op, reads, writes, eng="dve"):
        self.S.op(eng, lambda e: e.tensor_tensor(out=out, in0=in0, in1=in1, op=op), reads, writes,
                  est=self._e(eng, _fsz(out)))

    def ts(self, out, in0, s1, s2, op0, op1, reads, writes, eng="dve"):
        est = self._e(eng, _fsz(out))
        if s2 is None:
            self.S.op(eng, lambda e: e.tensor_scalar(out=out, in0=in0, scalar1=s1, scalar2=None, op0=op0),
                      reads, writes, est=est)
        else:
            self.S.op(eng, lambda e: e.tensor_scalar(out=out, in0=in0, scalar1=s1, scalar2=s2, op0=op0, op1=op1),
                      reads, writes, est=est)

    def stt(self, out, in0, scalar, in1, op0, op1, reads, writes):
        self.S.op("dve", lambda e: e.scalar_tensor_tensor(out=out, in0=in0, scalar=scalar, in1=in1,
                                                           op0=op0, op1=op1), reads, writes,
                  est=0.2 + _fsz(out) * 0.00105)

    def cp(self, out, in_, reads, writes, eng="dve"):
        if eng == "act":
            self.S.op("act", lambda e: e.activation(out=out, in_=in_, func=AF.Copy), reads, writes,
                      est=0.22 + _fsz(out) * 0.00104)
        else:
            self.S.op(eng, lambda e: e.tensor_copy(out=out, in_=in_), reads, writes, est=self._e(eng, _fsz(out)))

    def cpalt(self, out, in_, reads, writes):
        self.flip ^= 1
        self.cp(out, in_, reads, writes, eng="act" if self.flip else "dve")

    def recip(self, out, in_, reads, writes):
        self.S.op("dve", lambda e: e.reciprocal(out=out, in_=in_), reads, writes, est=0.2 + _fsz(out) * 0.0062)

    def memset(self, ap, val, writes, eng="dve"):
        self.S.op(eng, lambda e: e.memset(ap, val), [], writes, est=0.1 + _fsz(ap) * 0.0005)

    def mm(self, out, lhsT, rhs, start, stop, reads=(), writes=(), sig=False):
        self.S.op("pe", lambda e: e.matmul(out, lhsT=lhsT, rhs=rhs, start=start, stop=stop),
                  reads, writes, sig=sig, est=0.03 + _fsz(rhs) * 0.0006)

    def mmg(self, out, pairs, reads, writes):
        n = len(pairs)
        for i, (l, r) in enumerate(pairs):
            self.mm(out, l, r, start=(i == 0), stop=(i == n - 1),
                    reads=reads if i == 0 else (), writes=writes if i == 0 else (), sig=(i == n - 1))

    def tr(self, out, in_, ident, reads=(), writes=(), sig=False):
        self.S.op("pe", lambda e: e.transpose(out, in_, ident), reads, writes, sig=sig, est=0.12)

    def scan(self, out, d0, d1, init, op0, op1, reads, writes):
        self.S.op("dve", lambda e: e.tensor_tensor_scan(out=out, data0=d0, data1=d1, initial=init,
                                                         op0=op0, op1=op1), reads, writes,
                  est=0.2 + _fsz(out) * 0.0021)


class Arena:
    def __init__(self, nc):
        self.nc = nc
        base = (nc.sbuf_base + 63) // 64 * 64
        top = nc.sbuf_top
        KB = 1024
        self.reg = {
            "C": [base, base + 8 * KB],
            "G": [base + 8 * KB, base + 16 * KB],
            "H": [base + 16 * KB, base + 48 * KB],
            "M": [base + 48 * KB, base + 112 * KB],
            "ML": [base + 48 * KB, base + 80 * KB],
            "MH": [base + 80 * KB, base + 112 * KB],
            "R": [base + 112 * KB, top],
            "R0": [base + 112 * KB, base + 176 * KB],
            "R1": [base + 176 * KB, top],
        }
        self.cur = {k: v[0] for k, v in self.reg.items()}
        self.n = 0

    def reset(self, *names):
        for k in names:
            self.cur[k] = self.reg[k][0]

    def get(self, region, shape, dt, name=None):
        esz = 2 if dt == BF16 else 4
        nbytes = esz
        for s in shape[1:]:
            nbytes *= s
        nbytes = (nbytes + 63) // 64 * 64
        off = self.cur[region]
        assert off + nbytes <= self.reg[region][1], (region, name, shape, off + nbytes - self.reg[region][1])
        self.cur[region] = off + nbytes
        self.n += 1
        return self.nc.alloc_sbuf_tensor_at("%s_%d" % (name or region, self.n), list(shape), dt, offset=off)


def build_nc(upto="all", debug=False):
    nc = bass.Bass("TRN2", target_bir_lowering=False)

    def din(name, shape):
        return nc.dram_tensor(name, list(shape), F32, kind="ExternalInput").ap()

    x_d = din("x", [S_LEN, DM])
    mem_d = din("mem", [256, DM])
    g_mix = din("g_mix", [DM])
    w_in = din("w_in", [DM, 5664])
    conv_w = din("conv_w", [4, 1536])
    conv_b = din("conv_b", [1536])
    dt_bias = din("dt_bias", [16])
    a_log = din("a_log", [16])
    d_skip = din("d_skip", [16])
    ssm_norm_w = din("ssm_norm_w", [DM])
    g_q = din("g_q", [64])
    g_k = din("g_k", [64])
    f_bias = din("f_bias", [16])
    w_out = din("w_out", [2048, DM])
    g_xattn = din("g_xattn", [DM])
    g_mem = din("g_mem", [DM])
    xq_w = din("xq_w", [DM, DM])
    xkv_w = din("xkv_w", [DM, 2048])
    xg_q = din("xg_q", [256])
    xg_k = din("xg_k", [256])
    xo_w = din("xo_w", [DM, DM])
    g_mlp = din("g_mlp", [DM])
    w_up = din("w_up", [DM, 4096])
    w_down = din("w_down", [4096, DM])
    c_ident = din("c_ident", [128, 128])
    c_mask = din("c_mask", [128, 128])
    out_d = nc.dram_tensor("out", [S_LEN, DM], F32, kind="ExternalOutput").ap()

    S = Sched(nc)
    O = Ops(S)
    A = Arena(nc)
    dbg = []

    def col(ap1d, n):
        return ap1d.rearrange("(p o) -> p o", o=1)

    PP = [nc.alloc_psum_tensor("pp%d" % i, [128, 1024], F32) for i in range(4)]
    RB = [Reg(bank=i) for i in range(8)]

    def bank(b):
        return PP[b // 2][:, (b % 2) * 512:(b % 2) * 512 + 512]

    def bank_bf(b):
        return PP[b // 2][:, (b % 2) * 512:(b % 2) * 512 + 512].bitcast(BF16)

    ident_f = A.get("C", [128, 128], F32, "identf")
    maskf = A.get("C", [128, 128], F32, "maskf")
    ident_b = A.get("C", [128, 128], BF16, "identb")
    mask_b = A.get("C", [128, 128], BF16, "maskb")
    ones_b = A.get("C", [128, 128], BF16, "onesb")
    bones_b = A.get("C", [128, 128], BF16, "bonesb")
    sel127 = A.get("C", [128, 128], F32, "sel127")
    sel0 = A.get("C", [128, 128], F32, "sel0")
    cols = A.get("C", [128, 64], F32, "cols")
    ones_f = A.get("C", [128, 128], F32, "onesf")
    R_const = Reg()
    R_cols = Reg()
    RC = []

    def cdma(dst, src, **kw):
        r = Reg()
        RC.append(r)
        S.dma("pool", dst, src, reads=[R_cols], writes=[r], **kw)

    S.dma("pool", ident_f[:], c_ident, writes=[R_const])
    R_maskf = Reg()
    S.dma("pool", maskf[:], c_mask, writes=[R_maskf])
    O.cp(ident_b[:], ident_f[:], [R_const], [R_const])
    O.memset(ones_b[:], 1.0, [R_const])
    O.memset(ones_f[:], 1.0, [R_const])
    O.memset(bones_b[:], 0.0, [R_const])
    O.memset(bones_b[0:64, 0:64], 1.0, [R_const])
    O.memset(bones_b[64:128, 64:128], 1.0, [R_const])
    O.cp(sel127[:], ident_f[:, 127:128].to_broadcast([128, 128]), [R_const], [R_const])
    O.cp(sel0[:], ident_f[:, 0:1].to_broadcast([128, 128]), [R_const], [R_const])
    O.cp(mask_b[:], maskf[:], [R_const, R_maskf], [R_const])
    O.memset(cols[:], 0.0, [R_cols])
    cw = A.get("C", [128, 12, 4], F32, "cw")
    cb = A.get("C", [128, 12], F32, "cb")
    cdma(cols[0:64, 0:1], col(g_q, 64))
    cdma(cols[64:128, 0:1], col(g_q, 64))
    cdma(cols[0:64, 1:2], col(g_k, 64))
    cdma(cols[64:128, 1:2], col(g_k, 64))
    cdma(cols[0:16, 3:4], col(dt_bias, 16))
    cdma(cols[32:48, 5:6], col(f_bias, 16))
    cdma(cols[0:16, 4:5], col(a_log, 16))
    cdma(cols[:, 6:8], xg_q.rearrange("(c p) -> p c", p=128), allow_slow_non_contiguous=True)
    cdma(cols[:, 8:10], xg_k.rearrange("(c p) -> p c", p=128), allow_slow_non_contiguous=True)
    for k_ in range(4):
        cdma(cw[:, :, k_], conv_w[k_].rearrange("(j p) -> p j", p=128), allow_slow_non_contiguous=True)
    cdma(cb[:], conv_b.rearrange("(j p) -> p j", p=128), allow_slow_non_contiguous=True)
    O.act(cols[:, 0:1], cols[:, 0:1], AF.Copy, [R_cols] + RC, [R_cols], scale=0.125)
    O.act(cols[:, 6:8], cols[:, 6:8], AF.Copy, [R_cols], [R_cols], scale=1.0 / 16.0)
    O.memset(cols[0:64, 2:3], 1.0, [R_cols])
    O.memset(cols[32:64, 2:3], -1.0, [R_cols])
    O.act(cols[32:64, 3:4], cols[32:64, 5:6], AF.Copy, [R_cols], [R_cols], scale=-1.0)
    O.act(cols[0:16, 4:5], cols[0:16, 4:5], AF.Exp, [R_cols], [R_cols])
    O.act(cols[0:16, 4:5], cols[0:16, 4:5], AF.Copy, [R_cols], [R_cols], scale=-1.0)

    gA = A.get("G", [128, DM], F32, "gA")
    gB = A.get("G", [128, DM], F32, "gB")
    R_gA = Reg()
    R_gB = Reg()

    hT = A.get("H", [128, 8, S_LEN], BF16, "hT")
    R_hT = regs(NT)

    def norm_a(x_ap, x_regs, ws, i):
        junk, xn, stat, R_junk, R_xn, R_stat = ws
        ss = stat[:, 3 * i:3 * i + 1]
        sd = stat[:, 3 * i + 1:3 * i + 2]
        rs = stat[:, 3 * i + 2:3 * i + 3]
        O.act(junk[:], x_ap, AF.Square, x_regs, [R_junk, R_stat[i]], accum=ss)
        O.act(sd, ss, AF.Ln, [R_stat[i], R_eps], [R_stat[i]], scale=1.0 / DM, bias=eps_col[:])
        O.act(rs, sd, AF.Exp, [R_stat[i]], [R_stat[i]], scale=-0.5)

    def norm_b(x_ap, x_regs, g_bc, g_reg, dstT, dst_col0, dst_regs, ws, i, pbank):
        junk, xn, stat, R_junk, R_xn, R_stat = ws
        k = i % 2
        rs = stat[:, 3 * i + 2:3 * i + 3]
        O.stt(xn[k][:], x_ap, rs, g_bc[:], ALU.mult, ALU.mult, x_regs + [R_stat[i], g_reg], [R_xn[k]])
        pb = bank_bf(pbank)
        for kc in range(8):
            O.tr(pb[:, kc * 128:(kc + 1) * 128], xn[k][:, kc * 128:(kc + 1) * 128], ident_b[:],
                 reads=[R_xn[k], R_const] if kc == 0 else (), writes=[RB[pbank]] if kc == 0 else (), sig=(kc == 7))
        O.cpalt(dstT[:, :, dst_col0:dst_col0 + 128], pb.rearrange("p (k t) -> p k t", k=8), [RB[pbank]], dst_regs)

    def norm_tile(x_ap, x_regs, g_bc, g_reg, dstT, dst_col0, dst_regs, ws, i, pbank):
        norm_a(x_ap, x_regs, ws, i)
        norm_b(x_ap, x_regs, g_bc, g_reg, dstT, dst_col0, dst_regs, ws, i, pbank)

    def norm_all(src, g_bc, g_reg, dstT, dst_regs, ws):
        norm_a(src(0)[0], src(0)[1], ws, 0)
        for t in range(NT):
            if t + 1 < NT:
                norm_a(src(t + 1)[0], src(t + 1)[1], ws, t + 1)
            norm_b(src(t)[0], src(t)[1], g_bc, g_reg, dstT, t * 128, [dst_regs[t]], ws, t, t % 2)

    eps_col = A.get("C", [128, 1], F32, "epscol")
    maskf4 = A.get("C", [128, 4, 128], F32, "maskf4")
    ident_r = A.get("C", [128, 128], F32, "identr")
    O.cp(maskf4[:].bitcast(mybir.dt.float32r), maskf[:].unsqueeze(1).to_broadcast([128, 4, 128]), [R_const], [R_const])
    O.cp(ident_r[:].bitcast(mybir.dt.float32r), ident_f[:], [R_const], [R_const])
    mask01 = A.get("C", [128, 128], F32, "mask01")
    O.ts(mask01[:], maskf[:], 0.0, None, ALU.is_equal, None, [R_const], [R_const])
    F32R = mybir.dt.float32r
    maskf4_l = ident_r[:].bitcast(F32R)
    maskf4_r = maskf4[:].rearrange("p a b -> p (a b)").bitcast(F32R)
    R_eps = Reg()
    O.memset(eps_col[:], EPS, [R_eps])

    def bcast_load(dst, g1d, reg):
        S.dma("sp", dst[:], g1d.partition_broadcast(128), writes=[reg])

    def dump(name, ap, rg):
        dbg.append((name, ap, rg))

    A.reset("R", "MH")
    bcast_load(gA, g_mix, R_gA)
    xt = [A.get("MH", [128, DM], F32, "xt") for _ in range(3)]
    R_xt = regs(3)
    junk = A.get("MH", [128, DM], BF16, "junk")
    xn = [A.get("MH", [128, DM], BF16, "xn") for _ in range(2)]
    stat = A.get("MH", [128, 3 * NT], F32, "stat")
    wsA = (junk, xn, stat, Reg(), regs(2), regs(NT))
    def ld_x(t):
        S.dma("sp", xt[t % 3][:], x_d[t * 128:(t + 1) * 128, :], writes=[R_xt[t % 3]])

    ld_x(0)
    ld_x(1)
    norm_a(xt[0][:], [R_xt[0]], wsA, 0)
    for t in range(NT):
        if t + 2 < NT:
            ld_x(t + 2)
        if t + 1 < NT:
            norm_a(xt[(t + 1) % 3][:], [R_xt[(t + 1) % 3]], wsA, t + 1)
        norm_b(xt[t % 3][:], [R_xt[t % 3]], gA, R_gA, hT, t * 128, [R_hT[t]], wsA, t, t % 2)
    if upto == "A":
        dump("d_hT", hT[:], R_hT)
        return finish(nc, S, A, dbg, out_d, None)

    A.reset("R", "R0", "R1")

    def wcols(c0, n):
        return w_in[:, c0:c0 + n].rearrange("(kc p) c -> p kc c", p=128)

    ccT = A.get("R1", [64, S_LEN], F32, "ccT")
    selh = A.get("R1", [16, 16, 128], F32, "selh")
    dt_tm = A.get("R1", [128, NT, 16], F32, "dt_tm")
    Acs_tm = A.get("R1", [128, NT, 16], F32, "Acs_tm")
    nAcs_tm = A.get("R1", [128, NT, 16], F32, "nAcs_tm")
    cum_tm = A.get("R1", [128, NT, 16], F32, "cum_tm")
    eA_tm = A.get("R1", [128, NT, 16], F32, "eA_tm")
    f2_tm = A.get("R1", [128, NT, 16], F32, "f2_tm")
    cd_bc = A.get("R1", [128, NT, 16], F32, "cd_bc")
    cfirst = A.get("R1", [128, NT, 16], F32, "cfirst")
    biasT = A.get("R1", [128, 4, NT, 16], F32, "biasT")
    dsk_bc = A.get("R1", [128, 16], F32, "dsk_bc")
    R_tab = Reg()
    R_cc = Reg()

    wdtf = A.get("R0", [128, 8, 64], BF16, "wdtf")
    eT = A.get("R0", [64, S_LEN], F32, "eT")
    spT = A.get("R0", [64, S_LEN], F32, "spT")
    dAT = A.get("R0", [64, S_LEN], F32, "dAT")
    lfT = A.get("R0", [64, S_LEN], F32, "lfT")
    onesT = A.get("R0", [64, S_LEN], F32, "onesT")
    tmp_tm = A.get("R0", [128, NT, 16], F32, "tmp_tm")
    R_wdtf = Reg()
    R_e = regs(4)
    R_sp = regs(4)
    R_dAT = Reg()
    R_lf = Reg()
    R_onesT = Reg()
    R_tmp = Reg()

    O.memset(wdtf[:], 0.0, [R_wdtf])
    S.dma("pool", wdtf[:, :, 0:16], wcols(2560, 16), writes=[R_wdtf])
    S.dma("pool", wdtf[:, :, 32:48], wcols(5648, 16), writes=[R_wdtf])
    S.dma("sp", dsk_bc[:], d_skip.partition_broadcast(128), writes=[R_tab])
    O.memset(onesT[:], 1.0, [R_onesT])
    O.memset(ccT[:], 0.0, [R_cc])
    O.cp(selh[:], ident_f[0:16, 0:16].unsqueeze(2).to_broadcast([16, 16, 128]), [R_const], [R_tab])
    for tb in range(4):
        b = tb % 2
        sl = slice(tb * 512, (tb + 1) * 512)
        O.mmg(bank(b)[0:64, :], [(wdtf[:, kc, :], hT[:, kc, sl]) for kc in range(8)],
              reads=[R_wdtf] + R_hT[tb * 4:(tb + 1) * 4], writes=[RB[b]])
        O.act(eT[:, sl], bank(b)[0:64, :], AF.Exp, [RB[b], R_cols], [R_e[tb]],
              bias=cols[0:64, 3:4], scale=cols[0:64, 2:3])
        O.act(spT[:, sl], eT[:, sl], AF.Ln, [R_e[tb]], [R_sp[tb]], bias=1.0)
    O.ts(dAT[0:16, :], spT[0:16, :], cols[0:16, 4:5], None, ALU.mult, None, R_sp + [R_cols], [R_dAT])
    O.ts(lfT[32:48, :], spT[32:48, :], -1.0, None, ALU.mult, None, R_sp, [R_lf])
    for c in range(NT):
        cs = slice(c * 128, (c + 1) * 128)
        O.scan(ccT[0:16, cs], onesT[0:16, cs], dAT[0:16, cs], 0.0, ALU.mult, ALU.add,
               [R_dAT, R_onesT], [R_cc])
    O.scan(ccT[32:48, :], onesT[32:48, :], lfT[32:48, :], 0.0, ALU.mult, ALU.add, [R_lf, R_onesT], [R_cc])
    for tq in range(4):
        b = 2 + tq % 2
        for i in range(4):
            t = tq * 4 + i
            ts_ = slice(t * 128, (t + 1) * 128)
            O.tr(bank(b)[:, i * 128:i * 128 + 64], spT[0:64, ts_], ident_f[0:64, 0:64],
                 reads=R_sp + [R_const] if i == 0 else (), writes=[RB[b]] if i == 0 else ())
            O.tr(bank(b)[:, i * 128 + 64:i * 128 + 128], ccT[0:64, ts_], ident_f[0:64, 0:64],
                 reads=[R_cc] if i == 0 else (), sig=(i == 3))
        v = bank(b).rearrange("p (t w) -> p t w", w=128)
        tsl = slice(tq * 4, tq * 4 + 4)
        O.cp(dt_tm[:, tsl, :], v[:, :, 0:16], [RB[b]], [R_tab])
        O.cp(Acs_tm[:, tsl, :], v[:, :, 64:80], [RB[b]], [R_tab], eng="act")
        O.cp(cum_tm[:, tsl, :], v[:, :, 96:112], [RB[b]], [R_tab])

    def flat(t):
        return t[:].rearrange("p t h -> p (t h)")

    O.ts(flat(nAcs_tm), flat(Acs_tm), -1.0, None, ALU.mult, None, [R_tab], [R_tab])
    O.act(flat(eA_tm), flat(Acs_tm), AF.Exp, [R_tab], [R_tab])
    O.mmg(bank(0)[:, 0:256], [(sel127[:], flat(Acs_tm))], reads=[R_tab, R_const], writes=[RB[0]])
    O.act(flat(cd_bc), bank(0)[:, 0:256], AF.Exp, [RB[0]], [R_tab])
    O.tt(flat(tmp_tm), bank(0)[:, 0:256], flat(Acs_tm), ALU.subtract, [RB[0], R_tab], [R_tmp])
    O.act(flat(tmp_tm), flat(tmp_tm), AF.Exp, [R_tmp], [R_tmp])
    O.tt(flat(f2_tm), flat(tmp_tm), flat(dt_tm), ALU.mult, [R_tmp, R_tab], [R_tab])
    O.mmg(bank(1)[:, 0:256], [(sel0[:], flat(cum_tm))], reads=[R_tab, R_const], writes=[RB[1]])
    O.cp(flat(cfirst), bank(1)[:, 0:256], [RB[1]], [R_tab])
    for qb in range(4):
        O.tt(biasT[:, qb, :, :], cfirst[:, 4 * qb + 2:4 * qb + 3, :].to_broadcast([128, NT, 16]), cum_tm[:],
             ALU.subtract, [R_tab], [R_tab])
    if upto == "B1":
        dump("d_dt", dt_tm[:], [R_tab])
        dump("d_Acs", Acs_tm[:], [R_tab])
        dump("d_cum", cum_tm[:], [R_tab])
        dump("d_cd", cd_bc[:], [R_tab])
        dump("d_f2", f2_tm[:], [R_tab])
        dump("d_bias", biasT[:], [R_tab])
        return finish(nc, S, A, dbg, out_d, None)

    S.barrier()
    A.reset("R0", "MH")
    mixedT = A.get("M", [128, 16, S_LEN], BF16, "mixedT")
    R_mixS = [regs(NT), regs(NT)]
    R_mixA = [[regs(4), regs(4)] for _ in range(8)]
    bcast_load(gB, ssm_norm_w, R_gB)

    w_xs = A.get("R0", [128, 8, 512], BF16, "w_xs")
    w_z = A.get("R0", [128, 8, 512], BF16, "w_z")
    w_bc = A.get("R0", [128, 8, 256], BF16, "w_bc")
    uT = A.get("R0", [128, 6, 515], F32, "uT")
    xsT = [A.get("R0", [128, 4, 512], F32, "xsT") for _ in range(2)]
    bcT = [A.get("R0", [128, 2, 512], BF16, "bcT") for _ in range(2)]
    S_st = A.get("R0", [128, 512], F32, "S_st")
    S_bf = A.get("R0", [128, 512], BF16, "S_bf")
    xs_sb = [A.get("R0", [128, 512], F32, "xs_sb") for _ in range(2)]
    Xb = [A.get("R0", [128, 512], BF16, "Xb") for _ in range(2)]
    Xd = [A.get("R0", [128, 512], BF16, "Xd") for _ in range(2)]
    Btm = [A.get("R0", [128, 128], BF16, "Btm") for _ in range(2)]
    LT = [A.get("MH", [128, 8, 128], F32, "LT") for _ in range(2)]
    MT = [A.get("MH", [128, 8, 128], BF16, "MT") for _ in range(2)]
    sz = [A.get("MH", [128, 512], F32, "sz") for _ in range(2)]
    t1 = A.get("MH", [128, 512], F32, "t1")
    t3 = [A.get("MH", [128, 512], F32, "t3") for _ in range(2)]
    cbm = [A.get("MH", [128, 128], F32, "cbm") for _ in range(2)]
    yg = A.get("MH", [128, 512], F32, "yg")
    cacc = [A.get("MH", [128, 512], F32, "cacc") for _ in range(2)]
    junk2 = A.get("MH", [128, 512], BF16, "junk2")
    ymix = A.get("MH", [128, 512], BF16, "ymix")
    stat2 = A.get("MH", [128, 3 * 32], F32, "stat2")
    R_wg = Reg()
    R_uT = regs(6)
    R_xsT = regs(2)
    R_bcT = regs(2)
    R_S = Reg(); R_Sbf = Reg()
    R_xs = regs(2); R_X = regs(2); R_Xd = regs(2); R_Btm = regs(2); R_LT = regs(2); R_MT = regs(2); R_sz = regs(2)
    R_t1 = Reg(); R_t2 = Reg(); R_t3 = regs(2); R_yg = Reg(); R_cbm = regs(2)
    R_cacc = regs(2); R_junk2 = Reg(); R_ymix = Reg(); R_st2 = regs(32)
    RB2 = [Reg(bank=2) for _ in range(3)]

    def h8(ap):
        return ap.rearrange("p (h d) -> p h d", h=8)

    for g in range(2):
        S.dma("pool", w_xs[:], wcols(1024 + g * 512, 512), writes=[R_wg])
        S.dma("pool", w_z[:], wcols(g * 512, 512), writes=[R_wg])
        S.dma("pool", w_bc[:, :, 0:128], wcols(2048 + g * 128, 128), writes=[R_wg])
        S.dma("pool", w_bc[:, :, 128:256], wcols(2304 + g * 128, 128), writes=[R_wg])
        O.memset(uT[:, :, 0:3], 0.0, R_uT)
        O.memset(S_st[:], 0.0, [R_S])
        O.memset(S_bf[:], 0.0, [R_Sbf])
        jmap = [4 * g + 0, 4 * g + 1, 4 * g + 2, 4 * g + 3, 8 + g, 10 + g]
        hs = slice(g * 8, g * 8 + 8)

        def u_pe(tb, j):
            sl = slice(tb * 512, (tb + 1) * 512)
            wsrc = w_xs[:, :, j * 128:(j + 1) * 128] if j < 4 else w_bc[:, :, (j - 4) * 128:(j - 3) * 128]
            O.mmg(bank(7), [(wsrc[:, kc, :], hT[:, kc, sl]) for kc in range(8)],
                  reads=[R_wg] + R_hT[tb * 4:(tb + 1) * 4], writes=[RB[7]])

        def u_act1(tb, j):
            jj = jmap[j]
            q = j % 2
            O.cp(uT[:, j, 3:515], bank(7), [RB[7]], [R_uT[j]], eng="act")
            O.act(cacc[q][:], uT[:, j, 3:515], AF.Identity, [R_uT[j], R_cols], [R_cacc[q]],
                  bias=cb[:, jj:jj + 1], scale=cw[:, jj, 3:4])

        def u_dve(tb, j):
            jj = jmap[j]
            q = j % 2
            for k_ in (2, 1, 0):
                O.stt(cacc[q][:], uT[:, j, k_:k_ + 512], cw[:, jj, k_:k_ + 1], cacc[q][:], ALU.mult, ALU.add,
                      [R_uT[j], R_cacc[q], R_cols], [R_cacc[q]])
            O.cp(uT[:, j, 0:3], uT[:, j, 512:515], [R_uT[j]], [R_uT[j]])

        def u_act2(tb, j):
            kb = tb % 2
            q = j % 2
            if j < 4:
                O.act(xsT[kb][:, j, :], cacc[q][:], AF.Silu, [R_cacc[q]], [R_xsT[kb]])
            else:
                O.act(bcT[kb][:, j - 4, :], cacc[q][:], AF.Silu, [R_cacc[q]], [R_bcT[kb]])

        def proj_unit(tb, j):
            u_pe(tb, j); u_act1(tb, j); u_dve(tb, j); u_act2(tb, j)

        def bc8(tab, c):
            return tab[:, c, hs].unsqueeze(2).to_broadcast([128, 8, 64])

        def idx(c):
            tb, ci = c // 4, c % 4
            return tb % 2, c % 2, slice(ci * 128, (ci + 1) * 128), slice(c * 128, (c + 1) * 128)

        def s1_pe(c):
            kb, p, cs, gs = idx(c)
            O.mmg(bank(1), [(hT[:, kc, gs], w_z[:, kc, :]) for kc in range(8)],
                  reads=[R_wg, R_hT[c]], writes=[RB[1]])
            for j in range(4):
                O.tr(bank(0)[:, j * 128:(j + 1) * 128], xsT[kb][:, j, cs], ident_f[:],
                     reads=[R_xsT[kb], R_const] if j == 0 else (), writes=[RB[0]] if j == 0 else (),
                     sig=(j == 3))
            O.tr(bank_bf(2)[:, 256:384], bcT[kb][:, 0, cs], ident_b[:], reads=[R_bcT[kb], R_const],
                 writes=[RB2[1]], sig=True)
            O.mmg(bank(2)[:, 0:128], [(bcT[kb][:, 0, cs], bcT[kb][:, 1, cs])], reads=[R_bcT[kb]],
                  writes=[RB2[0]])
            for half in range(2):
                bk = 4 + half
                O.mm(bank(bk), maskf4_l, maskf4_r, True, False, reads=[R_const], writes=[RB[bk]])
                for i in range(4):
                    hh = g * 8 + half * 4 + i
                    O.mm(bank(bk)[:, i * 128:(i + 1) * 128], selh[0:16, hh, :], ccT[0:16, gs], False, i == 3,
                         reads=[R_tab, R_cc] if i == 0 else (), writes=[RB[bk]], sig=(i == 3))

        def s1_act(c):
            kb, p, cs, gs = idx(c)
            O.act(sz[p][:], bank(1), AF.Silu, [RB[1]], [R_sz[p]])
            O.cp(xs_sb[p][:], bank(0), [RB[0]], [R_xs[p]], eng="act")
            O.cp(Btm[p][:], bank_bf(2)[:, 256:384], [RB2[1]], [R_Btm[p]], eng="act")
            for half in range(2):
                bk = 4 + half
                for i in range(4):
                    hh = g * 8 + half * 4 + i
                    O.act(LT[p][:, half * 4 + i, :], bank(bk)[:, i * 128:(i + 1) * 128], AF.Exp,
                          [RB[bk], R_tab], [R_LT[p]], bias=nAcs_tm[:, c, hh:hh + 1])

        def s1_dve_a(c):
            kb, p, cs, gs = idx(c)
            xs3 = h8(xs_sb[p][:])
            O.tt(h8(Xb[p][:]), xs3, bc8(dt_tm, c), ALU.mult, [R_xs[p], R_tab], [R_X[p]])
            O.tt(h8(Xd[p][:]), xs3, bc8(f2_tm, c), ALU.mult, [R_xs[p], R_tab], [R_Xd[p]])
            O.tt(h8(t3[p][:]), xs3, dsk_bc[:, hs].unsqueeze(2).to_broadcast([128, 8, 64]), ALU.mult,
                 [R_xs[p], R_tab], [R_t3[p]], eng="pool")

        def s1_dve_b(c):
            kb, p, cs, gs = idx(c)
            O.tt(MT[p][:], LT[p][:], bank(2)[:, 0:128].unsqueeze(1).to_broadcast([128, 8, 128]), ALU.mult,
                 [R_LT[p], RB2[0]], [R_MT[p]])

        def s2_pe_a(c):
            kb, p, cs, gs = idx(c)
            for i in range(8):
                O.mm(bank(3)[:, i * 64:(i + 1) * 64], MT[p][:, i, :], Xb[p][:, i * 64:(i + 1) * 64], True, True,
                     reads=[R_MT[p], R_X[p]] if i == 0 else (), writes=[RB[3]] if i == 0 else (), sig=(i == 7))
            O.mmg(bank(7), [(Btm[p][:], Xd[p][:])], reads=[R_Btm[p], R_Xd[p]], writes=[RB[7]])
            O.mmg(bank(6), [(bcT[kb][:, 1, cs], S_bf[:])], reads=[R_bcT[kb], R_Sbf], writes=[RB[6]])

        def s2_state(c):
            S3 = h8(S_st[:])
            O.tt(S3, S3, bc8(cd_bc, c), ALU.mult, [R_S, R_tab], [R_S])
            O.tt(S_st[:], S_st[:], bank(7), ALU.add, [R_S, RB[7]], [R_S])
            O.cp(S_bf[:], S_st[:], [R_S], [R_Sbf], eng="act")

        def s2_dve_a(c):
            kb, p, cs, gs = idx(c)
            O.tt(h8(t1[:]), h8(bank(6)), bc8(eA_tm, c), ALU.mult, [RB[6], R_tab], [R_t1])
            O.tt(t1[:], bank(3), t1[:], ALU.add, [RB[3], R_t1], [R_t1])
            O.tt(t1[:], t1[:], t3[p][:], ALU.add, [R_t1, R_t3[p]], [R_t1])
            O.tt(yg[:], t1[:], sz[p][:], ALU.mult, [R_t1, R_sz[p]], [R_yg])
            si = g * 16 + c
            ss = stat2[:, 3 * si:3 * si + 1]
            S.op("dve", lambda e, ss=ss: e.scalar_tensor_tensor(out=junk2[:], in0=yg[:], scalar=1.0, in1=yg[:],
                                                                op0=ALU.mult, op1=ALU.mult, accum_out=ss),
                 [R_yg], [R_junk2, R_st2[si]])

        def s2_act_a(c):
            si = g * 16 + c
            ss = stat2[:, 3 * si:3 * si + 1]
            sd = stat2[:, 3 * si + 1:3 * si + 2]
            rs = stat2[:, 3 * si + 2:3 * si + 3]
            O.act(sd, ss, AF.Ln, [R_st2[si], R_eps], [R_st2[si]], scale=1.0 / 512.0, bias=eps_col[:])
            O.act(rs, sd, AF.Exp, [R_st2[si]], [R_st2[si]], scale=-0.5)

        def s2_tail(c):
            kb, p, cs, gs = idx(c)
            si = g * 16 + c
            rs = stat2[:, 3 * si + 2:3 * si + 3]
            O.stt(ymix[:], yg[:], rs, gB[:, g * 512:(g + 1) * 512], ALU.mult, ALU.mult,
                  [R_yg, R_st2[si], R_gB], [R_ymix])
            for j in range(4):
                O.tr(bank_bf(2)[:, 512 + j * 128:512 + (j + 1) * 128], ymix[:, j * 128:(j + 1) * 128],
                     ident_b[:], reads=[R_ymix, R_const] if j == 0 else (),
                     writes=[RB2[2]] if j == 0 else (), sig=(j == 3))
            O.cp(mixedT[:, g * 4:(g + 1) * 4, gs],
                 bank_bf(2)[:, 512:1024].rearrange("p (k t) -> p k t", k=4), [RB2[2]], [R_mixS[g][c]], eng="act")

        for j in range(6):
            proj_unit(0, j)
        s1_pe(0); s1_act(0); s1_dve_a(0); s1_dve_b(0)
        for c in range(NT):
            tb, ci = c // 4, c % 4
            units = []
            if tb + 1 < 4 and ci < 3:
                units = [(tb + 1, 2 * ci), (tb + 1, 2 * ci + 1)]
            n = c + 1 if c + 1 < NT else None
            s2_pe_a(c)
            for u in units:
                u_pe(*u) if False else None
            if n is not None:
                s1_pe(n)
            s2_state(c)
            s2_dve_a(c)
            ulist = list(units)
            if ulist:
                u_pe(*ulist[0]); u_act1(*ulist[0])
            if n is not None:
                s1_act(n)
            s2_act_a(c)
            if ulist:
                u_dve(*ulist[0])
                u_pe(*ulist[1]); u_act1(*ulist[1])
            s2_tail(c)
            if n is not None:
                s1_dve_a(n)
            if ulist:
                u_act2(*ulist[0])
                u_dve(*ulist[1])
            if n is not None:
                s1_dve_b(n)
            if ulist:
                u_act2(*ulist[1])
    if upto == "SSD":
        dump("d_mixS", mixedT[:, 0:8, :], R_mixS[0] + R_mixS[1])
        dump("d_dt", dt_tm[:], [R_tab])
        dump("d_Acs", Acs_tm[:], [R_tab])
        dump("d_cum", cum_tm[:], [R_tab])
        dump("d_cd", cd_bc[:], [R_tab])
        dump("d_f2", f2_tm[:], [R_tab])
        dump("d_bias", biasT[:], [R_tab])
        return finish(nc, S, A, dbg, out_d, None)

    S.barrier()
    A.reset("R0")
    Vh = [A.get("R0", [128, 8, 8, 3, 64], BF16, "Vh%d" % i) for i in range(2)]
    R_V = regs(NT)
    mark = A.cur["R0"]
    A.reset("MH")
    wv = [A.get("MH", [128, 8, 512], BF16, "wv") for _ in range(2)]
    R_wv = regs(2)
    for i in range(2):
        O.memset(Vh[i][:, :, :, 1, :], 1.0, R_V[i * 8:(i + 1) * 8])
    for cbk in range(2):
        S.dma("pool", wv[cbk][:], wcols(4624 + cbk * 512, 512), writes=[R_wv[cbk]])
    for t in range(NT):
        for cbk in range(2):
            b = (2 * t + cbk) % 4
            O.mmg(bank(b), [(hT[:, kc, t * 128:(t + 1) * 128], wv[cbk][:, kc, :]) for kc in range(8)],
                  reads=[R_hT[t], R_wv[cbk]], writes=[RB[b]])
            O.cpalt(Vh[t // 8][:, t % 8, 4 * cbk:4 * cbk + 4, 0:3:2, :],
                    bank(b).rearrange("p (q s d) -> p q s d", q=4, s=2), [RB[b]], [R_V[t]])
    if upto == "V":
        return finish(nc, S, A, dbg, out_d, None)
    A.cur["R0"] = mark
    A.reset("R1")
    A.reset("G")
    wqk = [A.get("G", [128, 8, 256], BF16, "wqk") for _ in range(2)]
    R_wqk = regs(2)
    qkT = [A.get("R0", [128, 3, S_LEN], BF16, "qkT0"), A.get("R1", [128, 3, S_LEN], BF16, "qkT1")]
    R_qk = [[regs(4), regs(4)] for _ in range(2)]
    sq = A.get("R1", [128, 512], BF16, "sq")
    raw = A.get("R1", [128, 512], F32, "raw")
    sdv = A.get("R1", [128, 512], F32, "sdv")
    rsv = A.get("R1", [128, 512], F32, "rsv")
    rden = [A.get("R1", [128, 512], F32, "rden") for _ in range(2)]
    rscr = A.get("R1", [128, 512], F32, "rscr")
    R_rscr = Reg()
    PT = [A.get("R0", [128, 512], BF16, "PT") for _ in range(4)]
    R_PT = regs(4)
    R_sq = Reg(); R_raw = Reg(); R_sd = Reg(); R_rs = Reg(); R_rden = regs(2)
    LOOK = 2
    for wb_ in range(2):
        O.memset(qkT[wb_][64:128, 1, :], 0.0, R_qk[wb_][1])
        O.memset(qkT[wb_][0:64, 2, :], 0.0, R_qk[wb_][1])

    def load_wqk(pp):
        wb = pp % 2
        S.dma("pool", wqk[wb][:, :, 0:128], wcols(2576 + pp * 128, 128), writes=[R_wqk[wb]])
        S.dma("pool", wqk[wb][:, :, 128:256], wcols(3600 + pp * 128, 128), writes=[R_wqk[wb]])

    def qk_block_a(pp, which, tb):
        wb = pp % 2
        sl = slice(tb * 512, (tb + 1) * 512)
        O.mmg(bank(7), [(wqk[wb][:, kc, which * 128:(which + 1) * 128], hT[:, kc, sl]) for kc in range(8)],
              reads=[R_wqk[wb]] + R_hT[tb * 4:(tb + 1) * 4], writes=[RB[7]])
        O.cp(raw[:], bank(7), [RB[7]], [R_raw])
        O.tt(sq[:], raw[:], raw[:], ALU.mult, [R_raw], [R_sq])

    def qk_block_b(pp, which, tb):
        wb = pp % 2
        sl = slice(tb * 512, (tb + 1) * 512)
        O.mmg(bank(7), [(bones_b[:], sq[:])], reads=[R_sq, R_const], writes=[RB[7]])
        O.act(sdv[:], bank(7), AF.Ln, [RB[7], R_eps], [R_sd], scale=1.0 / 64.0, bias=eps_col[:])
        O.act(rsv[:], sdv[:], AF.Exp, [R_sd], [R_rs], scale=-0.5)
        if which == 0:
            O.stt(qkT[wb][:, 0, sl], raw[:], cols[:, 0:1], rsv[:], ALU.mult, ALU.mult,
                  [R_raw, R_rs, R_cols], [R_qk[wb][0][tb]])
        else:
            O.stt(qkT[wb][0:64, 1, sl], raw[0:64, :], cols[0:64, 1:2], rsv[0:64, :], ALU.mult, ALU.mult,
                  [R_raw, R_rs, R_cols], [R_qk[wb][1][tb]])
            O.stt(qkT[wb][64:128, 2, sl], raw[64:128, :], cols[64:128, 1:2], rsv[64:128, :], ALU.mult, ALU.mult,
                  [R_raw, R_rs, R_cols], [R_qk[wb][1][tb]])

    state = {"s": 0, "xy": 0}

    def emit_S(pp, st):
        hh, qb, kt, bs, pi, xy = st
        wb = pp % 2
        head = 2 * pp + hh
        ps_ = slice(hh * 64, hh * 64 + 64)
        j = kt - 4 * qb
        c0 = max(j, 0) * 128
        diag = j >= 0
        O.mm(bank(bs)[:, c0:512], qkT[wb][:, 1 + hh, kt * 128:(kt + 1) * 128],
             qkT[wb][:, 0, qb * 512 + c0:(qb + 1) * 512], True, not diag,
             reads=[R_qk[wb][1][kt // 4], R_qk[wb][0][qb]], writes=[RB[bs]], sig=not diag)
        if diag:
            O.mm(bank(bs)[:, c0:c0 + 128], ident_b[:], mask_b[:], False, True,
                 reads=[R_const], writes=[RB[bs]], sig=True)
        O.act(PT[pi][:, c0:512], bank(bs)[:, c0:512], AF.Exp, [RB[bs], R_tab], [R_PT[pi]],
              bias=biasT[:, qb, kt, head:head + 1])

    def emit_PV(pp, st):
        hh, qb, kt, bs, pi, xy = st
        j = kt - 4 * qb
        c0 = max(j, 0) * 128
        nk = 4 * qb + 4
        bx = 3 + xy
        lhsT = Vh[kt // 8][:, kt % 8, pp, hh:hh + 2, :].rearrange("p s d -> p (s d)")
        O.mm(bank(bx)[:, c0:512], lhsT, PT[pi][:, c0:512], kt == 0, kt == nk - 1,
             reads=[R_V[kt], R_PT[pi]], writes=[RB[bx]], sig=True)
        if kt == nk - 1:
            po = slice(hh * 64, hh * 64 + 64)
            pd = slice(64 - hh * 64, 128 - hh * 64)
            O.recip(rden[xy][pd, :], bank(bx)[pd, :], [RB[bx]], [R_rden[xy]])
            O.tt(mixedT[po, 8 + pp, qb * 512:(qb + 1) * 512], bank(bx)[po, :], rden[xy][pd, :], ALU.mult,
                 [RB[bx], R_rden[xy]], [R_mixA[pp][hh][qb]] + (R_wv if pp == 0 else []))

    load_wqk(0)
    for which in range(2):
        for tb in range(4):
            qk_block_a(0, which, tb)
            qk_block_b(0, which, tb)
    for pp in range(8):
        pend = []
        if pp + 1 < 8:
            load_wqk(pp + 1)
            for which in range(2):
                for tb in range(4):
                    pend.append((qk_block_a, (pp + 1, which, tb)))
                    pend.append((qk_block_b, (pp + 1, which, tb)))
        steps = []
        for hh in range(2):
            for qb in range(4):
                xy = state["xy"] % 2
                state["xy"] += 1
                for kt in range(4 * qb + 4):
                    steps.append((hh, qb, kt, (0, 1, 2, 5)[state["s"] % 4], state["s"] % 4, xy))
                    state["s"] += 1
        n = len(steps)
        for i in range(n + LOOK):
            if i < n:
                emit_S(pp, steps[i])
            if i >= LOOK:
                emit_PV(pp, steps[i - LOOK])
            if pend and i % 5 == 2:
                f_, a_ = pend.pop(0)
                f_(*a_)
        while pend:
            f_, a_ = pend.pop(0)
            f_(*a_)
        if pp == 6:
            wo = nc.alloc_sbuf_tensor_at("wo_pref", [128, 16, DM], BF16, offset=A.reg["H"][0])
            R_wo = Reg()
            S.dma("pool", wo[:, 0:8, :], w_out[0:1024, :].rearrange("(fc p) c -> p fc c", p=128),
                  writes=[R_wo] + R_hT)
            S.dma("pool", wo[:, 8:16, :], w_out[1024:2048, :].rearrange("(fc p) c -> p fc c", p=128),
                  writes=[R_wo])
    R_mixA_all = [R_mixA[pp][hh][qb] for pp in range(8) for hh in range(2) for qb in range(4)]
    if upto == "ATT":
        dump("d_mixA", mixedT[:, 8:16, :], R_mixA_all)
        return finish(nc, S, A, dbg, out_d, None)

    S.barrier()
    A.reset("R", "R0", "R1", "H")
    x1 = A.get("R", [128, NT, DM], F32, "x1")
    R_x1 = [[Reg(), Reg()] for _ in range(NT)]
    rtail_mark = A.cur["R"]
    A.get("H", [128, 16, DM], BF16, "wo_placeholder")
    xr = [A.get("R", [128, DM], F32, "xr") for _ in range(2)]
    R_xr = regs(2)
    for t in range(NT):
        S.dma("sp", xr[t % 2][:], x_d[t * 128:(t + 1) * 128, :], writes=[R_xr[t % 2]])
        tsl = slice(t * 128, (t + 1) * 128)
        rd = [R_wo, R_mixS[0][t], R_mixS[1][t]] + [R_mixA[pp][hh][t // 4] for pp in range(8) for hh in range(2)]
        for cbk in range(2):
            b = (2 * t + cbk) % 4
            csl = slice(cbk * 512, (cbk + 1) * 512)
            O.mmg(bank(b), [(mixedT[:, fc, tsl], wo[:, fc, csl]) for fc in range(16)], reads=rd, writes=[RB[b]])
            O.tt(x1[:, t, csl], bank(b), xr[t % 2][:, csl], ALU.add, [RB[b], R_xr[t % 2]], [R_x1[t][cbk]])
    R_x1_all = [r for p in R_x1 for r in p]
    if upto == "X1":
        dump("d_x1", x1[:], R_x1_all)
        dump("d_mixA", mixedT[:, 8:16, :], R_mixA_all)
        return finish(nc, S, A, dbg, out_d, None)

    S.barrier()
    A.reset("H", "M", "ML", "MH")
    A.cur["R"] = rtail_mark
    h2T = A.get("H", [128, 8, S_LEN], BF16, "h2T")
    R_h2 = regs(NT)
    bcast_load(gA, g_xattn, R_gA)
    bcast_load(gB, g_mem, R_gB)
    qxT = A.get("ML", [128, 8, S_LEN], BF16, "qxT")
    R_qx = [regs(4) for _ in range(4)]
    junkD = A.get("MH", [128, DM], BF16, "junkD")
    xnD = [A.get("MH", [128, DM], BF16, "xnD") for _ in range(2)]
    statD = A.get("MH", [128, 3 * NT], F32, "statD")
    statM = A.get("MH", [128, 8], F32, "statM")
    mtile = [A.get("MH", [128, DM], F32, "mtile") for _ in range(2)]
    memT = A.get("MH", [128, 8, 256], BF16, "memT")
    kxT = A.get("MH", [128, 8, 256], BF16, "kxT")
    vx = A.get("MH", [128, 2, DM], BF16, "vx")
    rdenD = A.get("MH", [128, 512], F32, "rdenD")
    wbD = [A.get("R", [128, 8, 512], BF16, "wbD") for _ in range(2)]
    raw2 = A.get("R", [128, 2, 512], F32, "raw2")
    sq2 = A.get("R", [128, 2, 512], BF16, "sq2")
    sd2 = A.get("R", [128, 512], F32, "sd2")
    rs2 = A.get("R", [128, 512], F32, "rs2")
    PTx = [A.get("R", [128, 512], BF16, "PTx") for _ in range(4)]
    R_wbD = regs(2); R_mt = regs(2); R_memT = regs(2); R_kx = regs(4); R_vx = regs(2)
    R_raw2 = regs(2); R_sq2 = regs(2); R_sd2 = Reg(); R_rs2 = Reg(); R_PTx = regs(4); R_rdenD = Reg()
    wsD = (junkD, xnD, statD, Reg(), regs(2), regs(NT))
    norm_all(lambda t: (x1[:, t, :], R_x1[t]), gA, R_gA, h2T, R_h2, wsD)
    wsM = (junkD, xnD, statM, wsD[3], wsD[4], regs(2))
    for m_ in range(2):
        S.dma("sp", mtile[m_][:], mem_d[m_ * 128:(m_ + 1) * 128, :], writes=[R_mt[m_]])
        norm_tile(mtile[m_][:], [R_mt[m_]], gB, R_gB, memT, m_ * 128, [R_memT[m_]], wsM, m_, m_ % 2)
    nwb = 0

    def load_wb(src_ap):
        nonlocal nwb
        k = nwb % 2
        nwb += 1
        S.dma("pool", wbD[k][:], src_ap, writes=[R_wbD[k]])
        return wbD[k], R_wbD[k]

    S.barrier()
    mh0 = A.reg["MH"][0]
    raw2s = [raw2, nc.alloc_sbuf_tensor_at("raw2b", [128, 2, 512], F32, offset=mh0)]
    sq2s = [sq2, nc.alloc_sbuf_tensor_at("sq2b", [128, 2, 512], BF16, offset=mh0 + 4096)]
    mt0 = mh0 + 2048 + 2 * 2048 + 192 + 64
    sd2s = [sd2, nc.alloc_sbuf_tensor_at("sd2b", [128, 512], F32, offset=mt0)]
    rs2s = [rs2, nc.alloc_sbuf_tensor_at("rs2b", [128, 512], F32, offset=mt0 + 2048)]
    rdens = [rdenD, nc.alloc_sbuf_tensor_at("rdenDb", [128, 512], F32, offset=mt0 + 4096)]
    R_raw2 = [regs(2), regs(2)]
    R_sq2 = [regs(2), regs(2)]
    R_sd2 = regs(2); R_rs2 = regs(2); R_rdenD = regs(2)
    hn = [0]

    def proj_norm(lhs_fn, rhs_fn, reads, ncol, gcol0, dst_fn, dst_regs):
        k = hn[0] % 2
        hn[0] += 1
        pb = (3 * k, 3 * k + 1)
        nb = 3 * k + 2
        for dc in range(2):
            O.mmg(bank(pb[dc])[:, 0:ncol], [(lhs_fn(dc, kc), rhs_fn(kc)) for kc in range(8)],
                  reads=reads, writes=[RB[pb[dc]]])
        for dc in range(2):
            O.cp(raw2s[k][:, dc, 0:ncol], bank(pb[dc])[:, 0:ncol], [RB[pb[dc]]], [R_raw2[k][dc]], eng="act")
            O.tt(sq2s[k][:, dc, 0:ncol], raw2s[k][:, dc, 0:ncol], raw2s[k][:, dc, 0:ncol], ALU.mult,
                 [R_raw2[k][dc]], [R_sq2[k][dc]])
        O.mmg(bank(nb)[:, 0:ncol], [(ones_b[:], sq2s[k][:, dc, 0:ncol]) for dc in range(2)],
              reads=R_sq2[k] + [R_const], writes=[RB[nb]])
        O.act(sd2s[k][:, 0:ncol], bank(nb)[:, 0:ncol], AF.Ln, [RB[nb], R_eps], [R_sd2[k]], scale=1.0 / 256.0,
              bias=eps_col[:])
        O.act(rs2s[k][:, 0:ncol], sd2s[k][:, 0:ncol], AF.Exp, [R_sd2[k]], [R_rs2[k]], scale=-0.5)
        for dc in range(2):
            O.stt(dst_fn(dc), raw2s[k][:, dc, 0:ncol], cols[:, gcol0 + dc:gcol0 + dc + 1], rs2s[k][:, 0:ncol],
                  ALU.mult, ALU.mult, [R_raw2[k][dc], R_rs2[k], R_cols], dst_regs)

    for cbk in range(2):
        wbuf, rw = load_wb(xkv_w[:, cbk * 512:(cbk + 1) * 512].rearrange("(kc p) c -> p kc c", p=128))
        for hl in range(2):
            hd = cbk * 2 + hl
            proj_norm(lambda dc, kc, wbuf=wbuf, hl=hl: wbuf[:, kc, (hl * 2 + dc) * 128:(hl * 2 + dc + 1) * 128],
                      lambda kc: memT[:, kc, :], [rw] + R_memT, 256, 8,
                      lambda dc, hd=hd: kxT[:, 2 * hd + dc, :], [R_kx[hd]])
    for cbk in range(2):
        wbuf, rw = load_wb(xkv_w[:, 1024 + cbk * 512:1024 + (cbk + 1) * 512].rearrange("(kc p) c -> p kc c", p=128))
        for m_ in range(2):
            b = 6 + m_
            O.mmg(bank(b), [(memT[:, kc, m_ * 128:(m_ + 1) * 128], wbuf[:, kc, :]) for kc in range(8)],
                  reads=[rw, R_memT[m_]], writes=[RB[b]])
            O.cpalt(vx[:, m_, cbk * 512:(cbk + 1) * 512], bank(b), [RB[b]], [R_vx[m_]])
    for cbk in range(2):
        wbuf, rw = load_wb(xq_w[:, cbk * 512:(cbk + 1) * 512].rearrange("(kc p) c -> p kc c", p=128))
        for hl in range(2):
            hd = cbk * 2 + hl
            for tb in range(4):
                sl = slice(tb * 512, (tb + 1) * 512)
                proj_norm(lambda dc, kc, wbuf=wbuf, hl=hl: wbuf[:, kc, (hl * 2 + dc) * 128:(hl * 2 + dc + 1) * 128],
                          lambda kc, sl=sl: h2T[:, kc, sl], [rw] + R_h2[tb * 4:(tb + 1) * 4], 512, 6,
                          lambda dc, hd=hd, sl=sl: qxT[:, 2 * hd + dc, sl], [R_qx[hd][tb]])
    S.barrier()
    A.reset("H")
    oxT = A.get("H", [128, 8, S_LEN], BF16, "oxT")
    R_ox = regs(NT // 4)
    npx = 0
    it = 0
    for hd in range(4):
        for tb in range(4):
            sl = slice(tb * 512, (tb + 1) * 512)
            k = it % 2
            it += 1
            pts = []
            for m_ in range(2):
                b = 2 * k + m_
                O.mmg(bank(b), [(kxT[:, 2 * hd + dc, m_ * 128:(m_ + 1) * 128], qxT[:, 2 * hd + dc, sl]) for dc in range(2)],
                      reads=[R_kx[hd], R_qx[hd][tb]], writes=[RB[b]])
                pi = npx % 4
                npx += 1
                O.act(PTx[pi][:], bank(b), AF.Exp, [RB[b]], [R_PTx[pi]])
                pts.append(pi)
            for dc in range(2):
                b = 4 + dc
                O.mmg(bank(b), [(vx[:, m_, hd * 256 + dc * 128:hd * 256 + (dc + 1) * 128], PTx[pts[m_]][:]) for m_ in range(2)],
                      reads=R_vx + [R_PTx[p] for p in pts], writes=[RB[b]])
            bd = 6 + k
            O.mmg(bank(bd), [(ones_b[:], PTx[pts[m_]][:]) for m_ in range(2)],
                  reads=[R_const] + [R_PTx[p] for p in pts], writes=[RB[bd]])
            O.act(rdens[k][:], bank(bd), AF.Ln, [RB[bd]], [R_rdenD[k]])
            O.act(rdens[k][:], rdens[k][:], AF.Exp, [R_rdenD[k]], [R_rdenD[k]], scale=-1.0)
            for dc in range(2):
                O.tt(oxT[:, 2 * hd + dc, sl], bank(4 + dc), rdens[k][:], ALU.mult, [RB[4 + dc], R_rdenD[k]], [R_ox[tb]])
    wxo = []
    for cbk in range(2):
        wxo.append(load_wb(xo_w[:, cbk * 512:(cbk + 1) * 512].rearrange("(kc p) c -> p kc c", p=128)))
    h3T = nc.alloc_sbuf_tensor_at("h3T", [128, 8, S_LEN], BF16, offset=A.reg["ML"][0])
    R_h3 = regs(NT)
    bcast_load(gA, g_mlp, R_gA)
    statE = A.get("MH", [128, 3 * NT], F32, "statE")
    wsE = (junkD, xnD, statE, Reg(), regs(2), regs(NT))
    R_qx_all = [r for hq in R_qx for r in hq]
    for t in range(NT):
        tsl = slice(t * 128, (t + 1) * 128)
        for cbk in range(2):
            b = (2 * t + cbk) % 4
            csl = slice(cbk * 512, (cbk + 1) * 512)
            wbuf, rw = wxo[cbk]
            O.mmg(bank(b), [(oxT[:, c_, tsl], wbuf[:, c_, :]) for c_ in range(8)],
                  reads=[rw, R_ox[t // 4]], writes=[RB[b]])
            O.tt(x1[:, t, csl], bank(b), x1[:, t, csl], ALU.add, [RB[b], R_x1[t][cbk]], [R_x1[t][cbk]])
        norm_a(x1[:, t, :], R_x1[t], wsE, t)
        norm_b(x1[:, t, :], R_x1[t], gA, R_gA, h3T, t * 128, [R_h3[t]] + (R_qx_all if t == 0 else []), wsE, t,
               4 + t % 2)
    if upto == "X2":
        dump("d_x2", x1[:], R_x1_all)
        return finish(nc, S, A, dbg, out_d, None)

    S.barrier()
    A.reset("H", "MH")
    A.cur["R"] = rtail_mark
    rr = [A.get("MH", [128, 512], F32, "rr") for _ in range(2)]
    wdn = [A.get("MH", [128, 4, DM], BF16, "wdn") for _ in range(2)]
    uT2 = [A.get("H", [128, 4, S_LEN], BF16, "uT2") for _ in range(2)]
    wup = [A.get("R", [128, 8, 512], BF16, "wup") for _ in range(2)]
    R_rr = regs(2); R_wdn = regs(2); R_wup = regs(2)
    R_u2 = [regs(4), regs(4)]
    R_out = regs(NT)
    nrr = 0
    for grp in range(8):
        ub = grp % 2
        S.dma("pool", wup[ub][:], w_up[:, grp * 512:(grp + 1) * 512].rearrange("(kc p) c -> p kc c", p=128),
              writes=[R_wup[ub]])
        S.dma("pool", wdn[ub][:], w_down[grp * 512:(grp + 1) * 512, :].rearrange("(j p) c -> p j c", p=128),
              writes=[R_wdn[ub]])
        for j in range(4):
            for tb in range(4):
                sl = slice(tb * 512, (tb + 1) * 512)
                b = (j * 4 + tb) % 4
                O.mmg(bank(b), [(wup[ub][:, kc, j * 128:(j + 1) * 128], h3T[:, kc, sl]) for kc in range(8)],
                      reads=[R_wup[ub]] + R_h3[tb * 4:(tb + 1) * 4], writes=[RB[b]])
                ri = nrr % 2
                nrr += 1
                O.act(rr[ri][:], bank(b), AF.Relu, [RB[b]], [R_rr[ri]])
                O.tt(uT2[ub][:, j, sl], rr[ri][:], bank(b), ALU.mult, [R_rr[ri], RB[b]], [R_u2[ub][tb]])
        for t in range(NT):
            tsl = slice(t * 128, (t + 1) * 128)
            for cbk in range(2):
                b = 4 + (2 * t + cbk) % 4
                csl = slice(cbk * 512, (cbk + 1) * 512)
                O.mmg(bank(b), [(uT2[ub][:, j, tsl], wdn[ub][:, j, csl]) for j in range(4)],
                      reads=[R_wdn[ub], R_u2[ub][t // 4]], writes=[RB[b]])
                O.tt(x1[:, t, csl], bank(b), x1[:, t, csl], ALU.add, [RB[b], R_x1[t][cbk]], [R_x1[t][cbk]])
            if grp == 7:
                S.dma("sp", out_d[tsl, :], x1[:, t, :], reads=R_x1[t], writes=[R_out[t]])
    return finish(nc, S, A, dbg, out_d, R_out)


_DBG = {}


def finish(nc, S, A, dbg, out_d, out_regs):
    fin = []
    for name, ap, rg in dbg:
        shp = list(ap.shape)
        d = nc.dram_tensor(name, shp, ap.dtype, kind="ExternalOutput").ap()
        r = Reg()
        S.dma("sp", d, ap, reads=list(rg), writes=[r])
        fin.append(r)
    if out_regs is not None:
        fin += list(out_regs)
    S.final_wait("sp", fin)
    if S.pe_pending:
        raise RuntimeError("pe pending")
    S.emit()
    _DBG["est_total_us"] = S.est_total
    _DBG["n_ops"] = len(S.ops)
    return nc


def build_rest(L):
    raise NotImplementedError


def _consts():
    ident = np.eye(128, dtype=np.float32)
    s = np.arange(128)[:, None]
    l = np.arange(128)[None, :]
    mask = np.where(s > l, np.float32(NEG), np.float32(0.0)).astype(np.float32)
    return ident, mask


_W_NAMES = ["g_mix", "w_in", "conv_w", "conv_b", "dt_bias", "a_log", "d_skip", "ssm_norm_w", "g_q", "g_k",
            "f_bias", "w_out", "g_xattn", "g_mem", "xq_w", "xkv_w", "xg_q", "xg_k", "xo_w", "g_mlp", "w_up",
            "w_down"]


def make_in_maps(inputs, n_cores=8):
    ident, mask = _consts()
    shared = {k: np.ascontiguousarray(np.asarray(inputs[k], dtype=np.float32)[0]) for k in _W_NAMES}
    shared["c_ident"] = ident
    shared["c_mask"] = mask
    x = np.asarray(inputs["x"], dtype=np.float32)
    mem = np.asarray(inputs["mem"], dtype=np.float32)
    maps = []
    for c in range(n_cores):
        m = dict(shared)
        m["x"] = np.ascontiguousarray(x[c])
        m["mem"] = np.ascontiguousarray(mem[c])
        maps.append(m)
    return maps


def kernel(**inputs):
    nc = build_nc("all")
    in_maps = make_in_maps(inputs)
    res = run_bass_kernel_spmd(nc, in_maps, core_ids=list(range(8)))
    out = np.stack([np.asarray(r["out"], dtype=np.float32) for r in res.results], axis=0)
    return out
```

```python
import numpy as np
import concourse.bass as bass
import concourse.mybir as mybir
from concourse.bass_utils import run_bass_kernel_spmd

F32 = mybir.dt.float32
BF16 = mybir.dt.bfloat16
AF = mybir.ActivationFunctionType
ALU = mybir.AluOpType

ENGS = ("pe", "act", "dve", "pool", "sp")

S_LEN = 2048
NT = 16
DM = 1024
EPS = 1e-5
NEG = -30000.0


class Reg:
    __slots__ = ("W", "R", "bank")

    def __init__(self, bank=None):
        self.W = []
        self.R = []
        self.bank = bank


def regs(n):
    return [Reg() for _ in range(n)]


class _Op:
    __slots__ = ("i", "eng", "fns", "est", "preds", "kind", "epoch", "tab", "nun", "succ", "ready", "start",
                 "fin", "pos", "dkey", "dval", "done")


LAT_X = 0.35
LAT_S = 0.05
TAB_COST = 1.3


class Sched:
    def __init__(self, nc, n_dma_sems=32):
        self.nc = nc
        self.sem = {}
        self._ctx = []
        for e in ENGS:
            if e == "sp":
                continue
            c = nc.semaphore("s_" + e)
            self.sem[e] = c.__enter__()
            self._ctx.append(c)
        self.n_dma = n_dma_sems
        for i in range(n_dma_sems):
            c = nc.semaphore("s_dma%d" % i)
            self.sem[("dma", i)] = c.__enter__()
            self._ctx.append(c)
        self.ops = []
        self.epoch = 0
        self._pend = []
        self.bank_last = {}
        self.bank_w = {}
        self.bank_r = {}
        self.pe_pending = False

    def _new(self, eng, fns, est, kind, tab, reads, writes):
        op = _Op()
        op.i = len(self.ops)
        op.eng = eng
        op.fns = fns
        op.est = est
        op.kind = kind
        op.tab = tab
        op.epoch = self.epoch
        op.succ = []
        op.done = False
        preds = {}

        def add(p, raw):
            if p is op or p.epoch != op.epoch:
                return
            need = True
            if kind != "dma" and p.kind != "dma" and p.eng == eng and not raw and eng == "pe":
                need = False
            preds[p] = preds.get(p, False) or need

        for r in reads:
            for w in r.W:
                add(w, True)
            if r.bank is not None and eng in ("act", "dve"):
                lr = self.bank_last.get((r.bank, "dve" if eng == "act" else "act"))
                if lr is not None:
                    add(lr, False)
                lr = self.bank_last.get((r.bank, eng))
                if lr is not None:
                    add(lr, False)
            if r.bank is not None:
                bw = self.bank_w.get(r.bank)
                if bw is not None:
                    add(bw, True)
        for w in writes:
            for x in w.W:
                add(x, False)
            for x in w.R:
                add(x, False)
            if w.bank is not None and eng == "pe":
                for x in self.bank_r.get(w.bank, ()):
                    add(x, False)
        op.preds = preds
        for r in reads:
            if r.bank is not None and eng != "pe":
                self.bank_r.setdefault(r.bank, []).append(op)
        for w in writes:
            if w.bank is not None and eng == "pe":
                self.bank_w[w.bank] = op
                self.bank_r[w.bank] = []
        for r in reads:
            r.R.append(op)
            if r.bank is not None and eng in ("act", "dve"):
                self.bank_last[(r.bank, eng)] = op
        for w in writes:
            w.W = [op]
            w.R = []
        self.ops.append(op)
        return op

    def op(self, eng, fn, reads=(), writes=(), sig=True, est=0.5, tab=None):
        reads = list(reads)
        writes = list(writes)
        if eng == "pe" and not sig:
            self._pend.append((fn, reads, writes, est))
            self.pe_pending = True
            return
        if eng == "pe":
            fns = [p[0] for p in self._pend] + [fn]
            for p in self._pend:
                reads += p[1]
                writes += p[2]
                est += p[3]
            self._pend = []
            self.pe_pending = False
        else:
            assert not self._pend, "non-PE op declared inside an open PE group"
            fns = [fn]
        self._new(eng, fns, est, "op", tab, reads, writes)

    def dma(self, eng, out, in_, reads=(), writes=(), **kw):
        assert not self._pend
        nbytes = 4
        for d in out.shape:
            nbytes *= d
        op = self._new(eng, [(out, in_, kw)], 2.0 + nbytes / 150e3, "dma", None, list(reads), list(writes))
        return op

    def final_wait(self, eng, regs_):
        self._new(eng, [], 0.01, "wait", None, list(regs_), [])

    def barrier(self):
        assert not self._pend
        self.epoch += 1
        self.bank_last = {}
        self.bank_w = {}
        self.bank_r = {}

    def _schedule(self):
        free = {e: 0.0 for e in ENGS}
        tabcur = [None]
        win = {"pe": 64, "act": 96, "dve": 96, "pool": 1, "sp": 1}
        order = []
        nep = self.epoch + 1
        byep = [[] for _ in range(nep)]
        for o in self.ops:
            byep[o.epoch].append(o)
        t_ep = 0.0
        for ep in range(nep):
            ops = byep[ep]
            lst = {e: [] for e in ENGS}
            for o in ops:
                lst[o.eng].append(o)
                o.nun = len(o.preds)
                for p in o.preds:
                    p.succ.append(o)
                o.ready = t_ep
            head = {e: 0 for e in ENGS}
            for e in ENGS:
                free[e] = max(free[e], t_ep)
            remaining = len(ops)
            while remaining:
                best = None
                for e in ENGS:
                    L = lst[e]
                    h = head[e]
                    while h < len(L) and L[h].done:
                        h += 1
                    head[e] = h
                    k = h
                    seen = 0
                    w = win[e]
                    while k < len(L) and seen < w:
                        o = L[k]
                        k += 1
                        if o.done:
                            continue
                        seen += 1
                        if o.nun:
                            continue
                        st = max(free[e], o.ready)
                        if e == "act" and o.tab is not None and o.tab != tabcur[0]:
                            st += TAB_COST
                        key = (st, o.i)
                        if best is None or key < best[0]:
                            best = (key, o)
                assert best is not None, "scheduler deadlock"
                (st, _), o = best
                e = o.eng
                if e == "act" and o.tab is not None:
                    tabcur[0] = o.tab
                o.start = st
                if o.kind == "dma":
                    iss = 1.5 if e == "pool" else 0.1
                    free[e] = st + iss
                    o.fin = st + iss + o.est
                else:
                    free[e] = st + o.est
                    o.fin = st + o.est
                o.done = True
                remaining -= 1
                order.append(o)
                for sopp in o.succ:
                    sopp.nun -= 1
                    lat = LAT_S if (sopp.eng == e and o.kind != "dma") else LAT_X
                    if o.fin + lat > sopp.ready:
                        sopp.ready = o.fin + lat
            t_ep = max([t_ep] + [o.fin for o in ops])
        self.est_total = t_ep
        return order

    def emit(self):
        assert not self._pend
        order = self._schedule()
        q = {e: [] for e in ENGS}
        pos = {e: 0 for e in ENGS}
        waited = {e: {} for e in ENGS}
        dma_tot = [0] * self.n_dma
        rr_sw = 0
        rr_hw = 0
        cur_ep = {e: 0 for e in ENGS}
        snap = {}
        last_ep = 0

        def wait(e, key, v):
            if waited[e].get(key, 0) >= v:
                return
            waited[e][key] = v
            h = self.sem[key]
            q[e].append(lambda eng_, h=h, v=v: eng_.wait_ge(h, v))

        for o in order:
            e = o.eng
            if o.epoch > last_ep:
                tot = {k: v for k, v in pos.items() if k != "sp" and v > 0}
                for k in range(self.n_dma):
                    if dma_tot[k] > 0:
                        tot[("dma", k)] = dma_tot[k]
                for ep in range(last_ep + 1, o.epoch + 1):
                    snap[ep] = tot
                last_ep = o.epoch
            if o.epoch > cur_ep[e]:
                for key, v in snap[o.epoch].items():
                    if key != e:
                        wait(e, key, v)
                cur_ep[e] = o.epoch
            for p, need in o.preds.items():
                if not need:
                    continue
                if p.kind == "dma":
                    wait(e, p.dkey, p.dval)
                else:
                    wait(e, p.eng, p.pos)
            if o.kind == "dma":
                half = self.n_dma // 2
                if e == "pool":
                    k = rr_sw
                    rr_sw = (rr_sw + 1) % half
                else:
                    k = half + rr_hw
                    rr_hw = (rr_hw + 1) % half
                key = ("dma", k)
                if dma_tot[k] > 0:
                    wait(e, key, dma_tot[k])
                dma_tot[k] += 16
                o.dkey = key
                o.dval = dma_tot[k]
                out, in_, kw = o.fns[0]
                h = self.sem[key]
                q[e].append(lambda eng_, out=out, in_=in_, h=h, kw=kw:
                            eng_.dma_start(out=out, in_=in_, **kw).then_inc(h, 16))
            elif o.kind == "op":
                pos[e] += 1
                o.pos = pos[e]
                h = self.sem[e]
                for f in o.fns[:-1]:
                    q[e].append(lambda eng_, f=f: f(eng_))
                f = o.fns[-1]
                q[e].append(lambda eng_, f=f, h=h: f(eng_).then_inc(h, 1))
        nc = self.nc
        with nc.Block() as block:
            @block.tensor
            def _(e):
                for f in q["pe"]:
                    f(e)

            @block.scalar
            def _(e):
                for f in q["act"]:
                    f(e)

            @block.vector
            def _(e):
                for f in q["dve"]:
                    f(e)

            @block.gpsimd
            def _(e):
                for f in q["pool"]:
                    f(e)

            @block.sync
            def _(e):
                for f in q["sp"]:
                    f(e)


def _fsz(ap):
    n = 1
    for d in ap.shape[1:]:
        n *= d
    return n


_TAB = {}


class Ops:
    def __init__(self, S):
        self.S = S
        self.flip = 0
        if not _TAB:
            _TAB.update({AF.Exp: "E", AF.Ln: "E", AF.Silu: "S", AF.Square: "S", AF.Sqrt: "Q"})

    def _e(self, eng, n, fixed=0.15, per=0.00105):
        if eng == "pool":
            return 0.3 + n * 0.0023
        return fixed + n * per

    def act(self, out, in_, func, reads, writes, bias=None, scale=None, accum=None):
        kw = {}
        if bias is not None:
            kw["bias"] = bias
        if scale is not None:
            kw["scale"] = scale
        if accum is not None:
            kw["accum_out"] = accum
        est = 0.22 + _fsz(out) * 0.00104 + (0.1 if accum is not None else 0.0)
        self.S.op("act", lambda e: e.activation(out=out, in_=in_, func=func, **kw), reads, writes,
                  est=est, tab=_TAB.get(func))

    def tt(self, out, in0, in1, op, reads, writes, eng="dve"):
        self.S.op(eng, lambda e: e.tensor_tensor(out=out, in0=in0, in1=in1, op=op), reads, writes,
                  est=self._e(eng, _fsz(out)))

    def ts(self, out, in0, s1, s2, op0, op1, reads, writes, eng="dve"):
        est = self._e(eng, _fsz(out))
        if s2 is None:
            self.S.op(eng, lambda e: e.tensor_scalar(out=out, in0=in0, scalar1=s1, scalar2=None, op0=op0),
                      reads, writes, est=est)
        else:
            self.S.op(eng, lambda e: e.tensor_scalar(out=out, in0=in0, scalar1=s1, scalar2=s2, op0=op0, op1=op1),
                      reads, writes, est=est)

    def stt(self, out, in0, scalar, in1, op0, op1, reads, writes):
        self.S.op("dve", lambda e: e.scalar_tensor_tensor(out=out, in0=in0, scalar=scalar, in1=in1,
                                                           op0=op0, op1=op1), reads, writes,
                  est=0.2 + _fsz(out) * 0.00105)

    def cp(self, out, in_, reads, writes, eng="dve"):
        if eng == "act":
            self.S.op("act", lambda e: e.activation(out=out, in_=in_, func=AF.Copy), reads, writes,
                      est=0.22 + _fsz(out) * 0.00104)
        else:
            self.S.op(eng, lambda e: e.tensor_copy(out=out, in_=in_), reads, writes, est=self._e(eng, _fsz(out)))

    def cpalt(self, out, in_, reads, writes):
        self.flip ^= 1
        self.cp(out, in_, reads, writes, eng="act" if self.flip else "dve")

    def recip(self, out, in_, reads, writes):
        self.S.op("dve", lambda e: e.reciprocal(out=out, in_=in_), reads, writes, est=0.2 + _fsz(out) * 0.0062)

    def memset(self, ap, val, writes, eng="dve"):
        self.S.op(eng, lambda e: e.memset(ap, val), [], writes, est=0.1 + _fsz(ap) * 0.0005)

    def mm(self, out, lhsT, rhs, start, stop, reads=(), writes=(), sig=False):
        self.S.op("pe", lambda e: e.matmul(out, lhsT=lhsT, rhs=rhs, start=start, stop=stop),
                  reads, writes, sig=sig, est=0.03 + _fsz(rhs) * 0.0006)

    def mmg(self, out, pairs, reads, writes):
        n = len(pairs)
        for i, (l, r) in enumerate(pairs):
            self.mm(out, l, r, start=(i == 0), stop=(i == n - 1),
                    reads=reads if i == 0 else (), writes=writes if i == 0 else (), sig=(i == n - 1))

    def tr(self, out, in_, ident, reads=(), writes=(), sig=False):
        self.S.op("pe", lambda e: e.transpose(out, in_, ident), reads, writes, sig=sig, est=0.12)

    def scan(self, out, d0, d1, init, op0, op1, reads, writes):
        self.S.op("dve", lambda e: e.tensor_tensor_scan(out=out, data0=d0, data1=d1, initial=init,
                                                         op0=op0, op1=op1), reads, writes,
                  est=0.2 + _fsz(out) * 0.0021)


class Arena:
    def __init__(self, nc):
        self.nc = nc
        base = (nc.sbuf_base + 63) // 64 * 64
        top = nc.sbuf_top
        KB = 1024
        self.reg = {
            "C": [base, base + 8 * KB],
            "G": [base + 8 * KB, base + 16 * KB],
            "H": [base + 16 * KB, base + 48 * KB],
            "M": [base + 48 * KB, base + 112 * KB],
            "ML": [base + 48 * KB, base + 80 * KB],
            "MH": [base + 80 * KB, base + 112 * KB],
            "R": [base + 112 * KB, top],
            "R0": [base + 112 * KB, base + 176 * KB],
            "R1": [base + 176 * KB, top],
        }
        self.cur = {k: v[0] for k, v in self.reg.items()}
        self.n = 0

    def reset(self, *names):
        for k in names:
            self.cur[k] = self.reg[k][0]

    def get(self, region, shape, dt, name=None):
        esz = 2 if dt == BF16 else 4
        nbytes = esz
        for s in shape[1:]:
            nbytes *= s
        nbytes = (nbytes + 63) // 64 * 64
        off = self.cur[region]
        assert off + nbytes <= self.reg[region][1], (region, name, shape, off + nbytes - self.reg[region][1])
        self.cur[region] = off + nbytes
        self.n += 1
        return self.nc.alloc_sbuf_tensor_at("%s_%d" % (name or region, self.n), list(shape), dt, offset=off)


def build_nc(upto="all", debug=False):
    nc = bass.Bass("TRN2", target_bir_lowering=False)

    def din(name, shape):
        return nc.dram_tensor(name, list(shape), F32, kind="ExternalInput").ap()

    x_d = din("x", [S_LEN, DM])
    mem_d = din("mem", [256, DM])
    g_mix = din("g_mix", [DM])
    w_in = din("w_in", [DM, 5664])
    conv_w = din("conv_w", [4, 1536])
    conv_b = din("conv_b", [1536])
    dt_bias = din("dt_bias", [16])
    a_log = din("a_log", [16])
    d_skip = din("d_skip", [16])
    ssm_norm_w = din("ssm_norm_w", [DM])
    g_q = din("g_q", [64])
    g_k = din("g_k", [64])
    f_bias = din("f_bias", [16])
    w_out = din("w_out", [2048, DM])
    g_xattn = din("g_xattn", [DM])
    g_mem = din("g_mem", [DM])
    xq_w = din("xq_w", [DM, DM])
    xkv_w = din("xkv_w", [DM, 2048])
    xg_q = din("xg_q", [256])
    xg_k = din("xg_k", [256])
    xo_w = din("xo_w", [DM, DM])
    g_mlp = din("g_mlp", [DM])
    w_up = din("w_up", [DM, 4096])
    w_down = din("w_down", [4096, DM])
    c_ident = din("c_ident", [128, 128])
    c_mask = din("c_mask", [128, 128])
    out_d = nc.dram_tensor("out", [S_LEN, DM], F32, kind="ExternalOutput").ap()

    S = Sched(nc)
    O = Ops(S)
    A = Arena(nc)
    dbg = []

    def col(ap1d, n):
        return ap1d.rearrange("(p o) -> p o", o=1)

    PP = [nc.alloc_psum_tensor("pp%d" % i, [128, 1024], F32) for i in range(4)]
    RB = [Reg(bank=i) for i in range(8)]

    def bank(b):
        return PP[b // 2][:, (b % 2) * 512:(b % 2) * 512 + 512]

    def bank_bf(b):
        return PP[b // 2][:, (b % 2) * 512:(b % 2) * 512 + 512].bitcast(BF16)

    ident_f = A.get("C", [128, 128], F32, "identf")
    maskf = A.get("C", [128, 128], F32, "maskf")
    ident_b = A.get("C", [128, 128], BF16, "identb")
    mask_b = A.get("C", [128, 128], BF16, "maskb")
    ones_b = A.get("C", [128, 128], BF16, "onesb")
    bones_b = A.get("C", [128, 128], BF16, "bonesb")
    sel127 = A.get("C", [128, 128], F32, "sel127")
    sel0 = A.get("C", [128, 128], F32, "sel0")
    cols = A.get("C", [128, 64], F32, "cols")
    ones_f = A.get("C", [128, 128], F32, "onesf")
    R_const = Reg()
    R_cols = Reg()
    RC = []

    def cdma(dst, src, **kw):
        r = Reg()
        RC.append(r)
        S.dma("pool", dst, src, reads=[R_cols], writes=[r], **kw)

    S.dma("pool", ident_f[:], c_ident, writes=[R_const])
    R_maskf = Reg()
    S.dma("pool", maskf[:], c_mask, writes=[R_maskf])
    O.cp(ident_b[:], ident_f[:], [R_const], [R_const])
    O.memset(ones_b[:], 1.0, [R_const])
    O.memset(ones_f[:], 1.0, [R_const])
    O.memset(bones_b[:], 0.0, [R_const])
    O.memset(bones_b[0:64, 0:64], 1.0, [R_const])
    O.memset(bones_b[64:128, 64:128], 1.0, [R_const])
    O.cp(sel127[:], ident_f[:, 127:128].to_broadcast([128, 128]), [R_const], [R_const])
    O.cp(sel0[:], ident_f[:, 0:1].to_broadcast([128, 128]), [R_const], [R_const])
    O.cp(mask_b[:], maskf[:], [R_const, R_maskf], [R_const])
    O.memset(cols[:], 0.0, [R_cols])
    cw = A.get("C", [128, 12, 4], F32, "cw")
    cb = A.get("C", [128, 12], F32, "cb")
    cdma(cols[0:64, 0:1], col(g_q, 64))
    cdma(cols[64:128, 0:1], col(g_q, 64))
    cdma(cols[0:64, 1:2], col(g_k, 64))
    cdma(cols[64:128, 1:2], col(g_k, 64))
    cdma(cols[0:16, 3:4], col(dt_bias, 16))
    cdma(cols[32:48, 5:6], col(f_bias, 16))
    cdma(cols[0:16, 4:5], col(a_log, 16))
    cdma(cols[:, 6:8], xg_q.rearrange("(c p) -> p c", p=128), allow_slow_non_contiguous=True)
    cdma(cols[:, 8:10], xg_k.rearrange("(c p) -> p c", p=128), allow_slow_non_contiguous=True)
    for k_ in range(4):
        cdma(cw[:, :, k_], conv_w[k_].rearrange("(j p) -> p j", p=128), allow_slow_non_contiguous=True)
    cdma(cb[:], conv_b.rearrange("(j p) -> p j", p=128), allow_slow_non_contiguous=True)
    O.act(cols[:, 0:1], cols[:, 0:1], AF.Copy, [R_cols] + RC, [R_cols], scale=0.125)
    O.act(cols[:, 6:8], cols[:, 6:8], AF.Copy, [R_cols], [R_cols], scale=1.0 / 16.0)
    O.memset(cols[0:64, 2:3], 1.0, [R_cols])
    O.memset(cols[32:64, 2:3], -1.0, [R_cols])
    O.act(cols[32:64, 3:4], cols[32:64, 5:6], AF.Copy, [R_cols], [R_cols], scale=-1.0)
    O.act(cols[0:16, 4:5], cols[0:16, 4:5], AF.Exp, [R_cols], [R_cols])
    O.act(cols[0:16, 4:5], cols[0:16, 4:5], AF.Copy, [R_cols], [R_cols], scale=-1.0)

    gA = A.get("G", [128, DM], F32, "gA")
    gB = A.get("G", [128, DM], F32, "gB")
    R_gA = Reg()
    R_gB = Reg()

    hT = A.get("H", [128, 8, S_LEN], BF16, "hT")
    R_hT = regs(NT)

    def norm_a(x_ap, x_regs, ws, i):
        junk, xn, stat, R_junk, R_xn, R_stat = ws
        ss = stat[:, 3 * i:3 * i + 1]
        sd = stat[:, 3 * i + 1:3 * i + 2]
        rs = stat[:, 3 * i + 2:3 * i + 3]
        O.act(junk[:], x_ap, AF.Square, x_regs, [R_junk, R_stat[i]], accum=ss)
        O.act(sd, ss, AF.Ln, [R_stat[i], R_eps], [R_stat[i]], scale=1.0 / DM, bias=eps_col[:])
        O.act(rs, sd, AF.Exp, [R_stat[i]], [R_stat[i]], scale=-0.5)

    def norm_b(x_ap, x_regs, g_bc, g_reg, dstT, dst_col0, dst_regs, ws, i, pbank):
        junk, xn, stat, R_junk, R_xn, R_stat = ws
        k = i % 2
        rs = stat[:, 3 * i + 2:3 * i + 3]
        O.stt(xn[k][:], x_ap, rs, g_bc[:], ALU.mult, ALU.mult, x_regs + [R_stat[i], g_reg], [R_xn[k]])
        pb = bank_bf(pbank)
        for kc in range(8):
            O.tr(pb[:, kc * 128:(kc + 1) * 128], xn[k][:, kc * 128:(kc + 1) * 128], ident_b[:],
                 reads=[R_xn[k], R_const] if kc == 0 else (), writes=[RB[pbank]] if kc == 0 else (), sig=(kc == 7))
        O.cpalt(dstT[:, :, dst_col0:dst_col0 + 128], pb.rearrange("p (k t) -> p k t", k=8), [RB[pbank]], dst_regs)

    def norm_tile(x_ap, x_regs, g_bc, g_reg, dstT, dst_col0, dst_regs, ws, i, pbank):
        norm_a(x_ap, x_regs, ws, i)
        norm_b(x_ap, x_regs, g_bc, g_reg, dstT, dst_col0, dst_regs, ws, i, pbank)

    def norm_all(src, g_bc, g_reg, dstT, dst_regs, ws):
        norm_a(src(0)[0], src(0)[1], ws, 0)
        for t in range(NT):
            if t + 1 < NT:
                norm_a(src(t + 1)[0], src(t + 1)[1], ws, t + 1)
            norm_b(src(t)[0], src(t)[1], g_bc, g_reg, dstT, t * 128, [dst_regs[t]], ws, t, t % 2)

    eps_col = A.get("C", [128, 1], F32, "epscol")
    maskf4 = A.get("C", [128, 4, 128], F32, "maskf4")
    ident_r = A.get("C", [128, 128], F32, "identr")
    O.cp(maskf4[:].bitcast(mybir.dt.float32r), maskf[:].unsqueeze(1).to_broadcast([128, 4, 128]), [R_const], [R_const])
    O.cp(ident_r[:].bitcast(mybir.dt.float32r), ident_f[:], [R_const], [R_const])
    mask01 = A.get("C", [128, 128], F32, "mask01")
    O.ts(mask01[:], maskf[:], 0.0, None, ALU.is_equal, None, [R_const], [R_const])
    F32R = mybir.dt.float32r
    maskf4_l = ident_r[:].bitcast(F32R)
    maskf4_r = maskf4[:].rearrange("p a b -> p (a b)").bitcast(F32R)
    R_eps = Reg()
    O.memset(eps_col[:], EPS, [R_eps])

    def bcast_load(dst, g1d, reg):
        S.dma("sp", dst[:], g1d.partition_broadcast(128), writes=[reg])

    def dump(name, ap, rg):
        dbg.append((name, ap, rg))

    A.reset("R", "MH")
    bcast_load(gA, g_mix, R_gA)
    xt = [A.get("MH", [128, DM], F32, "xt") for _ in range(3)]
    R_xt = regs(3)
    junk = A.get("MH", [128, DM], BF16, "junk")
    xn = [A.get("MH", [128, DM], BF16, "xn") for _ in range(2)]
    stat = A.get("MH", [128, 3 * NT], F32, "stat")
    wsA = (junk, xn, stat, Reg(), regs(2), regs(NT))
    def ld_x(t):
        S.dma("sp", xt[t % 3][:], x_d[t * 128:(t + 1) * 128, :], writes=[R_xt[t % 3]])

    ld_x(0)
    ld_x(1)
    norm_a(xt[0][:], [R_xt[0]], wsA, 0)
    for t in range(NT):
        if t + 2 < NT:
            ld_x(t + 2)
        if t + 1 < NT:
            norm_a(xt[(t + 1) % 3][:], [R_xt[(t + 1) % 3]], wsA, t + 1)
        norm_b(xt[t % 3][:], [R_xt[t % 3]], gA, R_gA, hT, t * 128, [R_hT[t]], wsA, t, t % 2)
    if upto == "A":
        dump("d_hT", hT[:], R_hT)
        return finish(nc, S, A, dbg, out_d, None)

    A.reset("R", "R0", "R1")

    def wcols(c0, n):
        return w_in[:, c0:c0 + n].rearrange("(kc p) c -> p kc c", p=128)

    r0_end = A.reg["R0"][1]
    w_xs = nc.alloc_sbuf_tensor_at("w_xs", [128, 8, 512], BF16, offset=r0_end - 20 * 1024)
    w_z = nc.alloc_sbuf_tensor_at("w_z", [128, 8, 512], BF16, offset=r0_end - 12 * 1024)
    w_bc = nc.alloc_sbuf_tensor_at("w_bc", [128, 8, 256], BF16, offset=r0_end - 4 * 1024)
    R_wg = Reg()

    def load_ssd_weights(g):
        S.dma("pool", w_xs[:], wcols(1024 + g * 512, 512), writes=[R_wg])
        S.dma("pool", w_z[:], wcols(g * 512, 512), writes=[R_wg])
        S.dma("pool", w_bc[:, :, 0:128], wcols(2048 + g * 128, 128), writes=[R_wg])
        S.dma("pool", w_bc[:, :, 128:256], wcols(2304 + g * 128, 128), writes=[R_wg])

    load_ssd_weights(0)

    ccT = A.get("R1", [64, S_LEN], F32, "ccT")
    selh = A.get("R1", [16, 16, 128], F32, "selh")
    dt_tm = A.get("R1", [128, NT, 16], F32, "dt_tm")
    Acs_tm = A.get("R1", [128, NT, 16], F32, "Acs_tm")
    nAcs_tm = A.get("R1", [128, NT, 16], F32, "nAcs_tm")
    cum_tm = A.get("R1", [128, NT, 16], F32, "cum_tm")
    eA_tm = A.get("R1", [128, NT, 16], F32, "eA_tm")
    f2_tm = A.get("R1", [128, NT, 16], F32, "f2_tm")
    cd_bc = A.get("R1", [128, NT, 16], F32, "cd_bc")
    cfirst = A.get("R1", [128, NT, 16], F32, "cfirst")
    biasT = A.get("R1", [128, 4, NT, 16], F32, "biasT")
    dsk_bc = A.get("R1", [128, 16], F32, "dsk_bc")
    R_tab = Reg()
    R_cc = Reg()

    wdtf = A.get("R0", [128, 8, 64], BF16, "wdtf")
    eT = A.get("R0", [64, S_LEN], F32, "eT")
    spT = A.get("R0", [64, S_LEN], F32, "spT")
    dAT = A.get("R0", [64, S_LEN], F32, "dAT")
    lfT = A.get("R0", [64, S_LEN], F32, "lfT")
    onesT = A.get("R0", [64, S_LEN], F32, "onesT")
    tmp_tm = A.get("R0", [128, NT, 16], F32, "tmp_tm")
    R_wdtf = Reg()
    R_e = regs(4)
    R_sp = regs(4)
    R_dAT = Reg()
    R_lf = Reg()
    R_onesT = Reg()
    R_tmp = Reg()

    O.memset(wdtf[:], 0.0, [R_wdtf])
    S.dma("pool", wdtf[:, :, 0:16], wcols(2560, 16), writes=[R_wdtf])
    S.dma("pool", wdtf[:, :, 32:48], wcols(5648, 16), writes=[R_wdtf])
    S.dma("sp", dsk_bc[:], d_skip.partition_broadcast(128), writes=[R_tab])
    O.memset(onesT[:], 1.0, [R_onesT])
    O.memset(ccT[:], 0.0, [R_cc])
    O.cp(selh[:], ident_f[0:16, 0:16].unsqueeze(2).to_broadcast([16, 16, 128]), [R_const], [R_tab])
    for tb in range(4):
        b = tb % 2
        sl = slice(tb * 512, (tb + 1) * 512)
        O.mmg(bank(b)[0:64, :], [(wdtf[:, kc, :], hT[:, kc, sl]) for kc in range(8)],
              reads=[R_wdtf] + R_hT[tb * 4:(tb + 1) * 4], writes=[RB[b]])
        O.act(eT[:, sl], bank(b)[0:64, :], AF.Exp, [RB[b], R_cols], [R_e[tb]],
              bias=cols[0:64, 3:4], scale=cols[0:64, 2:3])
        O.act(spT[:, sl], eT[:, sl], AF.Ln, [R_e[tb]], [R_sp[tb]], bias=1.0)
    O.ts(dAT[0:16, :], spT[0:16, :], cols[0:16, 4:5], None, ALU.mult, None, R_sp + [R_cols], [R_dAT])
    O.ts(lfT[32:48, :], spT[32:48, :], -1.0, None, ALU.mult, None, R_sp, [R_lf])
    for c in range(NT):
        cs = slice(c * 128, (c + 1) * 128)
        O.scan(ccT[0:16, cs], onesT[0:16, cs], dAT[0:16, cs], 0.0, ALU.mult, ALU.add,
               [R_dAT, R_onesT], [R_cc])
    O.scan(ccT[32:48, :], onesT[32:48, :], lfT[32:48, :], 0.0, ALU.mult, ALU.add, [R_lf, R_onesT], [R_cc])
    for tq in range(4):
        b = 2 + tq % 2
        for i in range(4):
            t = tq * 4 + i
            ts_ = slice(t * 128, (t + 1) * 128)
            O.tr(bank(b)[:, i * 128:i * 128 + 64], spT[0:64, ts_], ident_f[0:64, 0:64],
                 reads=R_sp + [R_const] if i == 0 else (), writes=[RB[b]] if i == 0 else ())
            O.tr(bank(b)[:, i * 128 + 64:i * 128 + 128], ccT[0:64, ts_], ident_f[0:64, 0:64],
                 reads=[R_cc] if i == 0 else (), sig=(i == 3))
        v = bank(b).rearrange("p (t w) -> p t w", w=128)
        tsl = slice(tq * 4, tq * 4 + 4)
        O.cp(dt_tm[:, tsl, :], v[:, :, 0:16], [RB[b]], [R_tab])
        O.cp(Acs_tm[:, tsl, :], v[:, :, 64:80], [RB[b]], [R_tab], eng="act")
        O.cp(cum_tm[:, tsl, :], v[:, :, 96:112], [RB[b]], [R_tab])

    def flat(t):
        return t[:].rearrange("p t h -> p (t h)")

    O.ts(flat(nAcs_tm), flat(Acs_tm), -1.0, None, ALU.mult, None, [R_tab], [R_tab])
    O.act(flat(eA_tm), flat(Acs_tm), AF.Exp, [R_tab], [R_tab])
    O.mmg(bank(0)[:, 0:256], [(sel127[:], flat(Acs_tm))], reads=[R_tab, R_const], writes=[RB[0]])
    O.act(flat(cd_bc), bank(0)[:, 0:256], AF.Exp, [RB[0]], [R_tab])
    O.tt(flat(tmp_tm), bank(0)[:, 0:256], flat(Acs_tm), ALU.subtract, [RB[0], R_tab], [R_tmp])
    O.act(flat(tmp_tm), flat(tmp_tm), AF.Exp, [R_tmp], [R_tmp])
    O.tt(flat(f2_tm), flat(tmp_tm), flat(dt_tm), ALU.mult, [R_tmp, R_tab], [R_tab])
    O.mmg(bank(1)[:, 0:256], [(sel0[:], flat(cum_tm))], reads=[R_tab, R_const], writes=[RB[1]])
    O.cp(flat(cfirst), bank(1)[:, 0:256], [RB[1]], [R_tab])
    for qb in range(4):
        O.tt(biasT[:, qb, :, :], cfirst[:, 4 * qb + 2:4 * qb + 3, :].to_broadcast([128, NT, 16]), cum_tm[:],
             ALU.subtract, [R_tab], [R_tab])
    if upto == "B1":
        dump("d_dt", dt_tm[:], [R_tab])
        dump("d_Acs", Acs_tm[:], [R_tab])
        dump("d_cum", cum_tm[:], [R_tab])
        dump("d_cd", cd_bc[:], [R_tab])
        dump("d_f2", f2_tm[:], [R_tab])
        dump("d_bias", biasT[:], [R_tab])
        return finish(nc, S, A, dbg, out_d, None)

    S.barrier()
    A.reset("R0", "MH")
    mixedT = A.get("M", [128, 16, S_LEN], BF16, "mixedT")
    R_mixS = [regs(NT), regs(NT)]
    R_mixA = [[regs(4), regs(4)] for _ in range(8)]
    bcast_load(gB, ssm_norm_w, R_gB)

    uT = A.get("R0", [128, 6, 515], F32, "uT")
    xsT = [A.get("R0", [128, 4, 512], F32, "xsT") for _ in range(2)]
    bcT = [A.get("R0", [128, 2, 512], BF16, "bcT") for _ in range(2)]
    S_st = A.get("R0", [128, 512], F32, "S_st")
    S_bf = A.get("R0", [128, 512], BF16, "S_bf")
    xs_sb = [A.get("R0", [128, 512], F32, "xs_sb") for _ in range(2)]
    Xb = [A.get("R0", [128, 512], BF16, "Xb") for _ in range(2)]
    Xd = [A.get("R0", [128, 512], BF16, "Xd") for _ in range(2)]
    Btm = [A.get("R0", [128, 128], BF16, "Btm") for _ in range(2)]
    LT = [A.get("MH", [128, 8, 128], F32, "LT") for _ in range(2)]
    MT = [A.get("MH", [128, 8, 128], BF16, "MT") for _ in range(2)]
    sz = [A.get("MH", [128, 512], F32, "sz") for _ in range(2)]
    t1 = A.get("MH", [128, 512], F32, "t1")
    t3 = [A.get("MH", [128, 512], F32, "t3") for _ in range(2)]
    cbm = [A.get("MH", [128, 128], F32, "cbm") for _ in range(2)]
    yg = A.get("MH", [128, 512], F32, "yg")
    cacc = [A.get("MH", [128, 512], F32, "cacc") for _ in range(2)]
    junk2 = A.get("MH", [128, 512], BF16, "junk2")
    ymix = A.get("MH", [128, 512], BF16, "ymix")
    stat2 = A.get("MH", [128, 3 * 32], F32, "stat2")
    assert A.cur["R0"] <= r0_end - 20 * 1024, "SSD working set collides with its weight tiles"
    R_uT = regs(6)
    R_xsT = regs(2)
    R_bcT = regs(2)
    R_S = Reg(); R_Sbf = Reg()
    R_xs = regs(2); R_X = regs(2); R_Xd = regs(2); R_Btm = regs(2); R_LT = regs(2); R_MT = regs(2); R_sz = regs(2)
    R_t1 = Reg(); R_t2 = Reg(); R_t3 = regs(2); R_yg = Reg(); R_cbm = regs(2)
    R_cacc = regs(2); R_junk2 = Reg(); R_ymix = Reg(); R_st2 = regs(32)
    RB2 = [Reg(bank=2) for _ in range(3)]

    def h8(ap):
        return ap.rearrange("p (h d) -> p h d", h=8)

    for g in range(2):
        if g > 0:
            load_ssd_weights(g)
        O.memset(uT[:, :, 0:3], 0.0, R_uT)
        O.memset(S_st[:], 0.0, [R_S])
        O.memset(S_bf[:], 0.0, [R_Sbf])
        jmap = [4 * g + 0, 4 * g + 1, 4 * g + 2, 4 * g + 3, 8 + g, 10 + g]
        hs = slice(g * 8, g * 8 + 8)

        def u_pe(tb, j):
            sl = slice(tb * 512, (tb + 1) * 512)
            wsrc = w_xs[:, :, j * 128:(j + 1) * 128] if j < 4 else w_bc[:, :, (j - 4) * 128:(j - 3) * 128]
            O.mmg(bank(7), [(wsrc[:, kc, :], hT[:, kc, sl]) for kc in range(8)],
                  reads=[R_wg] + R_hT[tb * 4:(tb + 1) * 4], writes=[RB[7]])

        def u_act1(tb, j):
            jj = jmap[j]
            q = j % 2
            O.cp(uT[:, j, 3:515], bank(7), [RB[7]], [R_uT[j]], eng="act")
            O.act(cacc[q][:], uT[:, j, 3:515], AF.Identity, [R_uT[j], R_cols], [R_cacc[q]],
                  bias=cb[:, jj:jj + 1], scale=cw[:, jj, 3:4])

        def u_dve(tb, j):
            jj = jmap[j]
            q = j % 2
            for k_ in (2, 1, 0):
                O.stt(cacc[q][:], uT[:, j, k_:k_ + 512], cw[:, jj, k_:k_ + 1], cacc[q][:], ALU.mult, ALU.add,
                      [R_uT[j], R_cacc[q], R_cols], [R_cacc[q]])
            O.cp(uT[:, j, 0:3], uT[:, j, 512:515], [R_uT[j]], [R_uT[j]])

        def u_act2(tb, j):
            kb = tb % 2
            q = j % 2
            if j < 4:
                O.act(xsT[kb][:, j, :], cacc[q][:], AF.Silu, [R_cacc[q]], [R_xsT[kb]])
            else:
                O.act(bcT[kb][:, j - 4, :], cacc[q][:], AF.Silu, [R_cacc[q]], [R_bcT[kb]])

        def proj_unit(tb, j):
            u_pe(tb, j); u_act1(tb, j); u_dve(tb, j); u_act2(tb, j)

        def bc8(tab, c):
            return tab[:, c, hs].unsqueeze(2).to_broadcast([128, 8, 64])

        def idx(c):
            tb, ci = c // 4, c % 4
            return tb % 2, c % 2, slice(ci * 128, (ci + 1) * 128), slice(c * 128, (c + 1) * 128)

        def s1_pe(c):
            kb, p, cs, gs = idx(c)
            O.mmg(bank(1), [(hT[:, kc, gs], w_z[:, kc, :]) for kc in range(8)],
                  reads=[R_wg, R_hT[c]], writes=[RB[1]])
            for j in range(4):
                O.tr(bank(0)[:, j * 128:(j + 1) * 128], xsT[kb][:, j, cs], ident_f[:],
                     reads=[R_xsT[kb], R_const] if j == 0 else (), writes=[RB[0]] if j == 0 else (),
                     sig=(j == 3))
            O.tr(bank_bf(2)[:, 256:384], bcT[kb][:, 0, cs], ident_b[:], reads=[R_bcT[kb], R_const],
                 writes=[RB2[1]], sig=True)
            O.mmg(bank(2)[:, 0:128], [(bcT[kb][:, 0, cs], bcT[kb][:, 1, cs])], reads=[R_bcT[kb]],
                  writes=[RB2[0]])
            for half in range(2):
                bk = 4 + half
                O.mm(bank(bk), maskf4_l, maskf4_r, True, False, reads=[R_const], writes=[RB[bk]])
                for i in range(4):
                    hh = g * 8 + half * 4 + i
                    O.mm(bank(bk)[:, i * 128:(i + 1) * 128], selh[0:16, hh, :], ccT[0:16, gs], False, i == 3,
                         reads=[R_tab, R_cc] if i == 0 else (), writes=[RB[bk]], sig=(i == 3))

        def s1_act(c):
            kb, p, cs, gs = idx(c)
            O.act(sz[p][:], bank(1), AF.Silu, [RB[1]], [R_sz[p]])
            O.cp(xs_sb[p][:], bank(0), [RB[0]], [R_xs[p]], eng="act")
            O.cp(Btm[p][:], bank_bf(2)[:, 256:384], [RB2[1]], [R_Btm[p]], eng="act")
            for half in range(2):
                bk = 4 + half
                for i in range(4):
                    hh = g * 8 + half * 4 + i
                    O.act(LT[p][:, half * 4 + i, :], bank(bk)[:, i * 128:(i + 1) * 128], AF.Exp,
                          [RB[bk], R_tab], [R_LT[p]], bias=nAcs_tm[:, c, hh:hh + 1])

        def s1_dve_a(c):
            kb, p, cs, gs = idx(c)
            xs3 = h8(xs_sb[p][:])
            O.tt(h8(Xb[p][:]), xs3, bc8(dt_tm, c), ALU.mult, [R_xs[p], R_tab], [R_X[p]])
            O.tt(h8(Xd[p][:]), xs3, bc8(f2_tm, c), ALU.mult, [R_xs[p], R_tab], [R_Xd[p]])
            O.tt(h8(t3[p][:]), xs3, dsk_bc[:, hs].unsqueeze(2).to_broadcast([128, 8, 64]), ALU.mult,
                 [R_xs[p], R_tab], [R_t3[p]], eng="pool")

        def s1_dve_b(c):
            kb, p, cs, gs = idx(c)
            O.tt(MT[p][:], LT[p][:], bank(2)[:, 0:128].unsqueeze(1).to_broadcast([128, 8, 128]), ALU.mult,
                 [R_LT[p], RB2[0]], [R_MT[p]])

        def s2_pe_a(c):
            kb, p, cs, gs = idx(c)
            for i in range(8):
                O.mm(bank(3)[:, i * 64:(i + 1) * 64], MT[p][:, i, :], Xb[p][:, i * 64:(i + 1) * 64], True, True,
                     reads=[R_MT[p], R_X[p]] if i == 0 else (), writes=[RB[3]] if i == 0 else (), sig=(i == 7))
            O.mmg(bank(7), [(Btm[p][:], Xd[p][:])], reads=[R_Btm[p], R_Xd[p]], writes=[RB[7]])
            O.mmg(bank(6), [(bcT[kb][:, 1, cs], S_bf[:])], reads=[R_bcT[kb], R_Sbf], writes=[RB[6]])

        def s2_state(c):
            S3 = h8(S_st[:])
            O.tt(S3, S3, bc8(cd_bc, c), ALU.mult, [R_S, R_tab], [R_S])
            O.tt(S_st[:], S_st[:], bank(7), ALU.add, [R_S, RB[7]], [R_S])
            O.cp(S_bf[:], S_st[:], [R_S], [R_Sbf], eng="act")

        def s2_dve_a(c):
            kb, p, cs, gs = idx(c)
            O.tt(h8(t1[:]), h8(bank(6)), bc8(eA_tm, c), ALU.mult, [RB[6], R_tab], [R_t1])
            O.tt(t1[:], bank(3), t1[:], ALU.add, [RB[3], R_t1], [R_t1])
            O.tt(t1[:], t1[:], t3[p][:], ALU.add, [R_t1, R_t3[p]], [R_t1])
            O.tt(yg[:], t1[:], sz[p][:], ALU.mult, [R_t1, R_sz[p]], [R_yg])
            si = g * 16 + c
            ss = stat2[:, 3 * si:3 * si + 1]
            S.op("dve", lambda e, ss=ss: e.scalar_tensor_tensor(out=junk2[:], in0=yg[:], scalar=1.0, in1=yg[:],
                                                                op0=ALU.mult, op1=ALU.mult, accum_out=ss),
                 [R_yg], [R_junk2, R_st2[si]])

        def s2_act_a(c):
            si = g * 16 + c
            ss = stat2[:, 3 * si:3 * si + 1]
            sd = stat2[:, 3 * si + 1:3 * si + 2]
            rs = stat2[:, 3 * si + 2:3 * si + 3]
            O.act(sd, ss, AF.Ln, [R_st2[si], R_eps], [R_st2[si]], scale=1.0 / 512.0, bias=eps_col[:])
            O.act(rs, sd, AF.Exp, [R_st2[si]], [R_st2[si]], scale=-0.5)

        def s2_tail(c):
            kb, p, cs, gs = idx(c)
            si = g * 16 + c
            rs = stat2[:, 3 * si + 2:3 * si + 3]
            O.stt(ymix[:], yg[:], rs, gB[:, g * 512:(g + 1) * 512], ALU.mult, ALU.mult,
                  [R_yg, R_st2[si], R_gB], [R_ymix])
            for j in range(4):
                O.tr(bank_bf(2)[:, 512 + j * 128:512 + (j + 1) * 128], ymix[:, j * 128:(j + 1) * 128],
                     ident_b[:], reads=[R_ymix, R_const] if j == 0 else (),
                     writes=[RB2[2]] if j == 0 else (), sig=(j == 3))
            O.cp(mixedT[:, g * 4:(g + 1) * 4, gs],
                 bank_bf(2)[:, 512:1024].rearrange("p (k t) -> p k t", k=4), [RB2[2]], [R_mixS[g][c]], eng="act")

        for j in range(6):
            proj_unit(0, j)
        s1_pe(0); s1_act(0); s1_dve_a(0); s1_dve_b(0)
        for c in range(NT):
            tb, ci = c // 4, c % 4
            units = []
            if tb + 1 < 4 and ci < 3:
                units = [(tb + 1, 2 * ci), (tb + 1, 2 * ci + 1)]
            n = c + 1 if c + 1 < NT else None
            s2_pe_a(c)
            for u in units:
                u_pe(*u) if False else None
            if n is not None:
                s1_pe(n)
            s2_state(c)
            s2_dve_a(c)
            ulist = list(units)
            if ulist:
                u_pe(*ulist[0]); u_act1(*ulist[0])
            if n is not None:
                s1_act(n)
            s2_act_a(c)
            if ulist:
                u_dve(*ulist[0])
                u_pe(*ulist[1]); u_act1(*ulist[1])
            s2_tail(c)
            if n is not None:
                s1_dve_a(n)
            if ulist:
                u_act2(*ulist[0])
                u_dve(*ulist[1])
            if n is not None:
                s1_dve_b(n)
            if ulist:
                u_act2(*ulist[1])
    if upto == "SSD":
        dump("d_mixS", mixedT[:, 0:8, :], R_mixS[0] + R_mixS[1])
        dump("d_dt", dt_tm[:], [R_tab])
        dump("d_Acs", Acs_tm[:], [R_tab])
        dump("d_cum", cum_tm[:], [R_tab])
        dump("d_cd", cd_bc[:], [R_tab])
        dump("d_f2", f2_tm[:], [R_tab])
        dump("d_bias", biasT[:], [R_tab])
        return finish(nc, S, A, dbg, out_d, None)

    S.barrier()
    A.reset("R0")
    Vh = [A.get("R0", [128, 8, 8, 3, 64], BF16, "Vh%d" % i) for i in range(2)]
    R_V = regs(NT)
    mark = A.cur["R0"]
    A.reset("MH")
    wv = [A.get("MH", [128, 8, 512], BF16, "wv") for _ in range(2)]
    R_wv = regs(2)
    for i in range(2):
        O.memset(Vh[i][:, :, :, 1, :], 1.0, R_V[i * 8:(i + 1) * 8])
    for cbk in range(2):
        S.dma("pool", wv[cbk][:], wcols(4624 + cbk * 512, 512), writes=[R_wv[cbk]])
    for t in range(NT):
        for cbk in range(2):
            b = (2 * t + cbk) % 4
            O.mmg(bank(b), [(hT[:, kc, t * 128:(t + 1) * 128], wv[cbk][:, kc, :]) for kc in range(8)],
                  reads=[R_hT[t], R_wv[cbk]], writes=[RB[b]])
            O.cpalt(Vh[t // 8][:, t % 8, 4 * cbk:4 * cbk + 4, 0:3:2, :],
                    bank(b).rearrange("p (q s d) -> p q s d", q=4, s=2), [RB[b]], [R_V[t]])
    if upto == "V":
        return finish(nc, S, A, dbg, out_d, None)
    A.cur["R0"] = mark
    A.reset("R1")
    A.reset("G")
    wqk = [A.get("G", [128, 8, 256], BF16, "wqk") for _ in range(2)]
    R_wqk = regs(2)
    qkT = [A.get("R0", [128, 3, S_LEN], BF16, "qkT0"), A.get("R1", [128, 3, S_LEN], BF16, "qkT1")]
    R_qk = [[regs(4), regs(4)] for _ in range(2)]
    sq = A.get("R1", [128, 512], BF16, "sq")
    raw = A.get("R1", [128, 512], F32, "raw")
    sdv = A.get("R1", [128, 512], F32, "sdv")
    rsv = A.get("R1", [128, 512], F32, "rsv")
    rden = [A.get("R1", [128, 512], F32, "rden") for _ in range(2)]
    rscr = A.get("R1", [128, 512], F32, "rscr")
    R_rscr = Reg()
    PT = [A.get("R0", [128, 512], BF16, "PT") for _ in range(4)]
    R_PT = regs(4)
    R_sq = Reg(); R_raw = Reg(); R_sd = Reg(); R_rs = Reg(); R_rden = regs(2)
    LOOK = 2
    for wb_ in range(2):
        O.memset(qkT[wb_][64:128, 1, :], 0.0, R_qk[wb_][1])
        O.memset(qkT[wb_][0:64, 2, :], 0.0, R_qk[wb_][1])

    def load_wqk(pp):
        wb = pp % 2
        S.dma("pool", wqk[wb][:, :, 0:128], wcols(2576 + pp * 128, 128), writes=[R_wqk[wb]])
        S.dma("pool", wqk[wb][:, :, 128:256], wcols(3600 + pp * 128, 128), writes=[R_wqk[wb]])

    def qk_block_a(pp, which, tb):
        wb = pp % 2
        sl = slice(tb * 512, (tb + 1) * 512)
        O.mmg(bank(7), [(wqk[wb][:, kc, which * 128:(which + 1) * 128], hT[:, kc, sl]) for kc in range(8)],
              reads=[R_wqk[wb]] + R_hT[tb * 4:(tb + 1) * 4], writes=[RB[7]])
        O.cp(raw[:], bank(7), [RB[7]], [R_raw])
        O.tt(sq[:], raw[:], raw[:], ALU.mult, [R_raw], [R_sq])

    def qk_block_b(pp, which, tb):
        wb = pp % 2
        sl = slice(tb * 512, (tb + 1) * 512)
        O.mmg(bank(7), [(bones_b[:], sq[:])], reads=[R_sq, R_const], writes=[RB[7]])
        O.act(sdv[:], bank(7), AF.Ln, [RB[7], R_eps], [R_sd], scale=1.0 / 64.0, bias=eps_col[:])
        O.act(rsv[:], sdv[:], AF.Exp, [R_sd], [R_rs], scale=-0.5)
        if which == 0:
            O.stt(qkT[wb][:, 0, sl], raw[:], cols[:, 0:1], rsv[:], ALU.mult, ALU.mult,
                  [R_raw, R_rs, R_cols], [R_qk[wb][0][tb]])
        else:
            O.stt(qkT[wb][0:64, 1, sl], raw[0:64, :], cols[0:64, 1:2], rsv[0:64, :], ALU.mult, ALU.mult,
                  [R_raw, R_rs, R_cols], [R_qk[wb][1][tb]])
            O.stt(qkT[wb][64:128, 2, sl], raw[64:128, :], cols[64:128, 1:2], rsv[64:128, :], ALU.mult, ALU.mult,
                  [R_raw, R_rs, R_cols], [R_qk[wb][1][tb]])

    state = {"s": 0, "xy": 0}

    def emit_S(pp, st):
        hh, qb, kt, bs, pi, xy = st
        wb = pp % 2
        head = 2 * pp + hh
        ps_ = slice(hh * 64, hh * 64 + 64)
        j = kt - 4 * qb
        c0 = max(j, 0) * 128
        diag = j >= 0
        O.mm(bank(bs)[:, c0:512], qkT[wb][:, 1 + hh, kt * 128:(kt + 1) * 128],
             qkT[wb][:, 0, qb * 512 + c0:(qb + 1) * 512], True, not diag,
             reads=[R_qk[wb][1][kt // 4], R_qk[wb][0][qb]], writes=[RB[bs]], sig=not diag)
        if diag:
            O.mm(bank(bs)[:, c0:c0 + 128], ident_b[:], mask_b[:], False, True,
                 reads=[R_const], writes=[RB[bs]], sig=True)
        O.act(PT[pi][:, c0:512], bank(bs)[:, c0:512], AF.Exp, [RB[bs], R_tab], [R_PT[pi]],
              bias=biasT[:, qb, kt, head:head + 1])

    def emit_PV(pp, st):
        hh, qb, kt, bs, pi, xy = st
        j = kt - 4 * qb
        c0 = max(j, 0) * 128
        nk = 4 * qb + 4
        bx = 3 + xy
        lhsT = Vh[kt // 8][:, kt % 8, pp, hh:hh + 2, :].rearrange("p s d -> p (s d)")
        O.mm(bank(bx)[:, c0:512], lhsT, PT[pi][:, c0:512], kt == 0, kt == nk - 1,
             reads=[R_V[kt], R_PT[pi]], writes=[RB[bx]], sig=True)
        if kt == nk - 1:
            po = slice(hh * 64, hh * 64 + 64)
            pd = slice(64 - hh * 64, 128 - hh * 64)
            O.recip(rden[xy][pd, :], bank(bx)[pd, :], [RB[bx]], [R_rden[xy]])
            O.tt(mixedT[po, 8 + pp, qb * 512:(qb + 1) * 512], bank(bx)[po, :], rden[xy][pd, :], ALU.mult,
                 [RB[bx], R_rden[xy]], [R_mixA[pp][hh][qb]] + (R_wv if pp == 0 else []))

    load_wqk(0)
    for which in range(2):
        for tb in range(4):
            qk_block_a(0, which, tb)
            qk_block_b(0, which, tb)
    for pp in range(8):
        pend = []
        if pp + 1 < 8:
            load_wqk(pp + 1)
            for which in range(2):
                for tb in range(4):
                    pend.append((qk_block_a, (pp + 1, which, tb)))
                    pend.append((qk_block_b, (pp + 1, which, tb)))
        steps = []
        for hh in range(2):
            for qb in range(4):
                xy = state["xy"] % 2
                state["xy"] += 1
                for kt in range(4 * qb + 4):
                    steps.append((hh, qb, kt, (0, 1, 2, 5)[state["s"] % 4], state["s"] % 4, xy))
                    state["s"] += 1
        n = len(steps)
        for i in range(n + LOOK):
            if i < n:
                emit_S(pp, steps[i])
            if i >= LOOK:
                emit_PV(pp, steps[i - LOOK])
            if pend and i % 5 == 2:
                f_, a_ = pend.pop(0)
                f_(*a_)
        while pend:
            f_, a_ = pend.pop(0)
            f_(*a_)
        if pp == 6:
            wo = nc.alloc_sbuf_tensor_at("wo_pref", [128, 16, DM], BF16, offset=A.reg["H"][0])
            R_wo = Reg()
            S.dma("pool", wo[:, 0:8, :], w_out[0:1024, :].rearrange("(fc p) c -> p fc c", p=128),
                  writes=[R_wo] + R_hT)
            S.dma("pool", wo[:, 8:16, :], w_out[1024:2048, :].rearrange("(fc p) c -> p fc c", p=128),
                  writes=[R_wo])
    R_mixA_all = [R_mixA[pp][hh][qb] for pp in range(8) for hh in range(2) for qb in range(4)]
    if upto == "ATT":
        dump("d_mixA", mixedT[:, 8:16, :], R_mixA_all)
        return finish(nc, S, A, dbg, out_d, None)

    S.barrier()
    A.reset("R", "R0", "R1", "H")
    x1 = A.get("R", [128, NT, DM], F32, "x1")
    R_x1 = [[Reg(), Reg()] for _ in range(NT)]
    rtail_mark = A.cur["R"]
    A.get("H", [128, 16, DM], BF16, "wo_placeholder")
    xr = [A.get("R", [128, DM], F32, "xr") for _ in range(2)]
    R_xr = regs(2)
    for t in range(NT):
        S.dma("sp", xr[t % 2][:], x_d[t * 128:(t + 1) * 128, :], writes=[R_xr[t % 2]])
        tsl = slice(t * 128, (t + 1) * 128)
        rd = [R_wo, R_mixS[0][t], R_mixS[1][t]] + [R_mixA[pp][hh][t // 4] for pp in range(8) for hh in range(2)]
        for cbk in range(2):
            b = (2 * t + cbk) % 4
            csl = slice(cbk * 512, (cbk + 1) * 512)
            O.mmg(bank(b), [(mixedT[:, fc, tsl], wo[:, fc, csl]) for fc in range(16)], reads=rd, writes=[RB[b]])
            O.tt(x1[:, t, csl], bank(b), xr[t % 2][:, csl], ALU.add, [RB[b], R_xr[t % 2]], [R_x1[t][cbk]])
    R_x1_all = [r for p in R_x1 for r in p]
    if upto == "X1":
        dump("d_x1", x1[:], R_x1_all)
        dump("d_mixA", mixedT[:, 8:16, :], R_mixA_all)
        return finish(nc, S, A, dbg, out_d, None)

    S.barrier()
    A.reset("H", "M", "ML", "MH")
    A.cur["R"] = rtail_mark
    h2T = A.get("H", [128, 8, S_LEN], BF16, "h2T")
    R_h2 = regs(NT)
    bcast_load(gA, g_xattn, R_gA)
    bcast_load(gB, g_mem, R_gB)
    qxT = A.get("ML", [128, 8, S_LEN], BF16, "qxT")
    R_qx = [regs(4) for _ in range(4)]
    junkD = A.get("MH", [128, DM], BF16, "junkD")
    xnD = [A.get("MH", [128, DM], BF16, "xnD") for _ in range(2)]
    statD = A.get("MH", [128, 3 * NT], F32, "statD")
    statM = A.get("MH", [128, 8], F32, "statM")
    mtile = [A.get("MH", [128, DM], F32, "mtile") for _ in range(2)]
    memT = A.get("MH", [128, 8, 256], BF16, "memT")
    kxT = A.get("MH", [128, 8, 256], BF16, "kxT")
    vx = A.get("MH", [128, 2, DM], BF16, "vx")
    rdenD = A.get("MH", [128, 512], F32, "rdenD")
    wbD = [A.get("R", [128, 8, 512], BF16, "wbD") for _ in range(2)]
    raw2 = A.get("R", [128, 2, 512], F32, "raw2")
    sq2 = A.get("R", [128, 2, 512], BF16, "sq2")
    sd2 = A.get("R", [128, 512], F32, "sd2")
    rs2 = A.get("R", [128, 512], F32, "rs2")
    PTx = [A.get("R", [128, 512], BF16, "PTx") for _ in range(4)]
    R_wbD = regs(2); R_mt = regs(2); R_memT = regs(2); R_kx = regs(4); R_vx = regs(2)
    R_raw2 = regs(2); R_sq2 = regs(2); R_sd2 = Reg(); R_rs2 = Reg(); R_PTx = regs(4); R_rdenD = Reg()
    wsD = (junkD, xnD, statD, Reg(), regs(2), regs(NT))
    norm_all(lambda t: (x1[:, t, :], R_x1[t]), gA, R_gA, h2T, R_h2, wsD)
    wsM = (junkD, xnD, statM, wsD[3], wsD[4], regs(2))
    for m_ in range(2):
        S.dma("sp", mtile[m_][:], mem_d[m_ * 128:(m_ + 1) * 128, :], writes=[R_mt[m_]])
        norm_tile(mtile[m_][:], [R_mt[m_]], gB, R_gB, memT, m_ * 128, [R_memT[m_]], wsM, m_, m_ % 2)
    nwb = 0

    def load_wb(src_ap):
        nonlocal nwb
        k = nwb % 2
        nwb += 1
        S.dma("pool", wbD[k][:], src_ap, writes=[R_wbD[k]])
        return wbD[k], R_wbD[k]

    S.barrier()
    mh0 = A.reg["MH"][0]
    raw2s = [raw2, nc.alloc_sbuf_tensor_at("raw2b", [128, 2, 512], F32, offset=mh0)]
    sq2s = [sq2, nc.alloc_sbuf_tensor_at("sq2b", [128, 2, 512], BF16, offset=mh0 + 4096)]
    mt0 = mh0 + 2048 + 2 * 2048 + 192 + 64
    sd2s = [sd2, nc.alloc_sbuf_tensor_at("sd2b", [128, 512], F32, offset=mt0)]
    rs2s = [rs2, nc.alloc_sbuf_tensor_at("rs2b", [128, 512], F32, offset=mt0 + 2048)]
    rdens = [rdenD, nc.alloc_sbuf_tensor_at("rdenDb", [128, 512], F32, offset=mt0 + 4096)]
    R_raw2 = [regs(2), regs(2)]
    R_sq2 = [regs(2), regs(2)]
    R_sd2 = regs(2); R_rs2 = regs(2); R_rdenD = regs(2)
    hn = [0]

    def proj_norm(lhs_fn, rhs_fn, reads, ncol, gcol0, dst_fn, dst_regs):
        k = hn[0] % 2
        hn[0] += 1
        pb = (3 * k, 3 * k + 1)
        nb = 3 * k + 2
        for dc in range(2):
            O.mmg(bank(pb[dc])[:, 0:ncol], [(lhs_fn(dc, kc), rhs_fn(kc)) for kc in range(8)],
                  reads=reads, writes=[RB[pb[dc]]])
        for dc in range(2):
            O.cp(raw2s[k][:, dc, 0:ncol], bank(pb[dc])[:, 0:ncol], [RB[pb[dc]]], [R_raw2[k][dc]], eng="act")
            O.tt(sq2s[k][:, dc, 0:ncol], raw2s[k][:, dc, 0:ncol], raw2s[k][:, dc, 0:ncol], ALU.mult,
                 [R_raw2[k][dc]], [R_sq2[k][dc]])
        O.mmg(bank(nb)[:, 0:ncol], [(ones_b[:], sq2s[k][:, dc, 0:ncol]) for dc in range(2)],
              reads=R_sq2[k] + [R_const], writes=[RB[nb]])
        O.act(sd2s[k][:, 0:ncol], bank(nb)[:, 0:ncol], AF.Ln, [RB[nb], R_eps], [R_sd2[k]], scale=1.0 / 256.0,
              bias=eps_col[:])
        O.act(rs2s[k][:, 0:ncol], sd2s[k][:, 0:ncol], AF.Exp, [R_sd2[k]], [R_rs2[k]], scale=-0.5)
        for dc in range(2):
            O.stt(dst_fn(dc), raw2s[k][:, dc, 0:ncol], cols[:, gcol0 + dc:gcol0 + dc + 1], rs2s[k][:, 0:ncol],
                  ALU.mult, ALU.mult, [R_raw2[k][dc], R_rs2[k], R_cols], dst_regs)

    for cbk in range(2):
        wbuf, rw = load_wb(xkv_w[:, cbk * 512:(cbk + 1) * 512].rearrange("(kc p) c -> p kc c", p=128))
        for hl in range(2):
            hd = cbk * 2 + hl
            proj_norm(lambda dc, kc, wbuf=wbuf, hl=hl: wbuf[:, kc, (hl * 2 + dc) * 128:(hl * 2 + dc + 1) * 128],
                      lambda kc: memT[:, kc, :], [rw] + R_memT, 256, 8,
                      lambda dc, hd=hd: kxT[:, 2 * hd + dc, :], [R_kx[hd]])
    for cbk in range(2):
        wbuf, rw = load_wb(xkv_w[:, 1024 + cbk * 512:1024 + (cbk + 1) * 512].rearrange("(kc p) c -> p kc c", p=128))
        for m_ in range(2):
            b = 6 + m_
            O.mmg(bank(b), [(memT[:, kc, m_ * 128:(m_ + 1) * 128], wbuf[:, kc, :]) for kc in range(8)],
                  reads=[rw, R_memT[m_]], writes=[RB[b]])
            O.cpalt(vx[:, m_, cbk * 512:(cbk + 1) * 512], bank(b), [RB[b]], [R_vx[m_]])
    for cbk in range(2):
        wbuf, rw = load_wb(xq_w[:, cbk * 512:(cbk + 1) * 512].rearrange("(kc p) c -> p kc c", p=128))
        for hl in range(2):
            hd = cbk * 2 + hl
            for tb in range(4):
                sl = slice(tb * 512, (tb + 1) * 512)
                proj_norm(lambda dc, kc, wbuf=wbuf, hl=hl: wbuf[:, kc, (hl * 2 + dc) * 128:(hl * 2 + dc + 1) * 128],
                          lambda kc, sl=sl: h2T[:, kc, sl], [rw] + R_h2[tb * 4:(tb + 1) * 4], 512, 6,
                          lambda dc, hd=hd, sl=sl: qxT[:, 2 * hd + dc, sl], [R_qx[hd][tb]])
    S.barrier()
    A.reset("H")
    oxT = A.get("H", [128, 8, S_LEN], BF16, "oxT")
    R_ox = regs(NT // 4)
    npx = 0
    it = 0
    for hd in range(4):
        for tb in range(4):
            sl = slice(tb * 512, (tb + 1) * 512)
            k = it % 2
            it += 1
            pts = []
            for m_ in range(2):
                b = 2 * k + m_
                O.mmg(bank(b), [(kxT[:, 2 * hd + dc, m_ * 128:(m_ + 1) * 128], qxT[:, 2 * hd + dc, sl]) for dc in range(2)],
                      reads=[R_kx[hd], R_qx[hd][tb]], writes=[RB[b]])
                pi = npx % 4
                npx += 1
                O.act(PTx[pi][:], bank(b), AF.Exp, [RB[b]], [R_PTx[pi]])
                pts.append(pi)
            for dc in range(2):
                b = 4 + dc
                O.mmg(bank(b), [(vx[:, m_, hd * 256 + dc * 128:hd * 256 + (dc + 1) * 128], PTx[pts[m_]][:]) for m_ in range(2)],
                      reads=R_vx + [R_PTx[p] for p in pts], writes=[RB[b]])
            bd = 6 + k
            O.mmg(bank(bd), [(ones_b[:], PTx[pts[m_]][:]) for m_ in range(2)],
                  reads=[R_const] + [R_PTx[p] for p in pts], writes=[RB[bd]])
            O.act(rdens[k][:], bank(bd), AF.Ln, [RB[bd]], [R_rdenD[k]])
            O.act(rdens[k][:], rdens[k][:], AF.Exp, [R_rdenD[k]], [R_rdenD[k]], scale=-1.0)
            for dc in range(2):
                O.tt(oxT[:, 2 * hd + dc, sl], bank(4 + dc), rdens[k][:], ALU.mult, [RB[4 + dc], R_rdenD[k]], [R_ox[tb]])
    wxo = []
    for cbk in range(2):
        wxo.append(load_wb(xo_w[:, cbk * 512:(cbk + 1) * 512].rearrange("(kc p) c -> p kc c", p=128)))
    h3T = nc.alloc_sbuf_tensor_at("h3T", [128, 8, S_LEN], BF16, offset=A.reg["ML"][0])
    R_h3 = regs(NT)
    bcast_load(gA, g_mlp, R_gA)
    statE = A.get("MH", [128, 3 * NT], F32, "statE")
    wsE = (junkD, xnD, statE, Reg(), regs(2), regs(NT))
    R_qx_all = [r for hq in R_qx for r in hq]
    for t in range(NT):
        tsl = slice(t * 128, (t + 1) * 128)
        for cbk in range(2):
            b = (2 * t + cbk) % 4
            csl = slice(cbk * 512, (cbk + 1) * 512)
            wbuf, rw = wxo[cbk]
            O.mmg(bank(b), [(oxT[:, c_, tsl], wbuf[:, c_, :]) for c_ in range(8)],
                  reads=[rw, R_ox[t // 4]], writes=[RB[b]])
            O.tt(x1[:, t, csl], bank(b), x1[:, t, csl], ALU.add, [RB[b], R_x1[t][cbk]], [R_x1[t][cbk]])
        norm_a(x1[:, t, :], R_x1[t], wsE, t)
        norm_b(x1[:, t, :], R_x1[t], gA, R_gA, h3T, t * 128, [R_h3[t]] + (R_qx_all if t == 0 else []), wsE, t,
               4 + t % 2)
    if upto == "X2":
        dump("d_x2", x1[:], R_x1_all)
        return finish(nc, S, A, dbg, out_d, None)

    S.barrier()
    A.reset("H", "MH")
    A.cur["R"] = rtail_mark
    rr = [A.get("MH", [128, 512], F32, "rr") for _ in range(2)]
    wdn = [A.get("MH", [128, 4, DM], BF16, "wdn") for _ in range(2)]
    uT2 = [A.get("H", [128, 4, S_LEN], BF16, "uT2") for _ in range(2)]
    wup = [A.get("R", [128, 8, 512], BF16, "wup") for _ in range(2)]
    R_rr = regs(2); R_wdn = regs(2); R_wup = regs(2)
    R_u2 = [regs(4), regs(4)]
    R_out = regs(NT)
    nrr = 0
    for grp in range(8):
        ub = grp % 2
        S.dma("pool", wup[ub][:], w_up[:, grp * 512:(grp + 1) * 512].rearrange("(kc p) c -> p kc c", p=128),
              writes=[R_wup[ub]])
        S.dma("pool", wdn[ub][:], w_down[grp * 512:(grp + 1) * 512, :].rearrange("(j p) c -> p j c", p=128),
              writes=[R_wdn[ub]])
        for j in range(4):
            for tb in range(4):
                sl = slice(tb * 512, (tb + 1) * 512)
                b = (j * 4 + tb) % 4
                O.mmg(bank(b), [(wup[ub][:, kc, j * 128:(j + 1) * 128], h3T[:, kc, sl]) for kc in range(8)],
                      reads=[R_wup[ub]] + R_h3[tb * 4:(tb + 1) * 4], writes=[RB[b]])
                ri = nrr % 2
                nrr += 1
                O.act(rr[ri][:], bank(b), AF.Relu, [RB[b]], [R_rr[ri]])
                O.tt(uT2[ub][:, j, sl], rr[ri][:], bank(b), ALU.mult, [R_rr[ri], RB[b]], [R_u2[ub][tb]])
        for t in range(NT):
            tsl = slice(t * 128, (t + 1) * 128)
            for cbk in range(2):
                b = 4 + (2 * t + cbk) % 4
                csl = slice(cbk * 512, (cbk + 1) * 512)
                O.mmg(bank(b), [(uT2[ub][:, j, tsl], wdn[ub][:, j, csl]) for j in range(4)],
                      reads=[R_wdn[ub], R_u2[ub][t // 4]], writes=[RB[b]])
                O.tt(x1[:, t, csl], bank(b), x1[:, t, csl], ALU.add, [RB[b], R_x1[t][cbk]], [R_x1[t][cbk]])
            if grp == 7:
                S.dma("sp", out_d[tsl, :], x1[:, t, :], reads=R_x1[t], writes=[R_out[t]])
    return finish(nc, S, A, dbg, out_d, R_out)


_DBG = {}


def finish(nc, S, A, dbg, out_d, out_regs):
    fin = []
    for name, ap, rg in dbg:
        shp = list(ap.shape)
        d = nc.dram_tensor(name, shp, ap.dtype, kind="ExternalOutput").ap()
        r = Reg()
        S.dma("sp", d, ap, reads=list(rg), writes=[r])
        fin.append(r)
    if out_regs is not None:
        fin += list(out_regs)
    S.final_wait("sp", fin)
    if S.pe_pending:
        raise RuntimeError("pe pending")
    S.emit()
    _DBG["est_total_us"] = S.est_total
    _DBG["n_ops"] = len(S.ops)
    return nc


def build_rest(L):
    raise NotImplementedError


def _consts():
    ident = np.eye(128, dtype=np.float32)
    s = np.arange(128)[:, None]
    l = np.arange(128)[None, :]
    mask = np.where(s > l, np.float32(NEG), np.float32(0.0)).astype(np.float32)
    return ident, mask


_W_NAMES = ["g_mix", "w_in", "conv_w", "conv_b", "dt_bias", "a_log", "d_skip", "ssm_norm_w", "g_q", "g_k",
            "f_bias", "w_out", "g_xattn", "g_mem", "xq_w", "xkv_w", "xg_q", "xg_k", "xo_w", "g_mlp", "w_up",
            "w_down"]


def make_in_maps(inputs, n_cores=8):
    ident, mask = _consts()
    shared = {k: np.ascontiguousarray(np.asarray(inputs[k], dtype=np.float32)[0]) for k in _W_NAMES}
    shared["c_ident"] = ident
    shared["c_mask"] = mask
    x = np.asarray(inputs["x"], dtype=np.float32)
    mem = np.asarray(inputs["mem"], dtype=np.float32)
    maps = []
    for c in range(n_cores):
        m = dict(shared)
        m["x"] = np.ascontiguousarray(x[c])
        m["mem"] = np.ascontiguousarray(mem[c])
        maps.append(m)
    return maps


def kernel(**inputs):
    nc = build_nc("all")
    in_maps = make_in_maps(inputs)
    res = run_bass_kernel_spmd(nc, in_maps, core_ids=list(range(8)))
    out = np.stack([np.asarray(r["out"], dtype=np.float32) for r in res.results], axis=0)
    return out
```

```python
import numpy as np
import concourse.bass as bass
import concourse.mybir as mybir
from concourse.bass_utils import run_bass_kernel_spmd

F32 = mybir.dt.float32
BF16 = mybir.dt.bfloat16
AF = mybir.ActivationFunctionType
ALU = mybir.AluOpType

ENGS = ("pe", "act", "dve", "pool", "sp")

S_LEN = 2048
NT = 16
DM = 1024
EPS = 1e-5
NEG = -30000.0


class Reg:
    __slots__ = ("W", "R", "bank")

    def __init__(self, bank=None):
        self.W = []
        self.R = []
        self.bank = bank


def regs(n):
    return [Reg() for _ in range(n)]


class _Op:
    __slots__ = ("i", "eng", "fns", "est", "preds", "kind", "epoch", "tab", "nun", "succ", "ready", "start",
                 "fin", "pos", "dkey", "dval", "done")


LAT_X = 0.35
LAT_S = 0.05
TAB_COST = 1.3


class Sched:
    def __init__(self, nc, n_dma_sems=32):
        self.nc = nc
        self.sem = {}
        self._ctx = []
        for e in ENGS:
            if e == "sp":
                continue
            c = nc.semaphore("s_" + e)
            self.sem[e] = c.__enter__()
            self._ctx.append(c)
        self.n_dma = n_dma_sems
        for i in range(n_dma_sems):
            c = nc.semaphore("s_dma%d" % i)
            self.sem[("dma", i)] = c.__enter__()
            self._ctx.append(c)
        self.ops = []
        self.epoch = 0
        self._pend = []
        self.bank_last = {}
        self.bank_w = {}
        self.bank_r = {}
        self.pe_pending = False

    def _new(self, eng, fns, est, kind, tab, reads, writes):
        op = _Op()
        op.i = len(self.ops)
        op.eng = eng
        op.fns = fns
        op.est = est
        op.kind = kind
        op.tab = tab
        op.epoch = self.epoch
        op.succ = []
        op.done = False
        preds = {}

        def add(p, raw):
            if p is op or p.epoch != op.epoch:
                return
            need = True
            if kind != "dma" and p.kind != "dma" and p.eng == eng and not raw and eng == "pe":
                need = False
            preds[p] = preds.get(p, False) or need

        for r in reads:
            for w in r.W:
                add(w, True)
            if r.bank is not None and eng in ("act", "dve"):
                lr = self.bank_last.get((r.bank, "dve" if eng == "act" else "act"))
                if lr is not None:
                    add(lr, False)
                lr = self.bank_last.get((r.bank, eng))
                if lr is not None:
                    add(lr, False)
            if r.bank is not None:
                bw = self.bank_w.get(r.bank)
                if bw is not None:
                    add(bw, True)
        for w in writes:
            for x in w.W:
                add(x, False)
            for x in w.R:
                add(x, False)
            if w.bank is not None and eng == "pe":
                for x in self.bank_r.get(w.bank, ()):
                    add(x, False)
        op.preds = preds
        for r in reads:
            if r.bank is not None and eng != "pe":
                self.bank_r.setdefault(r.bank, []).append(op)
        for w in writes:
            if w.bank is not None and eng == "pe":
                self.bank_w[w.bank] = op
                self.bank_r[w.bank] = []
        for r in reads:
            r.R.append(op)
            if r.bank is not None and eng in ("act", "dve"):
                self.bank_last[(r.bank, eng)] = op
        for w in writes:
            w.W = [op]
            w.R = []
        self.ops.append(op)
        return op

    def op(self, eng, fn, reads=(), writes=(), sig=True, est=0.5, tab=None):
        reads = list(reads)
        writes = list(writes)
        if eng == "pe" and not sig:
            self._pend.append((fn, reads, writes, est))
            self.pe_pending = True
            return
        if eng == "pe":
            fns = [p[0] for p in self._pend] + [fn]
            for p in self._pend:
                reads += p[1]
                writes += p[2]
                est += p[3]
            self._pend = []
            self.pe_pending = False
        else:
            assert not self._pend, "non-PE op declared inside an open PE group"
            fns = [fn]
        self._new(eng, fns, est, "op", tab, reads, writes)

    def dma(self, eng, out, in_, reads=(), writes=(), **kw):
        assert not self._pend
        nbytes = 4
        for d in out.shape:
            nbytes *= d
        op = self._new(eng, [(out, in_, kw)], 2.0 + nbytes / 150e3, "dma", None, list(reads), list(writes))
        return op

    def final_wait(self, eng, regs_):
        self._new(eng, [], 0.01, "wait", None, list(regs_), [])

    def barrier(self):
        assert not self._pend
        self.epoch += 1
        self.bank_last = {}
        self.bank_w = {}
        self.bank_r = {}

    def _schedule(self):
        free = {e: 0.0 for e in ENGS}
        tabcur = [None]
        win = {"pe": 64, "act": 96, "dve": 96, "pool": 1, "sp": 1}
        order = []
        nep = self.epoch + 1
        byep = [[] for _ in range(nep)]
        for o in self.ops:
            byep[o.epoch].append(o)
        t_ep = 0.0
        for ep in range(nep):
            ops = byep[ep]
            lst = {e: [] for e in ENGS}
            for o in ops:
                lst[o.eng].append(o)
                o.nun = len(o.preds)
                for p in o.preds:
                    p.succ.append(o)
                o.ready = t_ep
            head = {e: 0 for e in ENGS}
            for e in ENGS:
                free[e] = max(free[e], t_ep)
            remaining = len(ops)
            while remaining:
                best = None
                for e in ENGS:
                    L = lst[e]
                    h = head[e]
                    while h < len(L) and L[h].done:
                        h += 1
                    head[e] = h
                    k = h
                    seen = 0
                    w = win[e]
                    while k < len(L) and seen < w:
                        o = L[k]
                        k += 1
                        if o.done:
                            continue
                        seen += 1
                        if o.nun:
                            continue
                        st = max(free[e], o.ready)
                        if e == "act" and o.tab is not None and o.tab != tabcur[0]:
                            st += TAB_COST
                        key = (st, o.i)
                        if best is None or key < best[0]:
                            best = (key, o)
                assert best is not None, "scheduler deadlock"
                (st, _), o = best
                e = o.eng
                if e == "act" and o.tab is not None:
                    tabcur[0] = o.tab
                o.start = st
                if o.kind == "dma":
                    iss = 1.5 if e == "pool" else 0.1
                    free[e] = st + iss
                    o.fin = st + iss + o.est
                else:
                    free[e] = st + o.est
                    o.fin = st + o.est
                o.done = True
                remaining -= 1
                order.append(o)
                for sopp in o.succ:
                    sopp.nun -= 1
                    lat = LAT_S if (sopp.eng == e and o.kind != "dma") else LAT_X
                    if o.fin + lat > sopp.ready:
                        sopp.ready = o.fin + lat
            t_ep = max([t_ep] + [o.fin for o in ops])
        self.est_total = t_ep
        return order

    def emit(self):
        assert not self._pend
        order = self._schedule()
        q = {e: [] for e in ENGS}
        pos = {e: 0 for e in ENGS}
        waited = {e: {} for e in ENGS}
        dma_tot = [0] * self.n_dma
        rr_sw = 0
        rr_hw = 0
        cur_ep = {e: 0 for e in ENGS}
        snap = {}
        last_ep = 0

        def wait(e, key, v):
            if waited[e].get(key, 0) >= v:
                return
            waited[e][key] = v
            h = self.sem[key]
            q[e].append(lambda eng_, h=h, v=v: eng_.wait_ge(h, v))

        for o in order:
            e = o.eng
            if o.epoch > last_ep:
                tot = {k: v for k, v in pos.items() if k != "sp" and v > 0}
                for k in range(self.n_dma):
                    if dma_tot[k] > 0:
                        tot[("dma", k)] = dma_tot[k]
                for ep in range(last_ep + 1, o.epoch + 1):
                    snap[ep] = tot
                last_ep = o.epoch
            if o.epoch > cur_ep[e]:
                for key, v in snap[o.epoch].items():
                    if key != e:
                        wait(e, key, v)
                cur_ep[e] = o.epoch
            for p, need in o.preds.items():
                if not need:
                    continue
                if p.kind == "dma":
                    wait(e, p.dkey, p.dval)
                else:
                    wait(e, p.eng, p.pos)
            if o.kind == "dma":
                half = self.n_dma // 2
                if e == "pool":
                    k = rr_sw
                    rr_sw = (rr_sw + 1) % half
                else:
                    k = half + rr_hw
                    rr_hw = (rr_hw + 1) % half
                key = ("dma", k)
                if dma_tot[k] > 0:
                    wait(e, key, dma_tot[k])
                dma_tot[k] += 16
                o.dkey = key
                o.dval = dma_tot[k]
                out, in_, kw = o.fns[0]
                h = self.sem[key]
                q[e].append(lambda eng_, out=out, in_=in_, h=h, kw=kw:
                            eng_.dma_start(out=out, in_=in_, **kw).then_inc(h, 16))
            elif o.kind == "op":
                pos[e] += 1
                o.pos = pos[e]
                h = self.sem[e]
                for f in o.fns[:-1]:
                    q[e].append(lambda eng_, f=f: f(eng_))
                f = o.fns[-1]
                q[e].append(lambda eng_, f=f, h=h: f(eng_).then_inc(h, 1))
        nc = self.nc
        with nc.Block() as block:
            @block.tensor
            def _(e):
                for f in q["pe"]:
                    f(e)

            @block.scalar
            def _(e):
                for f in q["act"]:
                    f(e)

            @block.vector
            def _(e):
                for f in q["dve"]:
                    f(e)

            @block.gpsimd
            def _(e):
                for f in q["pool"]:
                    f(e)

            @block.sync
            def _(e):
                for f in q["sp"]:
                    f(e)


def _fsz(ap):
    n = 1
    for d in ap.shape[1:]:
        n *= d
    return n


_TAB = {}


class Ops:
    def __init__(self, S):
        self.S = S
        self.flip = 0
        if not _TAB:
            _TAB.update({AF.Exp: "E", AF.Ln: "E", AF.Silu: "S", AF.Square: "S", AF.Sqrt: "Q"})

    def _e(self, eng, n, fixed=0.15, per=0.00105):
        if eng == "pool":
            return 0.3 + n * 0.0023
        return fixed + n * per

    def act(self, out, in_, func, reads, writes, bias=None, scale=None, accum=None):
        kw = {}
        if bias is not None:
            kw["bias"] = bias
        if scale is not None:
            kw["scale"] = scale
        if accum is not None:
            kw["accum_out"] = accum
        est = 0.22 + _fsz(out) * 0.00104 + (0.1 if accum is not None else 0.0)
        self.S.op("act", lambda e: e.activation(out=out, in_=in_, func=func, **kw), reads, writes,
                  est=est, tab=_TAB.get(func))

    def tt(self, out, in0, in1, op, reads, writes, eng="dve"):
        self.S.op(eng, lambda e: e.tensor_tensor(out=out, in0=in0, in1=in1, op=op), reads, writes,
                  est=self._e(eng, _fsz(out)))

    def ts(self, out, in0, s1, s2, op0, op1, reads, writes, eng="dve"):
        est = self._e(eng, _fsz(out))
        if s2 is None:
            self.S.op(eng, lambda e: e.tensor_scalar(out=out, in0=in0, scalar1=s1, scalar2=None, op0=op0),
                      reads, writes, est=est)
        else:
            self.S.op(eng, lambda e: e.tensor_scalar(out=out, in0=in0, scalar1=s1, scalar2=s2, op0=op0, op1=op1),
                      reads, writes, est=est)

    def stt(self, out, in0, scalar, in1, op0, op1, reads, writes):
        self.S.op("dve", lambda e: e.scalar_tensor_tensor(out=out, in0=in0, scalar=scalar, in1=in1,
                                                           op0=op0, op1=op1), reads, writes,
                  est=0.2 + _fsz(out) * 0.00105)

    def cp(self, out, in_, reads, writes, eng="dve"):
        if eng == "act":
            self.S.op("act", lambda e: e.activation(out=out, in_=in_, func=AF.Copy), reads, writes,
                      est=0.22 + _fsz(out) * 0.00104)
        else:
            self.S.op(eng, lambda e: e.tensor_copy(out=out, in_=in_), reads, writes, est=self._e(eng, _fsz(out)))

    def cpalt(self, out, in_, reads, writes):
        self.flip ^= 1
        self.cp(out, in_, reads, writes, eng="act" if self.flip else "dve")

    def recip(self, out, in_, reads, writes):
        self.S.op("dve", lambda e: e.reciprocal(out=out, in_=in_), reads, writes, est=0.2 + _fsz(out) * 0.0062)

    def memset(self, ap, val, writes, eng="dve"):
        self.S.op(eng, lambda e: e.memset(ap, val), [], writes, est=0.1 + _fsz(ap) * 0.0005)

    def mm(self, out, lhsT, rhs, start, stop, reads=(), writes=(), sig=False):
        self.S.op("pe", lambda e: e.matmul(out, lhsT=lhsT, rhs=rhs, start=start, stop=stop),
                  reads, writes, sig=sig, est=0.03 + _fsz(rhs) * 0.0006)

    def mmg(self, out, pairs, reads, writes):
        n = len(pairs)
        for i, (l, r) in enumerate(pairs):
            self.mm(out, l, r, start=(i == 0), stop=(i == n - 1),
                    reads=reads if i == 0 else (), writes=writes if i == 0 else (), sig=(i == n - 1))

    def tr(self, out, in_, ident, reads=(), writes=(), sig=False):
        self.S.op("pe", lambda e: e.transpose(out, in_, ident), reads, writes, sig=sig, est=0.12)

    def scan(self, out, d0, d1, init, op0, op1, reads, writes):
        self.S.op("dve", lambda e: e.tensor_tensor_scan(out=out, data0=d0, data1=d1, initial=init,
                                                         op0=op0, op1=op1), reads, writes,
                  est=0.2 + _fsz(out) * 0.0021)


class Arena:
    def __init__(self, nc):
        self.nc = nc
        base = (nc.sbuf_base + 63) // 64 * 64
        top = nc.sbuf_top
        KB = 1024
        self.reg = {
            "C": [base, base + 8 * KB],
            "G": [base + 8 * KB, base + 16 * KB],
            "H": [base + 16 * KB, base + 48 * KB],
            "M": [base + 48 * KB, base + 112 * KB],
            "ML": [base + 48 * KB, base + 80 * KB],
            "MH": [base + 80 * KB, base + 112 * KB],
            "R": [base + 112 * KB, top],
            "R0": [base + 112 * KB, base + 176 * KB],
            "R1": [base + 176 * KB, top],
        }
        self.cur = {k: v[0] for k, v in self.reg.items()}
        self.n = 0

    def reset(self, *names):
        for k in names:
            self.cur[k] = self.reg[k][0]

    def get(self, region, shape, dt, name=None):
        esz = 2 if dt == BF16 else 4
        nbytes = esz
        for s in shape[1:]:
            nbytes *= s
        nbytes = (nbytes + 63) // 64 * 64
        off = self.cur[region]
        assert off + nbytes <= self.reg[region][1], (region, name, shape, off + nbytes - self.reg[region][1])
        self.cur[region] = off + nbytes
        self.n += 1
        return self.nc.alloc_sbuf_tensor_at("%s_%d" % (name or region, self.n), list(shape), dt, offset=off)


def build_nc(upto="all", debug=False):
    nc = bass.Bass("TRN2", target_bir_lowering=False)

    def din(name, shape):
        return nc.dram_tensor(name, list(shape), F32, kind="ExternalInput").ap()

    x_d = din("x", [S_LEN, DM])
    mem_d = din("mem", [256, DM])
    g_mix = din("g_mix", [DM])
    w_in = din("w_in", [DM, 5664])
    conv_w = din("conv_w", [4, 1536])
    conv_b = din("conv_b", [1536])
    dt_bias = din("dt_bias", [16])
    a_log = din("a_log", [16])
    d_skip = din("d_skip", [16])
    ssm_norm_w = din("ssm_norm_w", [DM])
    g_q = din("g_q", [64])
    g_k = din("g_k", [64])
    f_bias = din("f_bias", [16])
    w_out = din("w_out", [2048, DM])
    g_xattn = din("g_xattn", [DM])
    g_mem = din("g_mem", [DM])
    xq_w = din("xq_w", [DM, DM])
    xkv_w = din("xkv_w", [DM, 2048])
    xg_q = din("xg_q", [256])
    xg_k = din("xg_k", [256])
    xo_w = din("xo_w", [DM, DM])
    g_mlp = din("g_mlp", [DM])
    w_up = din("w_up", [DM, 4096])
    w_down = din("w_down", [4096, DM])
    c_ident = din("c_ident", [128, 128])
    c_mask = din("c_mask", [128, 128])
    out_d = nc.dram_tensor("out", [S_LEN, DM], F32, kind="ExternalOutput").ap()

    S = Sched(nc)
    O = Ops(S)
    A = Arena(nc)
    dbg = []

    def col(ap1d, n):
        return ap1d.rearrange("(p o) -> p o", o=1)

    PP = [nc.alloc_psum_tensor("pp%d" % i, [128, 1024], F32) for i in range(4)]
    RB = [Reg(bank=i) for i in range(8)]

    def bank(b):
        return PP[b // 2][:, (b % 2) * 512:(b % 2) * 512 + 512]

    def bank_bf(b):
        return PP[b // 2][:, (b % 2) * 512:(b % 2) * 512 + 512].bitcast(BF16)

    ident_f = A.get("C", [128, 128], F32, "identf")
    maskf = A.get("C", [128, 128], F32, "maskf")
    ident_b = A.get("C", [128, 128], BF16, "identb")
    mask_b = A.get("C", [128, 128], BF16, "maskb")
    ones_b = A.get("C", [128, 128], BF16, "onesb")
    bones_b = A.get("C", [128, 128], BF16, "bonesb")
    sel127 = A.get("C", [128, 128], F32, "sel127")
    sel0 = A.get("C", [128, 128], F32, "sel0")
    cols = A.get("C", [128, 64], F32, "cols")
    ones_f = A.get("C", [128, 128], F32, "onesf")
    R_const = Reg()
    R_cols = Reg()
    RC = []

    def cdma(dst, src, **kw):
        r = Reg()
        RC.append(r)
        S.dma("pool", dst, src, reads=[R_cols], writes=[r], **kw)

    S.dma("pool", ident_f[:], c_ident, writes=[R_const])
    R_maskf = Reg()
    S.dma("pool", maskf[:], c_mask, writes=[R_maskf])
    O.cp(ident_b[:], ident_f[:], [R_const], [R_const])
    O.memset(ones_b[:], 1.0, [R_const])
    O.memset(ones_f[:], 1.0, [R_const])
    O.memset(bones_b[:], 0.0, [R_const])
    O.memset(bones_b[0:64, 0:64], 1.0, [R_const])
    O.memset(bones_b[64:128, 64:128], 1.0, [R_const])
    O.cp(sel127[:], ident_f[:, 127:128].to_broadcast([128, 128]), [R_const], [R_const])
    O.cp(sel0[:], ident_f[:, 0:1].to_broadcast([128, 128]), [R_const], [R_const])
    O.cp(mask_b[:], maskf[:], [R_const, R_maskf], [R_const])
    O.memset(cols[:], 0.0, [R_cols])
    cw = A.get("C", [128, 12, 4], F32, "cw")
    cb = A.get("C", [128, 12], F32, "cb")
    cdma(cols[0:64, 0:1], col(g_q, 64))
    cdma(cols[64:128, 0:1], col(g_q, 64))
    cdma(cols[0:64, 1:2], col(g_k, 64))
    cdma(cols[64:128, 1:2], col(g_k, 64))
    cdma(cols[0:16, 3:4], col(dt_bias, 16))
    cdma(cols[32:48, 5:6], col(f_bias, 16))
    cdma(cols[0:16, 4:5], col(a_log, 16))
    cdma(cols[:, 6:8], xg_q.rearrange("(c p) -> p c", p=128), allow_slow_non_contiguous=True)
    cdma(cols[:, 8:10], xg_k.rearrange("(c p) -> p c", p=128), allow_slow_non_contiguous=True)
    for k_ in range(4):
        cdma(cw[:, :, k_], conv_w[k_].rearrange("(j p) -> p j", p=128), allow_slow_non_contiguous=True)
    cdma(cb[:], conv_b.rearrange("(j p) -> p j", p=128), allow_slow_non_contiguous=True)
    O.act(cols[:, 0:1], cols[:, 0:1], AF.Copy, [R_cols] + RC, [R_cols], scale=0.125)
    O.act(cols[:, 6:8], cols[:, 6:8], AF.Copy, [R_cols], [R_cols], scale=1.0 / 16.0)
    O.memset(cols[0:64, 2:3], 1.0, [R_cols])
    O.memset(cols[32:64, 2:3], -1.0, [R_cols])
    O.act(cols[32:64, 3:4], cols[32:64, 5:6], AF.Copy, [R_cols], [R_cols], scale=-1.0)
    O.act(cols[0:16, 4:5], cols[0:16, 4:5], AF.Exp, [R_cols], [R_cols])
    O.act(cols[0:16, 4:5], cols[0:16, 4:5], AF.Copy, [R_cols], [R_cols], scale=-1.0)

    gA = A.get("G", [128, DM], F32, "gA")
    gB = A.get("G", [128, DM], F32, "gB")
    R_gA = Reg()
    R_gB = Reg()

    hT = A.get("H", [128, 8, S_LEN], BF16, "hT")
    R_hT = regs(NT)

    def norm_a(x_ap, x_regs, ws, i):
        junk, xn, stat, R_junk, R_xn, R_stat = ws
        ss = stat[:, 3 * i:3 * i + 1]
        sd = stat[:, 3 * i + 1:3 * i + 2]
        rs = stat[:, 3 * i + 2:3 * i + 3]
        O.act(junk[:], x_ap, AF.Square, x_regs, [R_junk, R_stat[i]], accum=ss)
        O.act(sd, ss, AF.Ln, [R_stat[i], R_eps], [R_stat[i]], scale=1.0 / DM, bias=eps_col[:])
        O.act(rs, sd, AF.Exp, [R_stat[i]], [R_stat[i]], scale=-0.5)

    def norm_b(x_ap, x_regs, g_bc, g_reg, dstT, dst_col0, dst_regs, ws, i, pbank):
        junk, xn, stat, R_junk, R_xn, R_stat = ws
        k = i % 2
        rs = stat[:, 3 * i + 2:3 * i + 3]
        O.stt(xn[k][:], x_ap, rs, g_bc[:], ALU.mult, ALU.mult, x_regs + [R_stat[i], g_reg], [R_xn[k]])
        pb = bank_bf(pbank)
        for kc in range(8):
            O.tr(pb[:, kc * 128:(kc + 1) * 128], xn[k][:, kc * 128:(kc + 1) * 128], ident_b[:],
                 reads=[R_xn[k], R_const] if kc == 0 else (), writes=[RB[pbank]] if kc == 0 else (), sig=(kc == 7))
        O.cpalt(dstT[:, :, dst_col0:dst_col0 + 128], pb.rearrange("p (k t) -> p k t", k=8), [RB[pbank]], dst_regs)

    def norm_tile(x_ap, x_regs, g_bc, g_reg, dstT, dst_col0, dst_regs, ws, i, pbank):
        norm_a(x_ap, x_regs, ws, i)
        norm_b(x_ap, x_regs, g_bc, g_reg, dstT, dst_col0, dst_regs, ws, i, pbank)

    def norm_all(src, g_bc, g_reg, dstT, dst_regs, ws):
        norm_a(src(0)[0], src(0)[1], ws, 0)
        for t in range(NT):
            if t + 1 < NT:
                norm_a(src(t + 1)[0], src(t + 1)[1], ws, t + 1)
            norm_b(src(t)[0], src(t)[1], g_bc, g_reg, dstT, t * 128, [dst_regs[t]], ws, t, t % 2)

    eps_col = A.get("C", [128, 1], F32, "epscol")
    maskf4 = A.get("C", [128, 4, 128], F32, "maskf4")
    ident_r = A.get("C", [128, 128], F32, "identr")
    O.cp(maskf4[:].bitcast(mybir.dt.float32r), maskf[:].unsqueeze(1).to_broadcast([128, 4, 128]), [R_const], [R_const])
    O.cp(ident_r[:].bitcast(mybir.dt.float32r), ident_f[:], [R_const], [R_const])
    mask01 = A.get("C", [128, 128], F32, "mask01")
    O.ts(mask01[:], maskf[:], 0.0, None, ALU.is_equal, None, [R_const], [R_const])
    F32R = mybir.dt.float32r
    maskf4_l = ident_r[:].bitcast(F32R)
    maskf4_r = maskf4[:].rearrange("p a b -> p (a b)").bitcast(F32R)
    R_eps = Reg()
    O.memset(eps_col[:], EPS, [R_eps])

    def bcast_load(dst, g1d, reg):
        S.dma("sp", dst[:], g1d.partition_broadcast(128), writes=[reg])

    def dump(name, ap, rg):
        dbg.append((name, ap, rg))

    A.reset("R", "MH")
    bcast_load(gA, g_mix, R_gA)
    xt = [A.get("MH", [128, DM], F32, "xt") for _ in range(3)]
    R_xt = regs(3)
    junk = A.get("MH", [128, DM], BF16, "junk")
    xn = [A.get("MH", [128, DM], BF16, "xn") for _ in range(2)]
    stat = A.get("MH", [128, 3 * NT], F32, "stat")
    wsA = (junk, xn, stat, Reg(), regs(2), regs(NT))
    def ld_x(t):
        S.dma("sp", xt[t % 3][:], x_d[t * 128:(t + 1) * 128, :], writes=[R_xt[t % 3]])

    ld_x(0)
    ld_x(1)
    norm_a(xt[0][:], [R_xt[0]], wsA, 0)
    for t in range(NT):
        if t + 2 < NT:
            ld_x(t + 2)
        if t + 1 < NT:
            norm_a(xt[(t + 1) % 3][:], [R_xt[(t + 1) % 3]], wsA, t + 1)
        norm_b(xt[t % 3][:], [R_xt[t % 3]], gA, R_gA, hT, t * 128, [R_hT[t]], wsA, t, t % 2)
    if upto == "A":
        dump("d_hT", hT[:], R_hT)
        return finish(nc, S, A, dbg, out_d, None)

    A.reset("R", "R0", "R1")

    def wcols(c0, n):
        return w_in[:, c0:c0 + n].rearrange("(kc p) c -> p kc c", p=128)

    r0_end = A.reg["R0"][1]
    w_xs = nc.alloc_sbuf_tensor_at("w_xs", [128, 8, 512], BF16, offset=r0_end - 20 * 1024)
    w_z = nc.alloc_sbuf_tensor_at("w_z", [128, 8, 512], BF16, offset=r0_end - 12 * 1024)
    w_bc = nc.alloc_sbuf_tensor_at("w_bc", [128, 8, 256], BF16, offset=r0_end - 4 * 1024)
    R_wg = Reg()

    def load_ssd_weights(g):
        S.dma("pool", w_xs[:], wcols(1024 + g * 512, 512), writes=[R_wg])
        S.dma("pool", w_z[:], wcols(g * 512, 512), writes=[R_wg])
        S.dma("pool", w_bc[:, :, 0:128], wcols(2048 + g * 128, 128), writes=[R_wg])
        S.dma("pool", w_bc[:, :, 128:256], wcols(2304 + g * 128, 128), writes=[R_wg])

    load_ssd_weights(0)

    ccT = A.get("R1", [64, S_LEN], F32, "ccT")
    selh = A.get("R1", [16, 16, 128], F32, "selh")
    dt_tm = A.get("R1", [128, NT, 16], F32, "dt_tm")
    Acs_tm = A.get("R1", [128, NT, 16], F32, "Acs_tm")
    nAcs_tm = A.get("R1", [128, NT, 16], F32, "nAcs_tm")
    cum_tm = A.get("R1", [128, NT, 16], F32, "cum_tm")
    eA_tm = A.get("R1", [128, NT, 16], F32, "eA_tm")
    f2_tm = A.get("R1", [128, NT, 16], F32, "f2_tm")
    cd_bc = A.get("R1", [128, NT, 16], F32, "cd_bc")
    cfirst = A.get("R1", [128, NT, 16], F32, "cfirst")
    biasT = A.get("R1", [128, 4, NT, 16], F32, "biasT")
    dsk_bc = A.get("R1", [128, 16], F32, "dsk_bc")
    R_tab = Reg()
    R_cc = Reg()

    wdtf = A.get("R0", [128, 8, 64], BF16, "wdtf")
    eT = A.get("R0", [64, S_LEN], F32, "eT")
    spT = A.get("R0", [64, S_LEN], F32, "spT")
    dAT = A.get("R0", [64, S_LEN], F32, "dAT")
    lfT = A.get("R0", [64, S_LEN], F32, "lfT")
    onesT = A.get("R0", [64, S_LEN], F32, "onesT")
    tmp_tm = A.get("R0", [128, NT, 16], F32, "tmp_tm")
    R_wdtf = Reg()
    R_e = regs(4)
    R_sp = regs(4)
    R_dAT = Reg()
    R_lf = Reg()
    R_onesT = Reg()
    R_tmp = Reg()

    O.memset(wdtf[:], 0.0, [R_wdtf])
    S.dma("pool", wdtf[:, :, 0:16], wcols(2560, 16), writes=[R_wdtf])
    S.dma("pool", wdtf[:, :, 32:48], wcols(5648, 16), writes=[R_wdtf])
    S.dma("sp", dsk_bc[:], d_skip.partition_broadcast(128), writes=[R_tab])
    O.memset(onesT[:], 1.0, [R_onesT])
    O.memset(ccT[:], 0.0, [R_cc])
    O.cp(selh[:], ident_f[0:16, 0:16].unsqueeze(2).to_broadcast([16, 16, 128]), [R_const], [R_tab])
    for tb in range(4):
        b = tb % 2
        sl = slice(tb * 512, (tb + 1) * 512)
        O.mmg(bank(b)[0:64, :], [(wdtf[:, kc, :], hT[:, kc, sl]) for kc in range(8)],
              reads=[R_wdtf] + R_hT[tb * 4:(tb + 1) * 4], writes=[RB[b]])
        O.act(eT[:, sl], bank(b)[0:64, :], AF.Exp, [RB[b], R_cols], [R_e[tb]],
              bias=cols[0:64, 3:4], scale=cols[0:64, 2:3])
        O.act(spT[:, sl], eT[:, sl], AF.Ln, [R_e[tb]], [R_sp[tb]], bias=1.0)
    O.ts(dAT[0:16, :], spT[0:16, :], cols[0:16, 4:5], None, ALU.mult, None, R_sp + [R_cols], [R_dAT])
    O.ts(lfT[32:48, :], spT[32:48, :], -1.0, None, ALU.mult, None, R_sp, [R_lf])
    for c in range(NT):
        cs = slice(c * 128, (c + 1) * 128)
        O.scan(ccT[0:16, cs], onesT[0:16, cs], dAT[0:16, cs], 0.0, ALU.mult, ALU.add,
               [R_dAT, R_onesT], [R_cc])
    O.scan(ccT[32:48, :], onesT[32:48, :], lfT[32:48, :], 0.0, ALU.mult, ALU.add, [R_lf, R_onesT], [R_cc])
    for tq in range(4):
        b = 2 + tq % 2
        for i in range(4):
            t = tq * 4 + i
            ts_ = slice(t * 128, (t + 1) * 128)
            O.tr(bank(b)[:, i * 128:i * 128 + 64], spT[0:64, ts_], ident_f[0:64, 0:64],
                 reads=R_sp + [R_const] if i == 0 else (), writes=[RB[b]] if i == 0 else ())
            O.tr(bank(b)[:, i * 128 + 64:i * 128 + 128], ccT[0:64, ts_], ident_f[0:64, 0:64],
                 reads=[R_cc] if i == 0 else (), sig=(i == 3))
        v = bank(b).rearrange("p (t w) -> p t w", w=128)
        tsl = slice(tq * 4, tq * 4 + 4)
        O.cp(dt_tm[:, tsl, :], v[:, :, 0:16], [RB[b]], [R_tab])
        O.cp(Acs_tm[:, tsl, :], v[:, :, 64:80], [RB[b]], [R_tab], eng="act")
        O.cp(cum_tm[:, tsl, :], v[:, :, 96:112], [RB[b]], [R_tab])

    def flat(t):
        return t[:].rearrange("p t h -> p (t h)")

    O.ts(flat(nAcs_tm), flat(Acs_tm), -1.0, None, ALU.mult, None, [R_tab], [R_tab])
    O.act(flat(eA_tm), flat(Acs_tm), AF.Exp, [R_tab], [R_tab])
    O.mmg(bank(0)[:, 0:256], [(sel127[:], flat(Acs_tm))], reads=[R_tab, R_const], writes=[RB[0]])
    O.act(flat(cd_bc), bank(0)[:, 0:256], AF.Exp, [RB[0]], [R_tab])
    O.tt(flat(tmp_tm), bank(0)[:, 0:256], flat(Acs_tm), ALU.subtract, [RB[0], R_tab], [R_tmp])
    O.act(flat(tmp_tm), flat(tmp_tm), AF.Exp, [R_tmp], [R_tmp])
    O.tt(flat(f2_tm), flat(tmp_tm), flat(dt_tm), ALU.mult, [R_tmp, R_tab], [R_tab])
    O.mmg(bank(1)[:, 0:256], [(sel0[:], flat(cum_tm))], reads=[R_tab, R_const], writes=[RB[1]])
    O.cp(flat(cfirst), bank(1)[:, 0:256], [RB[1]], [R_tab])
    for qb in range(4):
        O.tt(biasT[:, qb, :, :], cfirst[:, 4 * qb + 2:4 * qb + 3, :].to_broadcast([128, NT, 16]), cum_tm[:],
             ALU.subtract, [R_tab], [R_tab])
    if upto == "B1":
        dump("d_dt", dt_tm[:], [R_tab])
        dump("d_Acs", Acs_tm[:], [R_tab])
        dump("d_cum", cum_tm[:], [R_tab])
        dump("d_cd", cd_bc[:], [R_tab])
        dump("d_f2", f2_tm[:], [R_tab])
        dump("d_bias", biasT[:], [R_tab])
        return finish(nc, S, A, dbg, out_d, None)

    S.barrier()
    A.reset("R0", "MH")
    mixedT = A.get("M", [128, 16, S_LEN], BF16, "mixedT")
    R_mixS = [regs(NT), regs(NT)]
    R_mixA = [[regs(4), regs(4)] for _ in range(8)]
    bcast_load(gB, ssm_norm_w, R_gB)

    uT = A.get("R0", [128, 6, 515], F32, "uT")
    xsT = [A.get("R0", [128, 4, 512], F32, "xsT") for _ in range(2)]
    bcT = [A.get("R0", [128, 2, 512], BF16, "bcT") for _ in range(2)]
    S_st = A.get("R0", [128, 512], F32, "S_st")
    S_bf = A.get("R0", [128, 512], BF16, "S_bf")
    xs_sb = [A.get("R0", [128, 512], F32, "xs_sb") for _ in range(2)]
    Xb = [A.get("R0", [128, 512], BF16, "Xb") for _ in range(2)]
    Xd = [A.get("R0", [128, 512], BF16, "Xd") for _ in range(2)]
    Btm = [A.get("R0", [128, 128], BF16, "Btm") for _ in range(2)]
    LT = [A.get("MH", [128, 8, 128], F32, "LT") for _ in range(2)]
    MT = [A.get("MH", [128, 8, 128], BF16, "MT") for _ in range(2)]
    sz = [A.get("MH", [128, 512], F32, "sz") for _ in range(2)]
    t1 = A.get("MH", [128, 512], F32, "t1")
    t3 = [A.get("MH", [128, 512], F32, "t3") for _ in range(2)]
    cbm = [A.get("MH", [128, 128], F32, "cbm") for _ in range(2)]
    yg = A.get("MH", [128, 512], F32, "yg")
    cacc = [A.get("MH", [128, 512], F32, "cacc") for _ in range(2)]
    junk2 = A.get("MH", [128, 512], BF16, "junk2")
    ymix = A.get("MH", [128, 512], BF16, "ymix")
    stat2 = A.get("MH", [128, 3 * 32], F32, "stat2")
    assert A.cur["R0"] <= r0_end - 20 * 1024, "SSD working set collides with its weight tiles"
    R_uT = regs(6)
    R_xsT = regs(2)
    R_bcT = regs(2)
    R_S = Reg(); R_Sbf = Reg()
    R_xs = regs(2); R_X = regs(2); R_Xd = regs(2); R_Btm = regs(2); R_LT = regs(2); R_MT = regs(2); R_sz = regs(2)
    R_t1 = Reg(); R_t2 = Reg(); R_t3 = regs(2); R_yg = Reg(); R_cbm = regs(2)
    R_cacc = regs(2); R_junk2 = Reg(); R_ymix = Reg(); R_st2 = regs(32)
    RB2 = [Reg(bank=2) for _ in range(3)]

    def h8(ap):
        return ap.rearrange("p (h d) -> p h d", h=8)

    wvp0 = nc.alloc_sbuf_tensor_at("wvp0", [128, 8, 256], BF16, offset=A.reg["G"][0])
    R_wvp0 = Reg()
    for g in range(2):
        if g > 0:
            load_ssd_weights(g)
            S.dma("pool", wvp0[:], wcols(4624, 256), writes=[R_wvp0, R_gA])
        O.memset(uT[:, :, 0:3], 0.0, R_uT)
        O.memset(S_st[:], 0.0, [R_S])
        O.memset(S_bf[:], 0.0, [R_Sbf])
        jmap = [4 * g + 0, 4 * g + 1, 4 * g + 2, 4 * g + 3, 8 + g, 10 + g]
        hs = slice(g * 8, g * 8 + 8)

        def u_pe(tb, j):
            sl = slice(tb * 512, (tb + 1) * 512)
            wsrc = w_xs[:, :, j * 128:(j + 1) * 128] if j < 4 else w_bc[:, :, (j - 4) * 128:(j - 3) * 128]
            O.mmg(bank(7), [(wsrc[:, kc, :], hT[:, kc, sl]) for kc in range(8)],
                  reads=[R_wg] + R_hT[tb * 4:(tb + 1) * 4], writes=[RB[7]])

        def u_act1(tb, j):
            jj = jmap[j]
            q = j % 2
            O.cp(uT[:, j, 3:515], bank(7), [RB[7]], [R_uT[j]], eng="act")
            O.act(cacc[q][:], uT[:, j, 3:515], AF.Identity, [R_uT[j], R_cols], [R_cacc[q]],
                  bias=cb[:, jj:jj + 1], scale=cw[:, jj, 3:4])

        def u_dve(tb, j):
            jj = jmap[j]
            q = j % 2
            for k_ in (2, 1, 0):
                O.stt(cacc[q][:], uT[:, j, k_:k_ + 512], cw[:, jj, k_:k_ + 1], cacc[q][:], ALU.mult, ALU.add,
                      [R_uT[j], R_cacc[q], R_cols], [R_cacc[q]])
            O.cp(uT[:, j, 0:3], uT[:, j, 512:515], [R_uT[j]], [R_uT[j]])

        def u_act2(tb, j):
            kb = tb % 2
            q = j % 2
            if j < 4:
                O.act(xsT[kb][:, j, :], cacc[q][:], AF.Silu, [R_cacc[q]], [R_xsT[kb]])
            else:
                O.act(bcT[kb][:, j - 4, :], cacc[q][:], AF.Silu, [R_cacc[q]], [R_bcT[kb]])

        def proj_unit(tb, j):
            u_pe(tb, j); u_act1(tb, j); u_dve(tb, j); u_act2(tb, j)

        def bc8(tab, c):
            return tab[:, c, hs].unsqueeze(2).to_broadcast([128, 8, 64])

        def idx(c):
            tb, ci = c // 4, c % 4
            return tb % 2, c % 2, slice(ci * 128, (ci + 1) * 128), slice(c * 128, (c + 1) * 128)

        def s1_pe(c):
            kb, p, cs, gs = idx(c)
            O.mmg(bank(1), [(hT[:, kc, gs], w_z[:, kc, :]) for kc in range(8)],
                  reads=[R_wg, R_hT[c]], writes=[RB[1]])
            for j in range(4):
                O.tr(bank(0)[:, j * 128:(j + 1) * 128], xsT[kb][:, j, cs], ident_f[:],
                     reads=[R_xsT[kb], R_const] if j == 0 else (), writes=[RB[0]] if j == 0 else (),
                     sig=(j == 3))
            O.tr(bank_bf(2)[:, 256:384], bcT[kb][:, 0, cs], ident_b[:], reads=[R_bcT[kb], R_const],
                 writes=[RB2[1]], sig=True)
            O.mmg(bank(2)[:, 0:128], [(bcT[kb][:, 0, cs], bcT[kb][:, 1, cs])], reads=[R_bcT[kb]],
                  writes=[RB2[0]])
            for half in range(2):
                bk = 4 + half
                O.mm(bank(bk), maskf4_l, maskf4_r, True, False, reads=[R_const], writes=[RB[bk]])
                for i in range(4):
                    hh = g * 8 + half * 4 + i
                    O.mm(bank(bk)[:, i * 128:(i + 1) * 128], selh[0:16, hh, :], ccT[0:16, gs], False, i == 3,
                         reads=[R_tab, R_cc] if i == 0 else (), writes=[RB[bk]], sig=(i == 3))

        def s1_act(c):
            kb, p, cs, gs = idx(c)
            O.act(sz[p][:], bank(1), AF.Silu, [RB[1]], [R_sz[p]])
            O.cp(xs_sb[p][:], bank(0), [RB[0]], [R_xs[p]], eng="act")
            O.cp(Btm[p][:], bank_bf(2)[:, 256:384], [RB2[1]], [R_Btm[p]], eng="act")
            for half in range(2):
                bk = 4 + half
                for i in range(4):
                    hh = g * 8 + half * 4 + i
                    O.act(LT[p][:, half * 4 + i, :], bank(bk)[:, i * 128:(i + 1) * 128], AF.Exp,
                          [RB[bk], R_tab], [R_LT[p]], bias=nAcs_tm[:, c, hh:hh + 1])

        def s1_dve_a(c):
            kb, p, cs, gs = idx(c)
            xs3 = h8(xs_sb[p][:])
            O.tt(h8(Xb[p][:]), xs3, bc8(dt_tm, c), ALU.mult, [R_xs[p], R_tab], [R_X[p]])
            O.tt(h8(Xd[p][:]), xs3, bc8(f2_tm, c), ALU.mult, [R_xs[p], R_tab], [R_Xd[p]])
            O.tt(h8(t3[p][:]), xs3, dsk_bc[:, hs].unsqueeze(2).to_broadcast([128, 8, 64]), ALU.mult,
                 [R_xs[p], R_tab], [R_t3[p]], eng="pool")

        def s1_dve_b(c):
            kb, p, cs, gs = idx(c)
            O.tt(MT[p][:], LT[p][:], bank(2)[:, 0:128].unsqueeze(1).to_broadcast([128, 8, 128]), ALU.mult,
                 [R_LT[p], RB2[0]], [R_MT[p]])

        def s2_pe_a(c):
            kb, p, cs, gs = idx(c)
            for i in range(8):
                O.mm(bank(3)[:, i * 64:(i + 1) * 64], MT[p][:, i, :], Xb[p][:, i * 64:(i + 1) * 64], True, True,
                     reads=[R_MT[p], R_X[p]] if i == 0 else (), writes=[RB[3]] if i == 0 else (), sig=(i == 7))
            O.mmg(bank(7), [(Btm[p][:], Xd[p][:])], reads=[R_Btm[p], R_Xd[p]], writes=[RB[7]])
            O.mmg(bank(6), [(bcT[kb][:, 1, cs], S_bf[:])], reads=[R_bcT[kb], R_Sbf], writes=[RB[6]])

        def s2_state(c):
            S3 = h8(S_st[:])
            O.tt(S3, S3, bc8(cd_bc, c), ALU.mult, [R_S, R_tab], [R_S])
            O.tt(S_st[:], S_st[:], bank(7), ALU.add, [R_S, RB[7]], [R_S])
            O.cp(S_bf[:], S_st[:], [R_S], [R_Sbf], eng="act")

        def s2_dve_a(c):
            kb, p, cs, gs = idx(c)
            O.tt(h8(t1[:]), h8(bank(6)), bc8(eA_tm, c), ALU.mult, [RB[6], R_tab], [R_t1])
            O.tt(t1[:], bank(3), t1[:], ALU.add, [RB[3], R_t1], [R_t1])
            O.tt(t1[:], t1[:], t3[p][:], ALU.add, [R_t1, R_t3[p]], [R_t1])
            O.tt(yg[:], t1[:], sz[p][:], ALU.mult, [R_t1, R_sz[p]], [R_yg])
            si = g * 16 + c
            ss = stat2[:, 3 * si:3 * si + 1]
            S.op("dve", lambda e, ss=ss: e.scalar_tensor_tensor(out=junk2[:], in0=yg[:], scalar=1.0, in1=yg[:],
                                                                op0=ALU.mult, op1=ALU.mult, accum_out=ss),
                 [R_yg], [R_junk2, R_st2[si]])

        def s2_act_a(c):
            si = g * 16 + c
            ss = stat2[:, 3 * si:3 * si + 1]
            sd = stat2[:, 3 * si + 1:3 * si + 2]
            rs = stat2[:, 3 * si + 2:3 * si + 3]
            O.act(sd, ss, AF.Ln, [R_st2[si], R_eps], [R_st2[si]], scale=1.0 / 512.0, bias=eps_col[:])
            O.act(rs, sd, AF.Exp, [R_st2[si]], [R_st2[si]], scale=-0.5)

        def s2_tail(c):
            kb, p, cs, gs = idx(c)
            si = g * 16 + c
            rs = stat2[:, 3 * si + 2:3 * si + 3]
            O.stt(ymix[:], yg[:], rs, gB[:, g * 512:(g + 1) * 512], ALU.mult, ALU.mult,
                  [R_yg, R_st2[si], R_gB], [R_ymix])
            for j in range(4):
                O.tr(bank_bf(2)[:, 512 + j * 128:512 + (j + 1) * 128], ymix[:, j * 128:(j + 1) * 128],
                     ident_b[:], reads=[R_ymix, R_const] if j == 0 else (),
                     writes=[RB2[2]] if j == 0 else (), sig=(j == 3))
            O.cp(mixedT[:, g * 4:(g + 1) * 4, gs],
                 bank_bf(2)[:, 512:1024].rearrange("p (k t) -> p k t", k=4), [RB2[2]], [R_mixS[g][c]], eng="act")

        for j in range(6):
            proj_unit(0, j)
        s1_pe(0); s1_act(0); s1_dve_a(0); s1_dve_b(0)
        for c in range(NT):
            tb, ci = c // 4, c % 4
            units = []
            if tb + 1 < 4 and ci < 3:
                units = [(tb + 1, 2 * ci), (tb + 1, 2 * ci + 1)]
            n = c + 1 if c + 1 < NT else None
            s2_pe_a(c)
            for u in units:
                u_pe(*u) if False else None
            if n is not None:
                s1_pe(n)
            s2_state(c)
            s2_dve_a(c)
            ulist = list(units)
            if ulist:
                u_pe(*ulist[0]); u_act1(*ulist[0])
            if n is not None:
                s1_act(n)
            s2_act_a(c)
            if ulist:
                u_dve(*ulist[0])
                u_pe(*ulist[1]); u_act1(*ulist[1])
            s2_tail(c)
            if n is not None:
                s1_dve_a(n)
            if ulist:
                u_act2(*ulist[0])
                u_dve(*ulist[1])
            if n is not None:
                s1_dve_b(n)
            if ulist:
                u_act2(*ulist[1])
    if upto == "SSD":
        dump("d_mixS", mixedT[:, 0:8, :], R_mixS[0] + R_mixS[1])
        dump("d_dt", dt_tm[:], [R_tab])
        dump("d_Acs", Acs_tm[:], [R_tab])
        dump("d_cum", cum_tm[:], [R_tab])
        dump("d_cd", cd_bc[:], [R_tab])
        dump("d_f2", f2_tm[:], [R_tab])
        dump("d_bias", biasT[:], [R_tab])
        return finish(nc, S, A, dbg, out_d, None)

    S.barrier()
    A.reset("R0")
    Vh = [A.get("R0", [128, 8, 8, 3, 64], BF16, "Vh%d" % i) for i in range(2)]
    R_V = [regs(4) for _ in range(NT)]
    R_V_all = [r for rv in R_V for r in rv]
    mark = A.cur["R0"]
    A.reset("MH")
    wvp = [wvp0] + [A.get("MH", [128, 8, 256], BF16, "wvp") for _ in range(3)]
    R_wv = [R_wvp0] + regs(3)
    for i in range(2):
        O.memset(Vh[i][:, :, :, 1, :], 1.0, [r for t_ in range(i * 8, (i + 1) * 8) for r in R_V[t_]])
    for cb4 in range(1, 4):
        S.dma("pool", wvp[cb4][:], wcols(4624 + cb4 * 256, 256), writes=[R_wv[cb4]])
    nvb = 0
    for cb4 in range(4):
        for t in range(NT):
            b = nvb % 4
            nvb += 1
            O.mmg(bank(b)[:, 0:256], [(hT[:, kc, t * 128:(t + 1) * 128], wvp[cb4][:, kc, :]) for kc in range(8)],
                  reads=[R_hT[t], R_wv[cb4]], writes=[RB[b]])
            O.cpalt(Vh[t // 8][:, t % 8, 2 * cb4:2 * cb4 + 2, 0:3:2, :],
                    bank(b)[:, 0:256].rearrange("p (q s d) -> p q s d", q=2, s=2), [RB[b]], [R_V[t][cb4]])
    if upto == "V":
        return finish(nc, S, A, dbg, out_d, None)
    A.cur["R0"] = mark
    A.reset("R1")
    A.reset("G")
    wqk = [A.get("G", [128, 8, 256], BF16, "wqk") for _ in range(2)]
    R_wqk = regs(2)
    qkT = [A.get("R0", [128, 3, S_LEN], BF16, "qkT0"), A.get("R1", [128, 3, S_LEN], BF16, "qkT1")]
    R_qk = [[regs(4), regs(4)] for _ in range(2)]
    sq = A.get("R1", [128, 512], BF16, "sq")
    raw = A.get("R1", [128, 512], F32, "raw")
    sdv = A.get("R1", [128, 512], F32, "sdv")
    rsv = A.get("R1", [128, 512], F32, "rsv")
    rden = [A.get("R1", [128, 512], F32, "rden") for _ in range(2)]
    rscr = A.get("R1", [128, 512], F32, "rscr")
    R_rscr = Reg()
    PT = [A.get("R0", [128, 512], BF16, "PT") for _ in range(4)]
    R_PT = regs(4)
    R_sq = Reg(); R_raw = Reg(); R_sd = Reg(); R_rs = Reg(); R_rden = regs(2)
    LOOK = 2
    for wb_ in range(2):
        O.memset(qkT[wb_][64:128, 1, :], 0.0, R_qk[wb_][1])
        O.memset(qkT[wb_][0:64, 2, :], 0.0, R_qk[wb_][1])

    def load_wqk(pp):
        wb = (pp + 1) % 2
        S.dma("pool", wqk[wb][:, :, 0:128], wcols(2576 + pp * 128, 128),
              writes=[R_wqk[wb]] + ([R_wvp0] if wb == 0 else []))
        S.dma("pool", wqk[wb][:, :, 128:256], wcols(3600 + pp * 128, 128), writes=[R_wqk[wb]])

    def qk_block_a(pp, which, tb):
        wb = (pp + 1) % 2
        sl = slice(tb * 512, (tb + 1) * 512)
        O.mmg(bank(7), [(wqk[wb][:, kc, which * 128:(which + 1) * 128], hT[:, kc, sl]) for kc in range(8)],
              reads=[R_wqk[wb]] + R_hT[tb * 4:(tb + 1) * 4], writes=[RB[7]])
        O.cp(raw[:], bank(7), [RB[7]], [R_raw])
        O.tt(sq[:], raw[:], raw[:], ALU.mult, [R_raw], [R_sq])

    def qk_block_b(pp, which, tb):
        wb = (pp + 1) % 2
        sl = slice(tb * 512, (tb + 1) * 512)
        O.mmg(bank(7), [(bones_b[:], sq[:])], reads=[R_sq, R_const], writes=[RB[7]])
        O.act(sdv[:], bank(7), AF.Ln, [RB[7], R_eps], [R_sd], scale=1.0 / 64.0, bias=eps_col[:])
        O.act(rsv[:], sdv[:], AF.Exp, [R_sd], [R_rs], scale=-0.5)
        if which == 0:
            O.stt(qkT[wb][:, 0, sl], raw[:], cols[:, 0:1], rsv[:], ALU.mult, ALU.mult,
                  [R_raw, R_rs, R_cols], [R_qk[wb][0][tb]])
        else:
            O.stt(qkT[wb][0:64, 1, sl], raw[0:64, :], cols[0:64, 1:2], rsv[0:64, :], ALU.mult, ALU.mult,
                  [R_raw, R_rs, R_cols], [R_qk[wb][1][tb]])
            O.stt(qkT[wb][64:128, 2, sl], raw[64:128, :], cols[64:128, 1:2], rsv[64:128, :], ALU.mult, ALU.mult,
                  [R_raw, R_rs, R_cols], [R_qk[wb][1][tb]])

    state = {"s": 0, "xy": 0}

    def emit_S(pp, st):
        hh, qb, kt, bs, pi, xy = st
        wb = (pp + 1) % 2
        head = 2 * pp + hh
        ps_ = slice(hh * 64, hh * 64 + 64)
        j = kt - 4 * qb
        c0 = max(j, 0) * 128
        diag = j >= 0
        O.mm(bank(bs)[:, c0:512], qkT[wb][:, 1 + hh, kt * 128:(kt + 1) * 128],
             qkT[wb][:, 0, qb * 512 + c0:(qb + 1) * 512], True, not diag,
             reads=[R_qk[wb][1][kt // 4], R_qk[wb][0][qb]], writes=[RB[bs]], sig=not diag)
        if diag:
            O.mm(bank(bs)[:, c0:c0 + 128], ident_b[:], mask_b[:], False, True,
                 reads=[R_const], writes=[RB[bs]], sig=True)
        O.act(PT[pi][:, c0:512], bank(bs)[:, c0:512], AF.Exp, [RB[bs], R_tab], [R_PT[pi]],
              bias=biasT[:, qb, kt, head:head + 1])

    def emit_PV(pp, st):
        hh, qb, kt, bs, pi, xy = st
        j = kt - 4 * qb
        c0 = max(j, 0) * 128
        nk = 4 * qb + 4
        bx = 3 + xy
        lhsT = Vh[kt // 8][:, kt % 8, pp, hh:hh + 2, :].rearrange("p s d -> p (s d)")
        O.mm(bank(bx)[:, c0:512], lhsT, PT[pi][:, c0:512], kt == 0, kt == nk - 1,
             reads=[R_V[kt][pp // 2], R_PT[pi]], writes=[RB[bx]], sig=True)
        if kt == nk - 1:
            po = slice(hh * 64, hh * 64 + 64)
            pd = slice(64 - hh * 64, 128 - hh * 64)
            O.recip(rden[xy][pd, :], bank(bx)[pd, :], [RB[bx]], [R_rden[xy]])
            O.tt(mixedT[po, 8 + pp, qb * 512:(qb + 1) * 512], bank(bx)[po, :], rden[xy][pd, :], ALU.mult,
                 [RB[bx], R_rden[xy]], [R_mixA[pp][hh][qb]] + (R_wv if pp == 0 else []))

    load_wqk(0)
    for which in range(2):
        for tb in range(4):
            qk_block_a(0, which, tb)
            qk_block_b(0, which, tb)
    for pp in range(8):
        pend = []
        if pp + 1 < 8:
            load_wqk(pp + 1)
            for which in range(2):
                for tb in range(4):
                    pend.append((qk_block_a, (pp + 1, which, tb)))
                    pend.append((qk_block_b, (pp + 1, which, tb)))
        steps = []
        for hh in range(2):
            for qb in range(4):
                xy = state["xy"] % 2
                state["xy"] += 1
                for kt in range(4 * qb + 4):
                    steps.append((hh, qb, kt, (0, 1, 2, 5)[state["s"] % 4], state["s"] % 4, xy))
                    state["s"] += 1
        n = len(steps)
        for i in range(n + LOOK):
            if i < n:
                emit_S(pp, steps[i])
            if i >= LOOK:
                emit_PV(pp, steps[i - LOOK])
            if pend and i % 5 == 2:
                f_, a_ = pend.pop(0)
                f_(*a_)
        while pend:
            f_, a_ = pend.pop(0)
            f_(*a_)
        if pp == 6:
            wo = nc.alloc_sbuf_tensor_at("wo_pref", [128, 16, DM], BF16, offset=A.reg["H"][0])
            R_wo = Reg()
            S.dma("pool", wo[:, 0:8, :], w_out[0:1024, :].rearrange("(fc p) c -> p fc c", p=128),
                  writes=[R_wo] + R_hT)
            S.dma("pool", wo[:, 8:16, :], w_out[1024:2048, :].rearrange("(fc p) c -> p fc c", p=128),
                  writes=[R_wo])
    R_mixA_all = [R_mixA[pp][hh][qb] for pp in range(8) for hh in range(2) for qb in range(4)]
    if upto == "ATT":
        dump("d_mixA", mixedT[:, 8:16, :], R_mixA_all)
        return finish(nc, S, A, dbg, out_d, None)

    S.barrier()
    A.reset("R", "R0", "R1", "H")
    x1 = A.get("R", [128, NT, DM], F32, "x1")
    R_x1 = [[Reg(), Reg()] for _ in range(NT)]
    rtail_mark = A.cur["R"]
    A.get("H", [128, 16, DM], BF16, "wo_placeholder")
    xr = [A.get("R", [128, DM], F32, "xr") for _ in range(2)]
    R_xr = regs(2)
    for t in range(NT):
        S.dma("sp", xr[t % 2][:], x_d[t * 128:(t + 1) * 128, :], writes=[R_xr[t % 2]])
        tsl = slice(t * 128, (t + 1) * 128)
        rd = [R_wo, R_mixS[0][t], R_mixS[1][t]] + [R_mixA[pp][hh][t // 4] for pp in range(8) for hh in range(2)]
        for cbk in range(2):
            b = (2 * t + cbk) % 4
            csl = slice(cbk * 512, (cbk + 1) * 512)
            O.mmg(bank(b), [(mixedT[:, fc, tsl], wo[:, fc, csl]) for fc in range(16)], reads=rd, writes=[RB[b]])
            O.tt(x1[:, t, csl], bank(b), xr[t % 2][:, csl], ALU.add, [RB[b], R_xr[t % 2]], [R_x1[t][cbk]])
    R_x1_all = [r for p in R_x1 for r in p]
    if upto == "X1":
        dump("d_x1", x1[:], R_x1_all)
        dump("d_mixA", mixedT[:, 8:16, :], R_mixA_all)
        return finish(nc, S, A, dbg, out_d, None)

    S.barrier()
    A.reset("H", "M", "ML", "MH")
    A.cur["R"] = rtail_mark
    h2T = A.get("H", [128, 8, S_LEN], BF16, "h2T")
    R_h2 = regs(NT)
    bcast_load(gA, g_xattn, R_gA)
    bcast_load(gB, g_mem, R_gB)
    qxT = A.get("ML", [128, 8, S_LEN], BF16, "qxT")
    R_qx = [regs(4) for _ in range(4)]
    junkD = A.get("MH", [128, DM], BF16, "junkD")
    xnD = [A.get("MH", [128, DM], BF16, "xnD") for _ in range(2)]
    statD = A.get("MH", [128, 3 * NT], F32, "statD")
    statM = A.get("MH", [128, 8], F32, "statM")
    mtile = [A.get("MH", [128, DM], F32, "mtile") for _ in range(2)]
    memT = A.get("MH", [128, 8, 256], BF16, "memT")
    kxT = A.get("MH", [128, 8, 256], BF16, "kxT")
    vx = A.get("MH", [128, 2, DM], BF16, "vx")
    rdenD = A.get("MH", [128, 512], F32, "rdenD")
    wbD = [A.get("R", [128, 8, 512], BF16, "wbD") for _ in range(2)]
    raw2 = A.get("R", [128, 2, 512], F32, "raw2")
    sq2 = A.get("R", [128, 2, 512], BF16, "sq2")
    sd2 = A.get("R", [128, 512], F32, "sd2")
    rs2 = A.get("R", [128, 512], F32, "rs2")
    PTx = [A.get("R", [128, 512], BF16, "PTx") for _ in range(4)]
    R_wbD = regs(2); R_mt = regs(2); R_memT = regs(2); R_kx = regs(4); R_vx = regs(2)
    R_raw2 = regs(2); R_sq2 = regs(2); R_sd2 = Reg(); R_rs2 = Reg(); R_PTx = regs(4); R_rdenD = Reg()
    wsD = (junkD, xnD, statD, Reg(), regs(2), regs(NT))
    norm_all(lambda t: (x1[:, t, :], R_x1[t]), gA, R_gA, h2T, R_h2, wsD)
    wsM = (junkD, xnD, statM, wsD[3], wsD[4], regs(2))
    for m_ in range(2):
        S.dma("sp", mtile[m_][:], mem_d[m_ * 128:(m_ + 1) * 128, :], writes=[R_mt[m_]])
        norm_tile(mtile[m_][:], [R_mt[m_]], gB, R_gB, memT, m_ * 128, [R_memT[m_]], wsM, m_, m_ % 2)
    nwb = 0

    def load_wb(src_ap):
        nonlocal nwb
        k = nwb % 2
        nwb += 1
        S.dma("pool", wbD[k][:], src_ap, writes=[R_wbD[k]])
        return wbD[k], R_wbD[k]

    S.barrier()
    mh0 = A.reg["MH"][0]
    raw2s = [raw2, nc.alloc_sbuf_tensor_at("raw2b", [128, 2, 512], F32, offset=mh0)]
    sq2s = [sq2, nc.alloc_sbuf_tensor_at("sq2b", [128, 2, 512], BF16, offset=mh0 + 4096)]
    mt0 = mh0 + 2048 + 2 * 2048 + 192 + 64
    sd2s = [sd2, nc.alloc_sbuf_tensor_at("sd2b", [128, 512], F32, offset=mt0)]
    rs2s = [rs2, nc.alloc_sbuf_tensor_at("rs2b", [128, 512], F32, offset=mt0 + 2048)]
    rdens = [rdenD, nc.alloc_sbuf_tensor_at("rdenDb", [128, 512], F32, offset=mt0 + 4096)]
    R_raw2 = [regs(2), regs(2)]
    R_sq2 = [regs(2), regs(2)]
    R_sd2 = regs(2); R_rs2 = regs(2); R_rdenD = regs(2)
    hn = [0]

    def proj_norm(lhs_fn, rhs_fn, reads, ncol, gcol0, dst_fn, dst_regs):
        k = hn[0] % 2
        hn[0] += 1
        pb = (3 * k, 3 * k + 1)
        nb = 3 * k + 2
        for dc in range(2):
            O.mmg(bank(pb[dc])[:, 0:ncol], [(lhs_fn(dc, kc), rhs_fn(kc)) for kc in range(8)],
                  reads=reads, writes=[RB[pb[dc]]])
        for dc in range(2):
            O.cp(raw2s[k][:, dc, 0:ncol], bank(pb[dc])[:, 0:ncol], [RB[pb[dc]]], [R_raw2[k][dc]], eng="act")
            O.tt(sq2s[k][:, dc, 0:ncol], raw2s[k][:, dc, 0:ncol], raw2s[k][:, dc, 0:ncol], ALU.mult,
                 [R_raw2[k][dc]], [R_sq2[k][dc]])
        O.mmg(bank(nb)[:, 0:ncol], [(ones_b[:], sq2s[k][:, dc, 0:ncol]) for dc in range(2)],
              reads=R_sq2[k] + [R_const], writes=[RB[nb]])
        O.act(sd2s[k][:, 0:ncol], bank(nb)[:, 0:ncol], AF.Ln, [RB[nb], R_eps], [R_sd2[k]], scale=1.0 / 256.0,
              bias=eps_col[:])
        O.act(rs2s[k][:, 0:ncol], sd2s[k][:, 0:ncol], AF.Exp, [R_sd2[k]], [R_rs2[k]], scale=-0.5)
        for dc in range(2):
            O.stt(dst_fn(dc), raw2s[k][:, dc, 0:ncol], cols[:, gcol0 + dc:gcol0 + dc + 1], rs2s[k][:, 0:ncol],
                  ALU.mult, ALU.mult, [R_raw2[k][dc], R_rs2[k], R_cols], dst_regs)

    for cbk in range(2):
        wbuf, rw = load_wb(xkv_w[:, cbk * 512:(cbk + 1) * 512].rearrange("(kc p) c -> p kc c", p=128))
        for hl in range(2):
            hd = cbk * 2 + hl
            proj_norm(lambda dc, kc, wbuf=wbuf, hl=hl: wbuf[:, kc, (hl * 2 + dc) * 128:(hl * 2 + dc + 1) * 128],
                      lambda kc: memT[:, kc, :], [rw] + R_memT, 256, 8,
                      lambda dc, hd=hd: kxT[:, 2 * hd + dc, :], [R_kx[hd]])
    for cbk in range(2):
        wbuf, rw = load_wb(xkv_w[:, 1024 + cbk * 512:1024 + (cbk + 1) * 512].rearrange("(kc p) c -> p kc c", p=128))
        for m_ in range(2):
            b = 6 + m_
            O.mmg(bank(b), [(memT[:, kc, m_ * 128:(m_ + 1) * 128], wbuf[:, kc, :]) for kc in range(8)],
                  reads=[rw, R_memT[m_]], writes=[RB[b]])
            O.cpalt(vx[:, m_, cbk * 512:(cbk + 1) * 512], bank(b), [RB[b]], [R_vx[m_]])
    for cbk in range(2):
        wbuf, rw = load_wb(xq_w[:, cbk * 512:(cbk + 1) * 512].rearrange("(kc p) c -> p kc c", p=128))
        for hl in range(2):
            hd = cbk * 2 + hl
            for tb in range(4):
                sl = slice(tb * 512, (tb + 1) * 512)
                proj_norm(lambda dc, kc, wbuf=wbuf, hl=hl: wbuf[:, kc, (hl * 2 + dc) * 128:(hl * 2 + dc + 1) * 128],
                          lambda kc, sl=sl: h2T[:, kc, sl], [rw] + R_h2[tb * 4:(tb + 1) * 4], 512, 6,
                          lambda dc, hd=hd, sl=sl: qxT[:, 2 * hd + dc, sl], [R_qx[hd][tb]])
    S.barrier()
    A.reset("H")
    oxT = A.get("H", [128, 8, S_LEN], BF16, "oxT")
    R_ox = regs(NT // 4)
    npx = 0
    it = 0
    for hd in range(4):
        for tb in range(4):
            sl = slice(tb * 512, (tb + 1) * 512)
            k = it % 2
            it += 1
            pts = []
            for m_ in range(2):
                b = 2 * k + m_
                O.mmg(bank(b), [(kxT[:, 2 * hd + dc, m_ * 128:(m_ + 1) * 128], qxT[:, 2 * hd + dc, sl]) for dc in range(2)],
                      reads=[R_kx[hd], R_qx[hd][tb]], writes=[RB[b]])
                pi = npx % 4
                npx += 1
                O.act(PTx[pi][:], bank(b), AF.Exp, [RB[b]], [R_PTx[pi]])
                pts.append(pi)
            for dc in range(2):
                b = 4 + dc
                O.mmg(bank(b), [(vx[:, m_, hd * 256 + dc * 128:hd * 256 + (dc + 1) * 128], PTx[pts[m_]][:]) for m_ in range(2)],
                      reads=R_vx + [R_PTx[p] for p in pts], writes=[RB[b]])
            bd = 6 + k
            O.mmg(bank(bd), [(ones_b[:], PTx[pts[m_]][:]) for m_ in range(2)],
                  reads=[R_const] + [R_PTx[p] for p in pts], writes=[RB[bd]])
            O.act(rdens[k][:], bank(bd), AF.Ln, [RB[bd]], [R_rdenD[k]])
            O.act(rdens[k][:], rdens[k][:], AF.Exp, [R_rdenD[k]], [R_rdenD[k]], scale=-1.0)
            for dc in range(2):
                O.tt(oxT[:, 2 * hd + dc, sl], bank(4 + dc), rdens[k][:], ALU.mult, [RB[4 + dc], R_rdenD[k]], [R_ox[tb]])
    wxo = []
    for cbk in range(2):
        wxo.append(load_wb(xo_w[:, cbk * 512:(cbk + 1) * 512].rearrange("(kc p) c -> p kc c", p=128)))
    h3T = nc.alloc_sbuf_tensor_at("h3T", [128, 8, S_LEN], BF16, offset=A.reg["ML"][0])
    R_h3 = regs(NT)
    bcast_load(gA, g_mlp, R_gA)
    statE = A.get("MH", [128, 3 * NT], F32, "statE")
    wsE = (junkD, xnD, statE, Reg(), regs(2), regs(NT))
    R_qx_all = [r for hq in R_qx for r in hq]
    for t in range(NT):
        tsl = slice(t * 128, (t + 1) * 128)
        for cbk in range(2):
            b = (2 * t + cbk) % 4
            csl = slice(cbk * 512, (cbk + 1) * 512)
            wbuf, rw = wxo[cbk]
            O.mmg(bank(b), [(oxT[:, c_, tsl], wbuf[:, c_, :]) for c_ in range(8)],
                  reads=[rw, R_ox[t // 4]], writes=[RB[b]])
            O.tt(x1[:, t, csl], bank(b), x1[:, t, csl], ALU.add, [RB[b], R_x1[t][cbk]], [R_x1[t][cbk]])
        norm_a(x1[:, t, :], R_x1[t], wsE, t)
        norm_b(x1[:, t, :], R_x1[t], gA, R_gA, h3T, t * 128, [R_h3[t]] + (R_qx_all if t == 0 else []), wsE, t,
               4 + t % 2)
    if upto == "X2":
        dump("d_x2", x1[:], R_x1_all)
        return finish(nc, S, A, dbg, out_d, None)

    S.barrier()
    A.reset("H", "MH")
    A.cur["R"] = rtail_mark
    rr = [A.get("MH", [128, 512], F32, "rr") for _ in range(2)]
    wdn = [A.get("MH", [128, 4, DM], BF16, "wdn") for _ in range(2)]
    uT2 = [A.get("H", [128, 4, S_LEN], BF16, "uT2") for _ in range(2)]
    wup = [A.get("R", [128, 8, 512], BF16, "wup") for _ in range(2)]
    R_rr = regs(2); R_wdn = regs(2); R_wup = regs(2)
    R_u2 = [regs(4), regs(4)]
    R_out = regs(NT)
    nrr = 0
    for grp in range(8):
        ub = grp % 2
        S.dma("pool", wup[ub][:], w_up[:, grp * 512:(grp + 1) * 512].rearrange("(kc p) c -> p kc c", p=128),
              writes=[R_wup[ub]])
        S.dma("pool", wdn[ub][:], w_down[grp * 512:(grp + 1) * 512, :].rearrange("(j p) c -> p j c", p=128),
              writes=[R_wdn[ub]])
        for j in range(4):
            for tb in range(4):
                sl = slice(tb * 512, (tb + 1) * 512)
                b = (j * 4 + tb) % 4
                O.mmg(bank(b), [(wup[ub][:, kc, j * 128:(j + 1) * 128], h3T[:, kc, sl]) for kc in range(8)],
                      reads=[R_wup[ub]] + R_h3[tb * 4:(tb + 1) * 4], writes=[RB[b]])
                ri = nrr % 2
                nrr += 1
                O.act(rr[ri][:], bank(b), AF.Relu, [RB[b]], [R_rr[ri]])
                O.tt(uT2[ub][:, j, sl], rr[ri][:], bank(b), ALU.mult, [R_rr[ri], RB[b]], [R_u2[ub][tb]])
        for t in range(NT):
            tsl = slice(t * 128, (t + 1) * 128)
            for cbk in range(2):
                b = 4 + (2 * t + cbk) % 4
                csl = slice(cbk * 512, (cbk + 1) * 512)
                O.mmg(bank(b), [(uT2[ub][:, j, tsl], wdn[ub][:, j, csl]) for j in range(4)],
                      reads=[R_wdn[ub], R_u2[ub][t // 4]], writes=[RB[b]])
                O.tt(x1[:, t, csl], bank(b), x1[:, t, csl], ALU.add, [RB[b], R_x1[t][cbk]], [R_x1[t][cbk]])
            if grp == 7:
                S.dma("sp", out_d[tsl, :], x1[:, t, :], reads=R_x1[t], writes=[R_out[t]])
    return finish(nc, S, A, dbg, out_d, R_out)


_DBG = {}


def finish(nc, S, A, dbg, out_d, out_regs):
    fin = []
    for name, ap, rg in dbg:
        shp = list(ap.shape)
        d = nc.dram_tensor(name, shp, ap.dtype, kind="ExternalOutput").ap()
        r = Reg()
        S.dma("sp", d, ap, reads=list(rg), writes=[r])
        fin.append(r)
    if out_regs is not None:
        fin += list(out_regs)
    S.final_wait("sp", fin)
    if S.pe_pending:
        raise RuntimeError("pe pending")
    S.emit()
    _DBG["est_total_us"] = S.est_total
    _DBG["n_ops"] = len(S.ops)
    return nc


def build_rest(L):
    raise NotImplementedError


def _consts():
    ident = np.eye(128, dtype=np.float32)
    s = np.arange(128)[:, None]
    l = np.arange(128)[None, :]
    mask = np.where(s > l, np.float32(NEG), np.float32(0.0)).astype(np.float32)
    return ident, mask


_W_NAMES = ["g_mix", "w_in", "conv_w", "conv_b", "dt_bias", "a_log", "d_skip", "ssm_norm_w", "g_q", "g_k",
            "f_bias", "w_out", "g_xattn", "g_mem", "xq_w", "xkv_w", "xg_q", "xg_k", "xo_w", "g_mlp", "w_up",
            "w_down"]


def make_in_maps(inputs, n_cores=8):
    ident, mask = _consts()
    shared = {k: np.ascontiguousarray(np.asarray(inputs[k], dtype=np.float32)[0]) for k in _W_NAMES}
    shared["c_ident"] = ident
    shared["c_mask"] = mask
    x = np.asarray(inputs["x"], dtype=np.float32)
    mem = np.asarray(inputs["mem"], dtype=np.float32)
    maps = []
    for c in range(n_cores):
        m = dict(shared)
        m["x"] = np.ascontiguousarray(x[c])
        m["mem"] = np.ascontiguousarray(mem[c])
        maps.append(m)
    return maps


def kernel(**inputs):
    nc = build_nc("all")
    in_maps = make_in_maps(inputs)
    res = run_bass_kernel_spmd(nc, in_maps, core_ids=list(range(8)))
    out = np.stack([np.asarray(r["out"], dtype=np.float32) for r in res.results], axis=0)
    return out
```
